# Optimizing a Trainium2 kernel written in Bass

```python
import math
import jax, jax.numpy as jnp
from jax import lax
import numpy as np

D_MODEL = 1024
BATCH = 8
SEQ = 2048
DEPTH = 4
DEC_BATCH = 32
DEC_SEQ = 1
PAST_LEN = 16384
PAGE_SIZE = 128

N_META = 16
N_A = DEPTH // 2
N_B = DEPTH - N_A
CONV_W = 31
N_HEADS = 16
QK_NOPE = 64
QK_ROPE = 32
V_HEAD = 64
KV_LORA = 256
Q_LORA = 384
D_FF = -(-8 * D_MODEL // (3 * 256)) * 256
ROPE_BASE = 10000.0
EPS = 1e-6
Q_BLOCK = 128
SCALE = 1.0 / math.sqrt(QK_NOPE + QK_ROPE)

kernel_name = "yoco_conformer_conv_mla_decoder_step"


def rmsnorm(x, g):
    xf = x.astype(jnp.float32)
    y = xf * lax.rsqrt(jnp.mean(xf * xf, axis=-1, keepdims=True) + EPS)
    return (y * g.astype(jnp.float32)).astype(x.dtype)


def layernorm(x, g, b):
    xf = x.astype(jnp.float32)
    mu = jnp.mean(xf, axis=-1, keepdims=True)
    xc = xf - mu
    y = xc * lax.rsqrt(jnp.mean(xc * xc, axis=-1, keepdims=True) + EPS)
    return (y * g.astype(jnp.float32) + b.astype(jnp.float32)).astype(x.dtype)


def rope(x, pos):
    half = QK_ROPE // 2
    inv = jnp.power(ROPE_BASE, -jnp.arange(half, dtype=jnp.float32) / half)
    ang = pos.astype(jnp.float32)[:, None] * inv[None, :]
    cos = jnp.cos(ang)[:, None, :]
    sin = jnp.sin(ang)[:, None, :]
    xf = x.astype(jnp.float32)
    x1, x2 = xf[..., :half], xf[..., half:]
    return jnp.concatenate([x1 * cos - x2 * sin, x1 * sin + x2 * cos], axis=-1).astype(x.dtype)


def swiglu(h, w_gate, w_up, w_down):
    return (jax.nn.silu(h @ w_gate) * (h @ w_up)) @ w_down


def conv_module(h, state, pw1_w, pw1_b, dw_w, dw_b, ln_g, ln_b, pw2_w, pw2_b):
    u = h @ pw1_w + pw1_b
    g = u[..., :D_MODEL] * jax.nn.sigmoid(u[..., D_MODEL:])
    buf = jnp.concatenate([state.astype(g.dtype), g], axis=1)
    y = lax.conv_general_dilated(
        buf, dw_w[:, None, :], window_strides=(1,), padding='VALID',
        dimension_numbers=('NWC', 'WIO', 'NWC'), feature_group_count=D_MODEL) + dw_b
    y = jax.nn.silu(layernorm(y, ln_g, ln_b))
    return y @ pw2_w + pw2_b, buf[:, -(CONV_W - 1):]


def kv_side(h, pos, kv_norm_g, w_dkv, kv_latent_norm_g):
    ckr = rmsnorm(h, kv_norm_g) @ w_dkv
    c = rmsnorm(ckr[..., :KV_LORA], kv_latent_norm_g)
    kr = rope(ckr[..., KV_LORA:][:, :, None, :], pos)[:, :, 0]
    return c, kr


def mla_queries(hn, pos, w_dq, q_norm_g, w_uq):
    cq = rmsnorm(hn @ w_dq, q_norm_g)
    q = jnp.einsum('btr,rhd->bthd', cq, w_uq)
    return q[..., :QK_NOPE], rope(q[..., QK_NOPE:], pos)


def attend_prompt(q_nope, q_rope, k_nope, k_rope, v):
    b, t = q_nope.shape[0], q_nope.shape[1]
    n_blk = -(-t // Q_BLOCK)
    tp = n_blk * Q_BLOCK
    pad = ((0, 0), (0, tp - t), (0, 0), (0, 0))
    qn = jnp.pad(q_nope, pad).reshape(b, n_blk, Q_BLOCK, N_HEADS, QK_NOPE).transpose(1, 0, 2, 3, 4)
    qr = jnp.pad(q_rope, pad).reshape(b, n_blk, Q_BLOCK, N_HEADS, QK_ROPE).transpose(1, 0, 2, 3, 4)
    kpos = jnp.arange(t)

    def block(args):
        i, qn_b, qr_b = args
        s = (jnp.einsum('bqhd,bkhd->bhqk', qn_b, k_nope)
             + jnp.einsum('bqhr,bkr->bhqk', qr_b, k_rope)).astype(jnp.float32) * SCALE
        qpos = i * Q_BLOCK + jnp.arange(Q_BLOCK)
        s = jnp.where(kpos[None, :] <= qpos[:, None], s, -jnp.inf)
        p = jax.nn.softmax(s, axis=-1).astype(v.dtype)
        return jnp.einsum('bhqk,bkhd->bqhd', p, v)

    o = lax.map(block, (jnp.arange(n_blk), qn, qr))
    o = o.transpose(1, 0, 2, 3, 4).reshape(b, tp, N_HEADS * V_HEAD)
    return o[:, :t]


def attend_sample(q_nope, q_rope, c_past, kr_past, c_new, kr_new, w_uk, w_uv):
    b, n = q_nope.shape[0], q_nope.shape[1]
    n_past = c_past.shape[1]
    q_lat = jnp.einsum('bqhd,chd->bqhc', q_nope, w_uk)
    s_past = (jnp.einsum('bqhc,bkc->bhqk', q_lat, c_past)
              + jnp.einsum('bqhr,bkr->bhqk', q_rope, kr_past))
    s_new = (jnp.einsum('bqhc,bkc->bhqk', q_lat, c_new)
             + jnp.einsum('bqhr,bkr->bhqk', q_rope, kr_new))
    causal = jnp.tril(jnp.ones((n, n), dtype=bool))
    s_new = jnp.where(causal, s_new.astype(jnp.float32), -jnp.inf)
    s = jnp.concatenate([s_past.astype(jnp.float32), s_new], axis=-1) * SCALE
    p = jax.nn.softmax(s, axis=-1).astype(c_past.dtype)
    o_lat = (jnp.einsum('bhqk,bkc->bqhc', p[..., :n_past], c_past)
             + jnp.einsum('bhqk,bkc->bqhc', p[..., n_past:], c_new))
    o = jnp.einsum('bqhc,chd->bqhd', o_lat, w_uv)
    return o.reshape(b, n, N_HEADS * V_HEAD)


def setup_inputs(seed: int = 0) -> dict:
    key = jax.random.key(seed)
    k = jax.random.split(key, 31)
    f32 = jnp.float32
    n_pages = PAST_LEN // PAGE_SIZE
    n_used = DEC_BATCH * n_pages
    n_pool = n_used + n_used // 4

    def nrm(i, shape, scale=1.0):
        return scale * jax.random.normal(k[i], shape, f32)

    def gain(i, shape):
        return 1.0 + nrm(i, shape, 0.05)

    def bias(i, shape):
        return nrm(i, shape, 0.02)

    page_table = jax.random.permutation(k[5], n_pool)[:n_used].reshape(DEC_BATCH, n_pages).astype(jnp.int32)
    return {
        "x_prompt": nrm(0, (BATCH, SEQ, D_MODEL)),
        "x_sample": nrm(1, (DEC_BATCH, DEC_SEQ, D_MODEL)),
        "cache_latent": nrm(2, (n_pool, PAGE_SIZE, KV_LORA)),
        "cache_krope": nrm(3, (n_pool, PAGE_SIZE, QK_ROPE)),
        "state_conv": nrm(4, (N_A, DEC_BATCH, CONV_W - 1, D_MODEL), 0.5),
        "page_table": page_table,
        "meta_tokens": nrm(6, (N_META, D_MODEL)),
        "a_norm_g": gain(7, (N_A, D_MODEL)),
        "a_pw1_w": nrm(8, (N_A, D_MODEL, 2 * D_MODEL), D_MODEL ** -0.5),
        "a_pw1_b": bias(9, (N_A, 2 * D_MODEL)),
        "a_dw_w": nrm(10, (N_A, CONV_W, D_MODEL), CONV_W ** -0.5),
        "a_dw_b": bias(11, (N_A, D_MODEL)),
        "a_ln_g": gain(12, (N_A, D_MODEL)),
        "a_ln_b": bias(13, (N_A, D_MODEL)),
        "a_pw2_w": nrm(14, (N_A, D_MODEL, D_MODEL), D_MODEL ** -0.5),
        "a_pw2_b": bias(15, (N_A, D_MODEL)),
        "ffn_norm_g": gain(16, (DEPTH, D_MODEL)),
        "ffn_w_gate": nrm(17, (DEPTH, D_MODEL, D_FF), D_MODEL ** -0.5),
        "ffn_w_up": nrm(18, (DEPTH, D_MODEL, D_FF), D_MODEL ** -0.5),
        "ffn_w_down": nrm(19, (DEPTH, D_FF, D_MODEL), D_FF ** -0.5),
        "kv_norm_g": gain(20, (D_MODEL,)),
        "w_dkv": nrm(21, (D_MODEL, KV_LORA + QK_ROPE), D_MODEL ** -0.5),
        "kv_latent_norm_g": gain(22, (KV_LORA,)),
        "w_uk": nrm(23, (KV_LORA, N_HEADS, QK_NOPE), KV_LORA ** -0.5),
        "w_uv": nrm(24, (KV_LORA, N_HEADS, V_HEAD), KV_LORA ** -0.5),
        "b_norm_g": gain(25, (N_B, D_MODEL)),
        "b_w_dq": nrm(26, (N_B, D_MODEL, Q_LORA), D_MODEL ** -0.5),
        "b_q_norm_g": gain(27, (N_B, Q_LORA)),
        "b_w_uq": nrm(28, (N_B, Q_LORA, N_HEADS, QK_NOPE + QK_ROPE), Q_LORA ** -0.5),
        "b_w_o": nrm(29, (N_B, N_HEADS * V_HEAD, D_MODEL), (N_HEADS * V_HEAD) ** -0.5),
        "final_norm_g": gain(30, (D_MODEL,)),
    }


def reference(x_prompt, x_sample, cache_latent, cache_krope, state_conv, page_table,
              meta_tokens, a_norm_g, a_pw1_w, a_pw1_b, a_dw_w, a_dw_b, a_ln_g, a_ln_b,
              a_pw2_w, a_pw2_b, ffn_norm_g, ffn_w_gate, ffn_w_up, ffn_w_down,
              kv_norm_g, w_dkv, kv_latent_norm_g, w_uk, w_uv,
              b_norm_g, b_w_dq, b_q_norm_g, b_w_uq, b_w_o, final_norm_g):

    def trunk(x, pos, conv_state, make_attend):
        new_conv = []
        c = kr = None
        attn = None
        for l in range(DEPTH):
            if l < N_A:
                y, st = conv_module(rmsnorm(x, a_norm_g[l]), conv_state[l], a_pw1_w[l], a_pw1_b[l],
                                    a_dw_w[l], a_dw_b[l], a_ln_g[l], a_ln_b[l], a_pw2_w[l], a_pw2_b[l])
                new_conv.append(st)
            else:
                if l == N_A:
                    c, kr = kv_side(x, pos, kv_norm_g, w_dkv, kv_latent_norm_g)
                    attn = make_attend(c, kr)
                j = l - N_A
                qn, qr = mla_queries(rmsnorm(x, b_norm_g[j]), pos, b_w_dq[j], b_q_norm_g[j], b_w_uq[j])
                y = attn(qn, qr) @ b_w_o[j]
            x = x + y
            x = x + swiglu(rmsnorm(x, ffn_norm_g[l]), ffn_w_gate[l], ffn_w_up[l], ffn_w_down[l])
        return rmsnorm(x, final_norm_g), jnp.stack(new_conv), c, kr

    bp = x_prompt.shape[0]
    xp = jnp.concatenate([jnp.broadcast_to(meta_tokens[None].astype(x_prompt.dtype), (bp, N_META, D_MODEL)), x_prompt], axis=1)
    pos_p = jnp.arange(xp.shape[1])
    conv0_p = jnp.zeros((N_A, bp, CONV_W - 1, D_MODEL), xp.dtype)

    def prompt_attend(c, kr):
        k_nope = jnp.einsum('btc,chd->bthd', c, w_uk)
        v = jnp.einsum('btc,chd->bthd', c, w_uv)
        return lambda qn, qr: attend_prompt(qn, qr, k_nope, kr, v)

    out_p, conv_p, c_p, kr_p = trunk(xp, pos_p, conv0_p, prompt_attend)
    y_prompt = out_p[:, N_META:]

    bs, n_new = x_sample.shape[0], x_sample.shape[1]
    pos_s = PAST_LEN + jnp.arange(n_new)
    c_past = cache_latent[page_table].reshape(bs, -1, KV_LORA)
    kr_past = cache_krope[page_table].reshape(bs, -1, QK_ROPE)

    def sample_attend(c, kr):
        return lambda qn, qr: attend_sample(qn, qr, c_past, kr_past, c, kr, w_uk, w_uv)

    y_sample, conv_s, c_s, kr_s = trunk(x_sample, pos_s, state_conv, sample_attend)

    return (y_prompt, y_sample, c_p, kr_p, conv_p, c_s, kr_s, conv_s)
```

```python
import math
import numpy as np
from contextlib import ExitStack
import concourse.bass as bass
import concourse.mybir as mybir
from concourse.bass_utils import run_bass_kernel_spmd

F32 = mybir.dt.float32
BF16 = mybir.dt.bfloat16
I32 = mybir.dt.int32
ALU = mybir.AluOpType
AF = mybir.ActivationFunctionType
AX = mybir.AxisListType

NT = 2068
NP = 2064
TT = [(0, 512), (512, 512), (1024, 512), (1536, 512), (2048, 20)]
CTL = [(i * 256, 256) for i in range(8)] + [(2048, 20)]
HGS = [(0, 6), (6, 6), (12, 5), (17, 5)]
EPS = 1e-6
SCALE = 1.0 / math.sqrt(96.0)
TWO_PI = 2.0 * math.pi

VOFF = {}
_o = 0
for _name, _n in [("a_norm_g0", 8), ("a_norm_g1", 8), ("a_pw1_b0", 16), ("a_pw1_b1", 16), ("a_dw_b0", 8), ("a_dw_b1", 8),
                  ("a_ln_g0", 8), ("a_ln_g1", 8), ("a_ln_b0", 8), ("a_ln_b1", 8), ("a_pw2_b0", 8), ("a_pw2_b1", 8),
                  ("ffn_g0", 8), ("ffn_g1", 8), ("ffn_g2", 8), ("ffn_g3", 8), ("kv_g", 8), ("kvl_g", 2),
                  ("b_g0", 8), ("b_g1", 8), ("bq_g0", 3), ("bq_g1", 3), ("fin_g", 8)]:
    VOFF[_name] = _o
    _o += _n
NV = _o


class Sem:
    def __init__(self, h, name):
        self.h = h
        self.name = name
        self.count = 0


class Reg:
    __slots__ = ("name", "lw", "rs")

    def __init__(self, name):
        self.name = name
        self.lw = None
        self.rs = []


class Prog:
    ENG = ("pe", "act", "dve", "pool", "sp")

    def __init__(self, nc, stack):
        self.nc = nc
        self.stack = stack
        self.q = {e: [] for e in self.ENG}
        self.sem = {e: self.new_sem("e_" + e) for e in self.ENG}
        self.waited = {e: {} for e in self.ENG}
        self.regs = {}

    def new_sem(self, name):
        return Sem(self.stack.enter_context(self.nc.semaphore(name)), name)

    def sb(self, name, shape, dt):
        return self.stack.enter_context(self.nc.sbuf_tensor("sb_" + name, list(shape), dt))

    def psum(self, name, shape, dt):
        return self.stack.enter_context(self.nc.psum_tensor(name, list(shape), dt))

    def rg(self, *key):
        r = self.regs.get(key)
        if r is None:
            r = Reg(str(key))
            self.regs[key] = r
        return r

    def _waits(self, eng, reads, writes):
        deps = {}

        def add(d):
            if d is None:
                return
            s, v = d
            if deps.get(s, 0) < v:
                deps[s] = v
        for r in reads:
            add(r.lw)
        for w in writes:
            add(w.lw)
            for d in w.rs:
                add(d)
        out = []
        wd = self.waited[eng]
        for s, v in deps.items():
            if eng == "pe" and s is self.sem["pe"]:
                continue
            if wd.get(s, 0) >= v:
                continue
            wd[s] = v
            out.append((s, v))
        return out

    def _commit(self, tok, reads, writes):
        for r in reads:
            r.rs.append(tok)
            if len(r.rs) > 48:
                m = {}
                for s, v in r.rs:
                    if m.get(s, 0) < v:
                        m[s] = v
                r.rs = list(m.items())
        for w in writes:
            w.lw = tok
            w.rs = []

    def op(self, eng, fns, reads=(), writes=()):
        if callable(fns):
            fns = [fns]
        waits = self._waits(eng, reads, writes)
        s = self.sem[eng]
        s.count += 1
        val = s.count
        q = self.q[eng]
        for ws, wv in waits:
            q.append(lambda e, ws=ws, wv=wv: e.wait_ge(ws.h, wv))
        for f in fns[:-1]:
            q.append(f)
        last = fns[-1]
        q.append(lambda e, last=last, s=s: last(e).then_inc(s.h, 1))
        self._commit((s, val), reads, writes)

    def dma(self, queue, out, in_, dsem, reads=(), writes=(), **kw):
        self.raw(queue, lambda e, out=out, in_=in_, kw=kw: e.dma_start(out=out, in_=in_, **kw), dsem, reads, writes)

    def raw(self, queue, fn, dsem, reads=(), writes=()):
        waits = self._waits(queue, reads, writes)
        dsem.count += 16
        val = dsem.count
        q = self.q[queue]
        for ws, wv in waits:
            q.append(lambda e, ws=ws, wv=wv: e.wait_ge(ws.h, wv))
        q.append(lambda e, fn=fn, dsem=dsem: fn(e).then_inc(dsem.h, 16))
        self._commit((dsem, val), reads, writes)

    def final_wait(self, eng, regs):
        for ws, wv in self._waits(eng, regs, ()):
            self.q[eng].append(lambda e, ws=ws, wv=wv: e.wait_ge(ws.h, wv))

    def emit(self):
        with self.nc.Block() as block:
            @block.tensor
            def _(e):
                for f in self.q["pe"]:
                    f(e)

            @block.scalar
            def _(e):
                for f in self.q["act"]:
                    f(e)

            @block.vector
            def _(e):
                for f in self.q["dve"]:
                    f(e)

            @block.gpsimd
            def _(e):
                for f in self.q["pool"]:
                    f(e)

            @block.sync
            def _(e):
                for f in self.q["sp"]:
                    f(e)


def A_(out, in_, func, **kw):
    return lambda e: e.activation(out=out, in_=in_, func=func, **kw)


def TTo(out, a, b, op):
    return lambda e: e.tensor_tensor(out=out, in0=a, in1=b, op=op)


def STT(out, a, s, b, op0, op1):
    return lambda e: e.scalar_tensor_tensor(out=out, in0=a, scalar=s, in1=b, op0=op0, op1=op1)


def TS(out, a, s1, s2, op0, op1=None):
    if op1 is None:
        return lambda e: e.tensor_scalar(out=out, in0=a, scalar1=s1, scalar2=None, op0=op0)
    return lambda e: e.tensor_scalar(out=out, in0=a, scalar1=s1, scalar2=s2, op0=op0, op1=op1)


def CP(out, in_):
    return lambda e: e.tensor_copy(out=out, in_=in_)


def MM(out, l, r, st, sp):
    return lambda e: e.matmul(out, lhsT=l, rhs=r, start=st, stop=sp)


def TR(out, in_, ident):
    return lambda e: e.transpose(out, in_, ident)


def build_program(stage=99, debug=False):
    nc = bass.Bass("TRN2", target_bir_lowering=False)

    def din(name, shape, dt=F32):
        return nc.dram_tensor(name, list(shape), dt, kind="ExternalInput").ap()

    def dout(name, shape):
        return nc.dram_tensor(name, list(shape), F32, kind="ExternalOutput").ap()

    xin = din("xin", [NT, 1024])
    state = din("state", [2, 30, 4, 1024])
    ptT = din("ptT", [128, 4], I32)
    cache_lat = din("cache_lat", [5120 * 16, 2048])
    cache_kr = din("cache_kr", [5120 * 16, 256])
    w_pw1 = din("w_pw1", [2, 8, 128, 2048])
    w_pw2 = din("w_pw2", [2, 8, 128, 1024])
    w_gu = din("w_gu", [4, 22, 128, 2048])
    w_dn = din("w_dn", [4, 8, 128, 2816])
    w_dq = din("w_dq", [2, 3, 128, 1024])
    w_rs = din("w_rs", [2, 4, 128, 768])
    w_hd = din("w_hd", [2, 16, 128, 608])
    w_o = din("w_o", [2, 8, 128, 1024])
    w_dkvl = din("w_dkvl", [128, 2048])
    w_dkvr = din("w_dkvr", [128, 512])
    w_ukT = din("w_ukT", [64, 4096])
    w_uvp = din("w_uvp", [128, 4096])
    vecs_d = din("vecs", [128, NV])
    dww_d = din("dww", [128, 2 * 8 * 31])
    y_all = dout("y_all", [NT, 1024])
    lat_all = dout("lat_all", [NT, 256])
    kr_all = dout("kr_all", [NT, 32])
    conv_p = dout("conv_p", [2, 30, 1024])
    conv_s = dout("conv_s", [2, 4, 30, 1024])

    with ExitStack() as st:
        p = Prog(nc, st)
        rg = p.rg
        xT = p.sb("xT", [128, 8, NT], F32)
        R1 = p.sb("R1", [128, 8 * NT], BF16)
        R2 = p.sb("R2", [128, 8 * 2098], BF16)
        ringt = p.sb("ring", [128, 4, 2048], BF16)
        auxA = p.sb("auxA", [128, 4352], BF16)
        auxB = p.sb("auxB", [128, 2048], F32)
        C4 = p.sb("C4", [128, NT], BF16)
        S4 = p.sb("S4", [128, NT], BF16)
        krT = p.sb("krT", [32, NT], BF16)
        ident_f = p.sb("ident_f", [128, 128], F32)
        ident_b = p.sb("ident_b", [128, 128], BF16)
        ones_b = p.sb("ones_b", [128, 128], BF16)
        tri = p.sb("tri", [128, 128], BF16)
        negm = p.sb("negm", [128, 128], BF16)
        selK = p.sb("selK", [32, 96], BF16)
        selQ = p.sb("selQ", [128, 4, 96], BF16)
        vecs = p.sb("vecs", [128, NV], F32)
        dww = p.sb("dww", [128, 2, 8, 31], F32)
        sqb = [p.sb("sqb%d" % i, [128, 512], BF16) for i in range(2)]
        tA = [p.sb("tA%d" % i, [128, 512], F32) for i in range(2)]
        tB = [p.sb("tB%d" % i, [128, 512], F32) for i in range(2)]
        PT = [p.sb("PT%d" % i, [128, 512], BF16) for i in range(3)]
        iot = p.sb("iot", [128, 2, 1024], F32)
        io = [iot[:, 0, :], iot[:, 1, :]]
        iob = iot[:, :, :].rearrange("p a b -> p (a b)").bitcast(BF16)
        g_tail = p.sb("g_tail", [128, 8, 34], F32)
        bufT = p.sb("bufT", [128, 8, 4, 31], F32)
        cols = p.sb("cols", [128, 8], F32)
        icol = p.sb("icol", [128, 4], I32)
        idx = p.sb("idx", [128, 4], I32)
        idxg = p.sb("idxg", [128, 4, 16], I32)
        Qs = p.sb("Qs", [96, 16, 4], BF16)
        PS = [p.psum("ps%d" % i, [128, 512], F32) for i in range(8)]
        psr = [rg("ps", i) for i in range(8)]
        ps_ctr = [0]

        def nps():
            i = ps_ctr[0] % 6
            ps_ctr[0] += 1
            return PS[i], psr[i]

        aps_ctr = [0]

        def aps():
            i = 6 + aps_ctr[0] % 2
            aps_ctr[0] += 1
            return PS[i], psr[i]

        d_const = p.new_sem("d_const")
        d_io = [p.new_sem("d_io%d" % i) for i in range(2)]
        d_out = p.new_sem("d_out")
        d_ring = [p.new_sem("d_ring%d" % i) for i in range(4)]
        d_misc = p.new_sem("d_misc")
        out_reg = rg("out")

        hT = R1[:, :].rearrange("p (c t) -> p c t", c=8)
        gpad = R2[:, :].rearrange("p (c t) -> p c t", c=8)
        hid = R2[:, 0:6 * NT].rearrange("p (c t) -> p c t", c=6)
        cqn = R2[:, 0:3 * NT].rearrange("p (c t) -> p c t", c=3)
        Qh = R2[:, 3 * NT:4 * NT]
        Kh = R2[:, 4 * NT:5 * NT]
        cT = R2[:, 6 * NT:8 * NT].rearrange("p (c t) -> p c t", c=2)
        diagB = [auxA[:, 0:31 * 128].rearrange("p (k j) -> p k j", k=31),
                 auxB[:, :].bitcast(BF16)[:, 0:31 * 128].rearrange("p (k j) -> p k j", k=31)]
        Rq = auxA[:, 0:NT]
        ybuf = auxB[:, :].rearrange("p (c t) -> p c t", c=8)
        ring_ctr = [0]

        def ring_load(src_ap, n):
            i = ring_ctr[0] % 4
            ring_ctr[0] += 1
            r = rg("ring", i)
            np_ = src_ap.shape[0]
            dst = ringt[0:np_, i, 0:n]
            p.dma("pool", dst, src_ap, d_ring[i], writes=[r])
            return ringt[:, i, :], r

        def V(name, c=0):
            o = VOFF[name] + c
            return vecs[:, o:o + 1]

        p.dma("sp", vecs[:, :], vecs_d, d_const, writes=[rg("vecs")])
        p.dma("sp", dww[:, :, :, :], dww_d.rearrange("p (l c k) -> p l c k", l=2, c=8), d_const, writes=[rg("dww")])
        p.dma("sp", idx[:, :], ptT, d_const, writes=[rg("idx")])
        for g in range(16):
            p.op("dve", TS(idxg[:, :, g], idx[:, :], 16, g, ALU.mult, ALU.add), reads=[rg("idx")], writes=[rg("idxg")])
        R2f = R2[:, :].bitcast(F32)
        R2i = R2[:, :].bitcast(I32)
        iota_row = R2f[:, 0:128]
        scr = rg("scr")
        p.op("pool", lambda e: e.iota(iota_row, pattern=[[1, 128]], base=0, channel_multiplier=0,
                                      allow_small_or_imprecise_dtypes=True), writes=[scr])
        p.op("pool", lambda e: e.iota(cols[:, 0:1], pattern=[[0, 1]], base=0, channel_multiplier=1,
                                      allow_small_or_imprecise_dtypes=True), writes=[rg("cols")])
        p.op("pool", lambda e: e.iota(icol[:, 0:1], pattern=[[0, 1]], base=0, channel_multiplier=1), writes=[rg("icol")])
        p.op("dve", TS(ident_f[:, :], iota_row, cols[:, 0:1], None, ALU.is_equal), reads=[scr, rg("cols")], writes=[rg("ident")])
        p.op("dve", CP(ident_b[:, :], ident_f[:, :]), reads=[rg("ident")], writes=[rg("identb")])
        p.op("dve", TS(tri[:, :], iota_row, cols[:, 0:1], None, ALU.is_ge), reads=[scr, rg("cols")], writes=[rg("tri")])
        p.op("dve", TS(negm[:, :], tri[:, :], 30000.0, -30000.0, ALU.mult, ALU.add), reads=[rg("tri")], writes=[rg("negm")])
        p.op("pool", lambda e: e.memset(ones_b[:, :], 1.0), writes=[rg("ones")])
        p.op("pool", lambda e: e.memset(selK[:, :], 0.0), writes=[rg("selK")])
        p.op("pool", lambda e: e.memset(selQ[:, :, :], 0.0), writes=[rg("selQ")])
        p.op("dve", CP(selK[:, 64:96], ident_b[0:32, 0:32]), reads=[rg("identb")], writes=[rg("selK")])
        for i in range(4):
            p.op("dve", CP(selQ[:, i, 64:96], ident_b[:, 32 * i:32 * i + 32]), reads=[rg("identb")], writes=[rg("selQ")])
        p.op("dve", lambda e: e.tensor_single_scalar(out=icol[:, 1:2], in_=icol[:, 0:1], scalar=15, op=ALU.bitwise_and),
             reads=[rg("icol")], writes=[rg("icol")])
        p.op("dve", lambda e: e.tensor_single_scalar(out=icol[:, 2:3], in_=icol[:, 0:1], scalar=16, op=ALU.bitwise_and),
             reads=[rg("icol")], writes=[rg("icol")])
        p.op("dve", CP(cols[:, 1:3], icol[:, 1:3]), reads=[rg("icol")], writes=[rg("cols")])
        p.op("act", A_(cols[:, 1:2], cols[:, 1:2], AF.Exp, scale=-math.log(10000.0) / 16.0), reads=[rg("cols")], writes=[rg("cols")])
        p.op("dve", TS(cols[:, 2:3], cols[:, 2:3], 1.0 / 8.0, -1.0, ALU.mult, ALU.add), reads=[rg("cols")], writes=[rg("cols")])
        p.op("pool", lambda e: e.memset(cols[:, 3:4], EPS), writes=[rg("cols")])
        p.op("pool", lambda e: e.memset(cols[:, 4:5], 0.0), writes=[rg("cols")])
        pos = R2f[:, 0:NT]
        ang = R2f[:, NT:2 * NT]
        kf = R2f[:, 2 * NT:3 * NT]
        ki = R2i[:, 3 * NT:4 * NT]
        p.op("pool", lambda e: e.iota(pos, pattern=[[1, NT]], base=0, channel_multiplier=0,
                                      allow_small_or_imprecise_dtypes=True), reads=[rg("ident"), rg("tri")], writes=[scr])
        p.op("pool", lambda e: e.memset(pos[:, NP:NT], 16384.0), writes=[scr])
        for which, dst in ((0, S4), (1, C4)):
            if which == 0:
                p.op("dve", TS(ang, pos, cols[:, 1:2], None, ALU.mult), reads=[rg("cols")], writes=[scr])
            else:
                p.op("dve", TS(ang, pos, cols[:, 1:2], math.pi / 2.0, ALU.mult, ALU.add), reads=[rg("cols")], writes=[scr])
            p.op("dve", TS(kf, ang, 1.0 / TWO_PI, None, ALU.mult), writes=[scr])
            p.op("dve", CP(ki, kf), writes=[scr])
            p.op("dve", CP(kf, ki), writes=[scr])
            p.op("dve", STT(ang, kf, -TWO_PI, ang, ALU.mult, ALU.add), writes=[scr])
            p.op("dve", TS(ang, ang, 3.14159, -3.14159, ALU.min, ALU.max), writes=[scr])
            p.op("act", A_(ang, ang, AF.Sin), writes=[scr])
            if which == 0:
                p.op("dve", TS(dst[:, :], ang, cols[:, 2:3], None, ALU.mult), reads=[rg("cols")], writes=[scr, rg("tabs")])
            else:
                p.op("dve", CP(dst[:, :], ang), writes=[scr, rg("tabs")])

        def rstd_from(srcs, src_regs, n, D, sq_eng="act"):
            pss, pssr = nps()
            nc_ = len(srcs)
            for c, (s, sr) in enumerate(zip(srcs, src_regs)):
                b = c % 2
                if sq_eng == "act":
                    p.op("act", A_(sqb[b][:, 0:n], s, AF.Square), reads=[sr], writes=[rg("sqb", b)])
                else:
                    p.op(sq_eng, TTo(sqb[b][:, 0:n], s, s, ALU.mult), reads=[sr], writes=[rg("sqb", b)])
                p.op("pe", MM(pss[:, 0:n], ones_b[:, :], sqb[b][:, 0:n], c == 0, c == nc_ - 1),
                     reads=[rg("sqb", b), rg("ones")], writes=[pssr])
            k = rstd_from.ctr % 2
            rstd_from.ctr += 1
            t = tA[k]
            tr_ = rg("tA", k)
            p.op("dve", TS(t[:, 0:n], pss[:, 0:n], 1.0 / D, EPS, ALU.mult, ALU.add), reads=[pssr], writes=[tr_])
            p.op("act", A_(t[:, 0:n], t[:, 0:n], AF.Sqrt), reads=[tr_], writes=[tr_])
            p.op("dve", lambda e, t=t: e.reciprocal(out=t[:, 0:n], in_=t[:, 0:n]), reads=[tr_], writes=[tr_])
            return t, tr_
        rstd_from.ctr = 0

        def HR(ti):
            return [rg("hT", ti)] + [rg("hTc", c, ti) for c in range(8)]

        def norm_tile(gname, ti):
            t0, n = TT[ti]
            srcs = [xT[:, c, t0:t0 + n] for c in range(8)]
            t, tr_ = rstd_from(srcs, [rg("xT", c, ti) for c in range(8)], n, 1024.0, sq_eng="pool")
            for c in range(8):
                eng = "dve"
                if c == 0:
                    rd_, wr_ = [], [rg("hT", ti), rg("hTc", 0, ti)]
                else:
                    rd_, wr_ = [rg("hT", ti)], [rg("hTc", c, ti)]
                p.op(eng, STT(hT[:, c, t0:t0 + n], xT[:, c, t0:t0 + n], V(gname, c), t[:, 0:n], ALU.mult, ALU.mult),
                     reads=[rg("xT", c, ti), tr_, rg("vecs")] + rd_, writes=wr_)

        def first_pass(gname, per_tile):
            norm_tile(gname, 0)
            for ti in range(5):
                if ti + 1 < 5:
                    norm_tile(gname, ti + 1)
                per_tile(ti)

        def ffn(l):
            def gu(Wv, wr, nn, ti):
                t0, n = TT[ti]
                pg, pgr = nps()
                pu, pur = nps()
                p.op("pe", [MM(pg[:, 0:n], Wv[:, 0, k, :], hT[:, k, t0:t0 + n], k == 0, k == 7) for k in range(8)],
                     reads=[wr] + HR(ti), writes=[pgr])
                p.op("pe", [MM(pu[:, 0:n], Wv[:, 1, k, :], hT[:, k, t0:t0 + n], k == 0, k == 7) for k in range(8)],
                     reads=[wr] + HR(ti), writes=[pur])
                b = ti % 2
                p.op("act", A_(tB[b][:, 0:n], pg[:, 0:n], AF.Silu), reads=[pgr], writes=[rg("tB", b)])
                p.op("dve", TTo(hid[:, nn, t0:t0 + n], pu[:, 0:n], tB[b][:, 0:n], ALU.mult),
                     reads=[pur, rg("tB", b)], writes=[rg("hid", nn, ti)])
            KF = 3
            Wf = []
            for nn in range(KF):
                W, wr = ring_load(w_gu[l, nn], 2048)
                Wf.append((W.rearrange("p (g k j) -> p g k j", g=2, k=8), wr))

            def pt(ti):
                for nn in range(KF):
                    gu(Wf[nn][0], Wf[nn][1], nn, ti)
            first_pass("ffn_g%d" % l, pt)
            for gi, (n0, cnt) in enumerate(HGS):
                for nn in range(cnt):
                    if gi == 0 and nn < KF:
                        continue
                    W, wr = ring_load(w_gu[l, n0 + nn], 2048)
                    Wv = W.rearrange("p (g k j) -> p g k j", g=2, k=8)
                    for ti in range(5):
                        gu(Wv, wr, nn, ti)
                for m in range(8):
                    W, wr = ring_load(w_dn[l, m][:, n0 * 128:(n0 + cnt) * 128], cnt * 128)
                    Wv = W[:, 0:cnt * 128].rearrange("p (n j) -> p n j", n=cnt)
                    for ti, (t0, n) in enumerate(TT):
                        po, por = nps()
                        p.op("pe", [MM(po[:, 0:n], Wv[:, nn, :], hid[:, nn, t0:t0 + n], nn == 0, nn == cnt - 1) for nn in range(cnt)],
                             reads=[wr] + [rg("hid", nn, ti) for nn in range(cnt)], writes=[por])
                        p.op("dve", TTo(xT[:, m, t0:t0 + n], po[:, 0:n], xT[:, m, t0:t0 + n], ALU.add),
                             reads=[por], writes=[rg("xT", m, ti)])

        for bi in range(17):
            r0 = bi * 128
            nb = min(128, NT - r0)
            ti = r0 // 512
            b = bi % 2
            p.dma("sp", io[b][0:nb, :], xin[r0:r0 + nb, :], d_io[b], writes=[rg("io", b)])
            for half in range(2):
                pt_, ptr_ = nps()
                p.op("pe", [TR(pt_[:, j * 128:j * 128 + nb], io[b][0:nb, (half * 4 + j) * 128:(half * 4 + j + 1) * 128], ident_f[0:nb, 0:nb])
                            for j in range(4)], reads=[rg("io", b), rg("ident")], writes=[ptr_])
                eng = "act" if half == 0 else "dve"
                src = pt_[:, :].rearrange("p (j t) -> p j t", j=4)[:, :, 0:nb]
                dstv = xT[:, half * 4:half * 4 + 4, r0:r0 + nb]
                if eng == "act":
                    p.op("act", A_(dstv, src, AF.Copy), reads=[ptr_], writes=[rg("xT", half * 4 + j, ti) for j in range(4)])
                else:
                    p.op("dve", CP(dstv, src), reads=[ptr_], writes=[rg("xT", half * 4 + j, ti) for j in range(4)])

        def a_layer(l):
            p.op("pool", lambda e: e.memset(gpad[:, :, 0:30], 0.0),
                 writes=[rg("gpad", c) for c in range(8)] + [scr] + [rg("hid", nn, ti) for nn in range(6) for ti in range(5)])

            def pw1(Wv, wr, c, ti):
                t0, n = TT[ti]
                pa, par = nps()
                pb, pbr = nps()
                p.op("pe", [MM(pa[:, 0:n], Wv[:, 0, k, :], hT[:, k, t0:t0 + n], k == 0, k == 7) for k in range(8)],
                     reads=[wr] + HR(ti), writes=[par])
                p.op("pe", [MM(pb[:, 0:n], Wv[:, 1, k, :], hT[:, k, t0:t0 + n], k == 0, k == 7) for k in range(8)],
                     reads=[wr] + HR(ti), writes=[pbr])
                b = ti % 2
                p.op("act", A_(tB[b][:, 0:n], pb[:, 0:n], AF.Sigmoid, bias=V("a_pw1_b%d" % l, 8 + c)),
                     reads=[pbr, rg("vecs")], writes=[rg("tB", b)])
                p.op("dve", STT(gpad[:, c, 30 + t0:30 + t0 + n], pa[:, 0:n], V("a_pw1_b%d" % l, c), tB[b][:, 0:n], ALU.add, ALU.mult),
                     reads=[par, rg("tB", b)], writes=[rg("gpad", c)])
                if t0 + n > 2034:
                    s0 = max(t0, 2034)
                    p.op("dve", STT(g_tail[:, c, s0 - 2034:t0 + n - 2034], pa[:, s0 - t0:n], V("a_pw1_b%d" % l, c),
                                    tB[b][:, s0 - t0:n], ALU.add, ALU.mult),
                         reads=[par, rg("tB", b)], writes=[rg("g_tail")])
            KF = 3
            Wf = []
            for c in range(KF):
                W, wr = ring_load(w_pw1[l, c], 2048)
                Wf.append((W.rearrange("p (g k j) -> p g k j", g=2, k=8), wr))

            def pt(ti):
                for c in range(KF):
                    pw1(Wf[c][0], Wf[c][1], c, ti)
            first_pass("a_norm_g%d" % l, pt)
            for c in range(KF, 8):
                W, wr = ring_load(w_pw1[l, c], 2048)
                Wv = W.rearrange("p (g k j) -> p g k j", g=2, k=8)
                for ti in range(5):
                    pw1(Wv, wr, c, ti)
            for b4 in range(4):
                b = b4 % 2
                p.dma("sp", io[b][0:30, :], state[l, :, b4, :], d_io[b], writes=[rg("io", b)])
                p.dma("sp", conv_s[l, b4, 0:29, :], io[b][1:30, :], d_out, reads=[rg("io", b)])
                pt_, ptr_ = nps()
                p.op("pe", [TR(pt_[:, c * 30:c * 30 + 30], io[b][0:30, c * 128:(c + 1) * 128], ident_f[0:30, 0:30]) for c in range(8)],
                     reads=[rg("io", b), rg("ident")], writes=[ptr_])
                p.op("act", A_(bufT[:, :, b4, 0:30], pt_[:, 0:240].rearrange("p (c t) -> p c t", c=8), AF.Copy),
                     reads=[ptr_], writes=[rg("bufT")])
            p.op("dve", CP(bufT[:, :, :, 30], g_tail[:, :, 30:34]), reads=[rg("g_tail")], writes=[rg("bufT")])
            b = 0
            for half in range(2):
                pt_, ptr_ = nps()
                p.op("pe", [TR(pt_[0:34, j * 128:(j + 1) * 128], g_tail[:, half * 4 + j, :], ident_f[:, :]) for j in range(4)],
                     reads=[rg("g_tail"), rg("ident")], writes=[ptr_])
                p.op("act", A_(io[b][0:34, half * 512:(half + 1) * 512], pt_[0:34, :], AF.Copy), reads=[ptr_], writes=[rg("io", b)])
            p.dma("sp", conv_p[l, :, :], io[b][0:30, :], d_out, reads=[rg("io", b)])
            for b4 in range(4):
                p.dma("sp", conv_s[l, b4, 29:30, :], io[b][30 + b4:31 + b4, :], d_out, reads=[rg("io", b)])
            tmp = tB[0][:, 0:8 * 31 * 2].rearrange("p (c b k) -> p c b k", c=8, b=2)
            ysr = rg("ys")
            for hb in range(2):
                p.op("dve", TTo(tmp, bufT[:, :, hb * 2:hb * 2 + 2, :],
                                dww[:, l, :, :].unsqueeze(2).broadcast_to([128, 8, 2, 31]), ALU.mult),
                     reads=[rg("bufT"), rg("dww")], writes=[rg("tB", 0)])
                p.op("dve", lambda e, hb=hb: e.tensor_reduce(out=tA[0][:, hb * 16:hb * 16 + 16].rearrange("p (c b) -> p c b", c=8),
                                                              in_=tmp, axis=AX.X, op=ALU.add),
                     reads=[rg("tB", 0)], writes=[rg("tA", 0), ysr])
            for c in range(8):
                db = c % 2
                dg = diagB[db]
                p.op("pool", TTo(dg[:, :, :], ident_b[:, :].unsqueeze(1).broadcast_to([128, 31, 128]),
                                 dww[:, l, c, :].unsqueeze(2).broadcast_to([128, 31, 128]), ALU.mult),
                     reads=[rg("identb"), rg("dww")], writes=[rg("diag", db)] + ([rg("auxB"), rg("clf"), rg("krf"), rg("Vh", 1)] if db == 1 else [rg("auxA"), rg("Vh", 0)] + [rg("Rq", ti) for ti in range(5)]))
                for ti, (t0, n) in enumerate(TT):
                    pc, pcr = nps()
                    p.op("pe", [MM(pc[:, 0:n], dg[:, k, :], gpad[:, c, t0 + k:t0 + k + n], k == 0, k == 30) for k in range(31)],
                         reads=[rg("diag", db), rg("gpad", c)], writes=[pcr])
                    p.op("act", A_(hT[:, c, t0:t0 + n], pc[:, 0:n], AF.Identity, bias=V("a_dw_b%d" % l, c)),
                         reads=[pcr, rg("vecs")], writes=[rg("yc", c, ti)] + ([rg("hT", ti)] if c == 0 else []))
                for hb in range(2):
                    p.op("dve", TS(hT[:, c, NP + hb * 2:NP + 2 + hb * 2], tA[0][:, hb * 16 + c * 2:hb * 16 + c * 2 + 2],
                                   V("a_dw_b%d" % l, c), None, ALU.add),
                         reads=[ysr, rg("tA", 0), rg("vecs")], writes=[rg("yc", c, 4)])
            def ln_tile(ti):
                t0, n = TT[ti]
                ps1, ps1r = aps()
                ps2, ps2r = aps()
                for c in range(8):
                    b = c % 2
                    p.op("pool", TTo(sqb[b][:, 0:n], hT[:, c, t0:t0 + n], hT[:, c, t0:t0 + n], ALU.mult), reads=[rg("yc", c, ti)], writes=[rg("sqb", b)])
                    p.op("pe", [MM(ps2[:, 0:n], ones_b[:, :], sqb[b][:, 0:n], c == 0, c == 7),
                                MM(ps1[:, 0:n], ones_b[:, :], hT[:, c, t0:t0 + n], c == 0, c == 7)],
                         reads=[rg("sqb", b), rg("ones"), rg("yc", c, ti)], writes=[ps1r, ps2r])
                mean = tA[1][:, 0:n]
                var = tA[0][:, 0:n]
                mr = rg("tA", 1)
                vr = rg("tA", 0)
                p.op("dve", TS(mean, ps1[:, 0:n], 1.0 / 1024.0, None, ALU.mult), reads=[ps1r], writes=[mr])
                p.op("dve", TTo(var, mean, mean, ALU.mult), reads=[mr, ysr], writes=[vr])
                p.op("dve", STT(var, ps2[:, 0:n], 1.0 / 1024.0, var, ALU.mult, ALU.subtract), reads=[ps2r, vr], writes=[vr])
                p.op("act", A_(var, var, AF.Sqrt, bias=cols[:, 3:4]), reads=[vr, rg("cols")], writes=[vr])
                p.op("dve", lambda e, var=var: e.reciprocal(out=var, in_=var), reads=[vr], writes=[vr])
                for c in range(8):
                    b = c % 2
                    eng = "dve"
                    tz = tB[b][:, 0:n]
                    p.op(eng, TTo(tz, hT[:, c, t0:t0 + n], mean, ALU.subtract), reads=[rg("yc", c, ti), mr], writes=[rg("tB", b)])
                    p.op(eng, TTo(tz, tz, var, ALU.mult), reads=[vr, rg("tB", b)], writes=[rg("tB", b)])
                    p.op("act", A_(hT[:, c, t0:t0 + n], tz, AF.Silu, bias=V("a_ln_b%d" % l, c), scale=V("a_ln_g%d" % l, c)),
                         reads=[rg("tB", b), rg("vecs")], writes=[rg("yc", c, ti)])

            def pw2(Wv, wr, m, ti):
                t0, n = TT[ti]
                po, por = nps()
                p.op("pe", [MM(po[:, 0:n], Wv[:, k, :], hT[:, k, t0:t0 + n], k == 0, k == 7) for k in range(8)],
                     reads=[wr, rg("hT", ti)] + [rg("yc", c, ti) for c in range(8)], writes=[por])
                p.op("dve", STT(xT[:, m, t0:t0 + n], po[:, 0:n], V("a_pw2_b%d" % l, m), xT[:, m, t0:t0 + n], ALU.add, ALU.add),
                     reads=[por, rg("vecs")], writes=[rg("xT", m, ti)])
            KF2 = 4
            Wf2 = []
            for m in range(KF2):
                W, wr = ring_load(w_pw2[l, m], 1024)
                Wf2.append((W[:, 0:1024].rearrange("p (k j) -> p k j", k=8), wr))
            ln_tile(0)
            for ti in range(5):
                if ti + 1 < 5:
                    ln_tile(ti + 1)
                for m in range(KF2):
                    pw2(Wf2[m][0], Wf2[m][1], m, ti)
            for m in range(KF2, 8):
                W, wr = ring_load(w_pw2[l, m], 1024)
                Wv = W[:, 0:1024].rearrange("p (k j) -> p k j", k=8)
                for ti in range(5):
                    pw2(Wv, wr, m, ti)
            ffn(l)

        for l in range(2):
            a_layer(l)

        Wl, wlr = ring_load(w_dkvl, 2048)
        Wlv = Wl.rearrange("p (k j) -> p k j", k=8)
        Wr_, wrr = ring_load(w_dkvr, 512)
        Wrv = Wr_[:, 0:512].rearrange("p (k j) -> p k j", k=8)
        clf = auxB[:, 0:1024].rearrange("p (c t) -> p c t", c=2)
        krf = auxB[0:32, 1024:1536]
        def kv_tile(ti):
            t0, n = TT[ti]
            pl = [nps() for _ in range(2)]
            for kc in range(2):
                p.op("pe", [MM(pl[kc][0][:, 0:n], Wlv[:, k, kc * 128:(kc + 1) * 128], hT[:, k, t0:t0 + n], k == 0, k == 7) for k in range(8)],
                     reads=[wlr] + HR(ti), writes=[pl[kc][1]])
            pr1, pr1r = nps()
            pr2, pr2r = nps()
            p.op("pe", [MM(pr1[0:32, 0:n], Wrv[:, k, 0:32], hT[:, k, t0:t0 + n], k == 0, k == 7) for k in range(8)],
                 reads=[wrr] + HR(ti), writes=[pr1r])
            p.op("pe", [MM(pr2[0:32, 0:n], Wrv[:, k, 32:64], hT[:, k, t0:t0 + n], k == 0, k == 7) for k in range(8)],
                 reads=[wrr] + HR(ti), writes=[pr2r])
            t, tr_ = rstd_from([pl[0][0][:, 0:n], pl[1][0][:, 0:n]], [pl[0][1], pl[1][1]], n, 256.0)
            for kc in range(2):
                p.op("dve", STT(clf[:, kc, 0:n], pl[kc][0][:, 0:n], V("kvl_g", kc), t[:, 0:n], ALU.mult, ALU.mult),
                     reads=[pl[kc][1], tr_, rg("vecs")], writes=[rg("clf")])
            p.op("act", A_(cT[:, :, t0:t0 + n], clf[:, :, 0:n], AF.Copy), reads=[rg("clf")], writes=[rg("cT", ti)])
            t1 = tB[0][0:32, 0:n]
            t2 = tB[1][0:32, 0:n]
            p.op("dve", TTo(t1, pr1[0:32, 0:n], C4[0:32, t0:t0 + n], ALU.mult), reads=[pr1r, rg("tabs")], writes=[rg("tB", 0)])
            p.op("dve", TTo(t2, pr2[0:32, 0:n], S4[0:32, t0:t0 + n], ALU.mult), reads=[pr2r, rg("tabs")], writes=[rg("tB", 1)])
            p.op("dve", TTo(krf[:, 0:n], t1, t2, ALU.add), reads=[rg("tB", 0), rg("tB", 1)], writes=[rg("krf")])
            p.op("act", A_(krT[:, t0:t0 + n], krf[:, 0:n], AF.Copy), reads=[rg("krf")], writes=[rg("krT", ti)])
            for bi in range((n + 127) // 128):
                c0 = bi * 128
                nb = min(128, n - c0)
                b = bi % 2
                pt_, ptr_ = nps()
                p.op("pe", [TR(pt_[0:nb, 0:128], clf[:, 0, c0:c0 + nb], ident_f[:, :]),
                            TR(pt_[0:nb, 128:256], clf[:, 1, c0:c0 + nb], ident_f[:, :]),
                            TR(pt_[0:nb, 256:288], krf[:, c0:c0 + nb], ident_f[0:32, 0:32])],
                     reads=[rg("clf"), rg("krf"), rg("ident")], writes=[ptr_])
                p.op("act", A_(io[b][0:nb, 0:288], pt_[0:nb, 0:288], AF.Copy), reads=[ptr_], writes=[rg("io", b)])
                p.dma("sp", lat_all[t0 + c0:t0 + c0 + nb, :], io[b][0:nb, 0:256], d_out, reads=[rg("io", b)])
                p.dma("sp", kr_all[t0 + c0:t0 + c0 + nb, :], io[b][0:nb, 256:288], d_out, reads=[rg("io", b)])

        first_pass("kv_g", kv_tile)

        Vh = [auxA[:, 2080:2080 + 2176].rearrange("p (k j) -> p k j", k=17),
              auxB[:, :].bitcast(BF16)[:, 0:2176].rearrange("p (k j) -> p k j", k=17)]
        OT = hT
        QhB = [R2[:, 3 * NT:4 * NT], R2[:, 5 * NT:6 * NT]]
        KhB = [R2[:, 4 * NT:5 * NT], iob[:, 0:NT]]

        def b_layer(j):
            l = 2 + j
            Wd = [ring_load(w_dq[j, n3], 1024) for n3 in range(3)]

            def dq_tile(ti):
                t0, n = TT[ti]
                pq = [nps() for _ in range(3)]
                for n3 in range(3):
                    Wv = Wd[n3][0][:, 0:1024].rearrange("p (k j) -> p k j", k=8)
                    p.op("pe", [MM(pq[n3][0][:, 0:n], Wv[:, k, :], hT[:, k, t0:t0 + n], k == 0, k == 7) for k in range(8)],
                         reads=[Wd[n3][1]] + HR(ti), writes=[pq[n3][1]])
                t, tr_ = rstd_from([pq[i][0][:, 0:n] for i in range(3)], [pq[i][1] for i in range(3)], n, 384.0)
                for n3 in range(3):
                    p.op("dve", STT(cqn[:, n3, t0:t0 + n], pq[n3][0][:, 0:n], V("bq_g%d" % j, n3), t[:, 0:n], ALU.mult, ALU.mult),
                         reads=[pq[n3][1], tr_, rg("vecs")], writes=[rg("cqn", ti)])
            first_pass("b_g%d" % j, dq_tile)
            p.op("pool", lambda e: e.memset(Vh[0][:, :, 64:128], 1.0), writes=[rg("Vh", 0), rg("auxA"), rg("auxB"), rg("diag", 0), rg("diag", 1), rg("clf"), rg("krf")])
            p.op("pool", lambda e: e.memset(Vh[1][:, :, 0:64], 1.0), writes=[rg("Vh", 1), rg("auxA"), rg("auxB"), rg("diag", 0), rg("diag", 1), rg("clf"), rg("krf")] + [rg("ybuf", c) for c in range(8)])
            def rope(q4):
                W, wr = ring_load(w_rs[j, q4], 768)
                Wv = W[:, 0:768].rearrange("p (k s j) -> p k s j", k=3, s=2)
                for ti, (t0, n) in enumerate(TT):
                    pR, pRr = nps()
                    pS, pSr = nps()
                    p.op("pe", [MM(pR[:, 0:n], Wv[:, k, 0, :], cqn[:, k, t0:t0 + n], k == 0, k == 2) for k in range(3)],
                         reads=[wr, rg("cqn", ti)], writes=[pRr])
                    p.op("pe", [MM(pS[:, 0:n], Wv[:, k, 1, :], cqn[:, k, t0:t0 + n], k == 0, k == 2) for k in range(3)],
                         reads=[wr, rg("cqn", ti)], writes=[pSr])
                    p.op("dve", TTo(tB[0][:, 0:n], pR[:, 0:n], C4[:, t0:t0 + n], ALU.mult), reads=[pRr, rg("tabs")], writes=[rg("tB", 0)])
                    p.op("dve", TTo(tB[1][:, 0:n], pS[:, 0:n], S4[:, t0:t0 + n], ALU.mult), reads=[pSr, rg("tabs")], writes=[rg("tB", 1)])
                    p.op("pool", TTo(Rq[:, t0:t0 + n], tB[0][:, 0:n], tB[1][:, 0:n], ALU.add),
                         reads=[rg("tB", 0), rg("tB", 1)], writes=[rg("Rq", ti)])

            def gen(h):
                par = h % 2
                e_ = par
                vo = 64 * e_
                Qh = QhB[par]
                Kh = KhB[par]
                kx = [rg("io", 0), rg("io", 1)] if par == 1 else []
                W, wr = ring_load(w_hd[j, h], 608)
                uqn = W[:, 0:288].rearrange("p (k j) -> p k j", k=3)
                ukp = W[:, 288:480].rearrange("p (k j) -> p k j", k=2)
                uvh = W[:, 480:608].rearrange("p (k j) -> p k j", k=2)
                for ti, (t0, n) in enumerate(TT):
                    pq_, pqr = nps()
                    p.op("pe", [MM(pq_[0:96, 0:n], uqn[:, k, :], cqn[:, k, t0:t0 + n], k == 0, False) for k in range(3)]
                         + [MM(pq_[0:96, 0:n], selQ[:, h % 4, :], Rq[:, t0:t0 + n], False, True)],
                         reads=[wr, rg("cqn", ti), rg("Rq", ti), rg("selQ")], writes=[pqr])
                    p.op("act", A_(Qh[0:96, t0:t0 + n], pq_[0:96, 0:n], AF.Copy), reads=[pqr], writes=[rg("Qh", par, ti)])
                    if ti == 4:
                        p.op("act", A_(Qs[:, h, :], pq_[0:96, 16:20], AF.Copy), reads=[pqr], writes=[rg("Qs")])
                    nk_ = min(n, NP - t0)
                    pk_, pkr = nps()
                    p.op("pe", [MM(pk_[0:96, 0:nk_], ukp[:, k, :], cT[:, k, t0:t0 + nk_], k == 0, False) for k in range(2)]
                         + [MM(pk_[0:96, 0:nk_], selK[:, :], krT[:, t0:t0 + nk_], False, True)],
                         reads=[wr, rg("cT", ti), rg("krT", ti), rg("selK")], writes=[pkr])
                    p.op("dve", CP(Kh[0:96, t0:t0 + nk_], pk_[0:96, 0:nk_]), reads=[pkr], writes=[rg("Kh", par, ti)] + kx)
                for g0 in (0, 8, 16):
                    ng = min(8, 17 - g0)
                    pv, pvr = nps()
                    fl = []
                    for i in range(ng):
                        kb = g0 + i
                        k0 = kb * 128
                        nk = min(128, NP - k0)
                        for k in range(2):
                            fl.append(MM(pv[0:nk, i * 64:(i + 1) * 64], cT[:, k, k0:k0 + nk], uvh[:, k, :], k == 0, k == 1))
                    p.op("pe", fl, reads=[wr] + [rg("cT", ti) for ti in range(5)], writes=[pvr])
                    nkk = 128 if g0 < 16 else 16
                    p.op("dve", CP(Vh[e_][0:nkk, g0:g0 + ng, vo:vo + 64], pv[0:nkk, 0:ng * 64].rearrange("p (i d) -> p i d", i=ng)),
                         reads=[pvr], writes=[rg("Vh", e_)])
                if h % 4 == 3 and h < 15:
                    rope(h // 4 + 1)

            def attn(h):
                par = h % 2
                e_ = par
                pair = h // 2
                vo = 64 * e_
                do = 64 - vo
                Qh = QhB[par]
                Kh = KhB[par]
                kx = [rg("io", 0), rg("io", 1)] if par == 1 else []
                steps = []
                for ti, (q0, n) in enumerate(TT):
                    nq = min(n, NP - q0)
                    nkb = (q0 + nq - 1) // 128 + 1
                    pO, pOr = aps()
                    for kb in range(nkb):
                        steps.append((ti, q0, nq, nkb, kb, pO, pOr))
                LOOK = 2
                pis = {}

                def emit_S(i):
                    ti, q0, nq, nkb, kb, pO, pOr = steps[i]
                    k0 = kb * 128
                    nk = min(128, NP - k0)
                    qs = max(q0, k0)
                    w = q0 + nq - qs
                    pS_, pSr_ = nps()
                    if k0 >= q0:
                        dw = min(128, w)
                        p.op("pe", [MM(pS_[0:nk, 0:w], Kh[0:96, k0:k0 + nk], Qh[0:96, qs:qs + w], True, False),
                                    MM(pS_[0:nk, 0:dw], ident_b[0:nk, 0:nk], negm[0:nk, 0:dw], False, True)],
                             reads=[rg("Kh", par, k0 // 512), rg("Qh", par, ti), rg("negm"), rg("identb")] + kx, writes=[pSr_])
                    else:
                        p.op("pe", MM(pS_[0:nk, 0:w], Kh[0:96, k0:k0 + nk], Qh[0:96, qs:qs + w], True, True),
                             reads=[rg("Kh", par, k0 // 512), rg("Qh", par, ti)] + kx, writes=[pSr_])
                    pi = att_ctr[0] % 3
                    att_ctr[0] += 1
                    pis[i] = pi
                    p.op("act", A_(PT[pi][0:nk, 0:w], pS_[0:nk, 0:w], AF.Exp, scale=SCALE), reads=[pSr_], writes=[rg("PT", pi)])

                def emit_PV(i):
                    ti, q0, nq, nkb, kb, pO, pOr = steps[i]
                    k0 = kb * 128
                    nk = min(128, NP - k0)
                    qs = max(q0, k0)
                    w = q0 + nq - qs
                    pi = pis[i]
                    p.op("pe", MM(pO[:, qs - q0:qs - q0 + w], Vh[e_][0:nk, kb, :], PT[pi][0:nk, 0:w], kb == 0, kb == nkb - 1),
                         reads=[rg("PT", pi), rg("Vh", e_)], writes=[pOr])
                    if kb == nkb - 1:
                        b = ti % 2
                        p.op("dve", lambda e, b=b, pO=pO, nq=nq, do=do: e.reciprocal(out=tA[b][do:do + 64, 0:nq], in_=pO[do:do + 64, 0:nq]),
                             reads=[pOr], writes=[rg("tA", b)])
                        p.op("dve", TTo(OT[vo:vo + 64, pair, q0:q0 + nq], pO[vo:vo + 64, 0:nq], tA[b][do:do + 64, 0:nq], ALU.mult),
                             reads=[pOr, rg("tA", b)], writes=[rg("hT", ti)])

                for i in range(len(steps)):
                    emit_S(i)
                    if i >= LOOK:
                        emit_PV(i - LOOK)
                for i in range(max(0, len(steps) - LOOK), len(steps)):
                    emit_PV(i)

            rope(0)
            gen(0)
            for h in range(16):
                if h + 1 < 16:
                    gen(h + 1)
                attn(h)
            if debug and j == 0:
                allr = list(p.regs.values())
                dd = lambda name, shape, dt=BF16: nc.dram_tensor(name, list(shape), dt, kind="ExternalOutput").ap()
                p.dma("sp", dd("dbg_OT", [128, 8 * NT]), R1[:, :], d_out, reads=allr)
                p.dma("sp", dd("dbg_R2", [128, 8 * 2098]), R2[:, :], d_out, reads=allr)
                p.dma("sp", dd("dbg_auxA", [128, 4352]), auxA[:, :], d_out, reads=allr)
                p.dma("sp", dd("dbg_auxB", [128, 2048], F32), auxB[:, :], d_out, reads=allr)
                p.dma("sp", dd("dbg_tri", [128, 128]), tri[:, :], d_out, reads=allr)
                p.dma("sp", dd("dbg_PT", [128, 512]), PT[0][:, :], d_out, reads=allr)
                p.dma("sp", dd("dbg_tA", [128, 512], F32), tA[0][:, :], d_out, reads=allr)
                p.dma("sp", dd("dbg_xT", [128, 8 * NT], F32), xT[:, :, :].rearrange("p c t -> p (c t)"), d_out, reads=allr)
                p.dma("sp", dd("dbg_C4", [128, NT]), C4[:, :], d_out, reads=allr)
                p.dma("sp", dd("dbg_S4", [128, NT]), S4[:, :], d_out, reads=allr)
            if stage >= 4:
                sample_attention(j)
            else:
                p.op("pool", lambda e: e.memset(OT[:, :, NP:NT], 0.0), writes=[rg("hT", 4)])
            for m in range(8):
                W, wr = ring_load(w_o[j, m], 1024)
                Wv = W[:, 0:1024].rearrange("p (k j) -> p k j", k=8)
                for ti, (t0, n) in enumerate(TT):
                    po, por = nps()
                    p.op("pe", [MM(po[:, 0:n], Wv[:, k, :], OT[:, k, t0:t0 + n], k == 0, k == 7) for k in range(8)],
                         reads=[wr, rg("hT", ti)], writes=[por])
                    p.op("dve", TTo(xT[:, m, t0:t0 + n], po[:, 0:n], xT[:, m, t0:t0 + n], ALU.add),
                         reads=[por], writes=[rg("xT", m, ti)])
            ffn(l)

        att_ctr = [0]

        ukT = auxA[0:64, 0:4096].rearrange("p (h c) -> p h c", h=16)
        uvp = auxB[:, :].bitcast(BF16).rearrange("p (k h j) -> p k h j", k=2, h=16)
        SA = R2[:, 0:6 * NT]
        CLb2 = [SA[:, i * 2048:(i + 1) * 2048] for i in range(4)]
        CLb = [c_.rearrange("p (r c) -> p r c", r=8) for c_ in CLb2]
        KRg = [iob[:, 3072 + i * 256:3072 + (i + 1) * 256] for i in range(4)]
        tcT = [SA[:, 8192 + i * 1536: 8192 + i * 1536 + 1024].rearrange("p (k t) -> p k t", k=2) for i in range(2)]
        tkr = [SA[:, 8192 + i * 1536 + 1024: 8192 + i * 1536 + 1152] for i in range(2)]
        QRz = SA[:, 8192 + 1152:8192 + 1152 + 256].rearrange("p (r b h) -> p r b h", r=4, b=4)
        o0 = 8192 + 3072
        QL = SA[:, o0:o0 + 128].rearrange("p (k b h) -> p k b h", k=2, b=4)
        QR = SA[0:32, o0 + 128:o0 + 192].rearrange("p (b h) -> p b h", b=4)
        OLT = SA[:, o0 + 192:o0 + 320].rearrange("p (k b h) -> p k b h", k=2, b=4)
        PTs = [SA[:, o0 + 320 + i * 64:o0 + 384 + i * 64] for i in range(2)]
        PTn = SA[0:1, o0 + 448:o0 + 464]
        crow = SA[0:1, o0 + 464:o0 + 720]
        OL = SA[0:16, o0 + 720:o0 + 976]
        rd = cols[0:16, 5:6]
        d_gl = [p.new_sem("d_gl%d" % i) for i in range(4)]
        d_gk = [p.new_sem("d_gk%d" % i) for i in range(4)]

        def sample_attention(j):
            sar = rg("sa_small")
            p.dma("pool", auxA[0:64, 0:4096], w_ukT, d_misc, writes=[rg("auxA"), rg("diag", 0)] + [rg("Rq", ti) for ti in range(5)] + [rg("Vh", 0), rg("Vh", 1)])
            p.dma("pool", auxB[:, :].bitcast(BF16), w_uvp, d_misc, writes=[rg("auxB"), rg("diag", 1), rg("clf"), rg("krf")] + [rg("ybuf", c) for c in range(8)])
            sa_regs = ([rg("cqn", ti) for ti in range(5)] + [rg("Qh", pp, ti) for ti in range(5) for pp in range(2)]
                       + [rg("Kh", pp, ti) for ti in range(5) for pp in range(2)] + [rg("io", 1)])
            pq1, pq1r = nps()
            fl = []
            for h in range(16):
                for kc in range(2):
                    c0 = (kc * 16 + h) * 4
                    fl.append(MM(pq1[:, c0:c0 + 4], ukT[:, h, kc * 128:(kc + 1) * 128], Qs[0:64, h, :], True, True))
            p.op("pe", fl, reads=[rg("auxA"), rg("Qs")], writes=[pq1r])
            p.op("act", A_(QL.rearrange("p k b h -> p k h b"), pq1[:, 0:128].rearrange("p (k h b) -> p k h b", k=2, h=16), AF.Copy),
                 reads=[pq1r], writes=[sar] + sa_regs)
            pq2, pq2r = nps()
            p.op("pe", [MM(pq2[0:32, h * 4:h * 4 + 4], ident_b[0:96, 64:96], Qs[0:96, h, :], True, True) for h in range(16)],
                 reads=[rg("identb"), rg("Qs")], writes=[pq2r])
            p.op("pool", lambda e: e.memset(QRz[:, :, :, :], 0.0), reads=[pq1r], writes=[sar] + sa_regs)
            for rr in range(4):
                p.op("dve", CP(QRz[32 * rr:32 * rr + 32, rr, :, :].rearrange("p b h -> p h b"),
                               pq2[0:32, 0:64].rearrange("p (h b) -> p h b", h=16)), reads=[pq2r], writes=[sar])
            steps = [(b4, g, r4) for b4 in range(4) for g in range(16) for r4 in range(2)]
            acc = {}
            st_ = {}

            def finalize(b4):
                pO, pOr = acc[b4]
                pD = pO[0:16, 256:257]
                col = NP + b4
                pS_, pSr_ = nps()
                p.op("pe", [MM(pS_[0:1, 0:16], cT[:, 0, col:col + 1], QL[:, 0, b4, :], True, False),
                            MM(pS_[0:1, 0:16], cT[:, 1, col:col + 1], QL[:, 1, b4, :], False, False),
                            MM(pS_[0:1, 0:16], krT[:, col:col + 1], QRz[0:32, 0, b4, :], False, True)],
                     reads=[rg("cT", 4), rg("krT", 4), sar], writes=[pSr_])
                p.op("act", A_(PTn, pS_[0:1, 0:16], AF.Exp, scale=SCALE), reads=[pSr_], writes=[rg("PTn")])
                pC, pCr = nps()
                pCb = pC[:, :].bitcast(BF16)
                p.op("pe", [TR(pCb[0:1, kc * 128:(kc + 1) * 128], cT[:, kc, col:col + 1], ident_b[:, :]) for kc in range(2)],
                     reads=[rg("cT", 4), rg("identb")], writes=[pCr])
                p.op("dve", CP(crow, pCb[0:1, 0:256]), reads=[pCr], writes=[rg("crow")])
                p.op("pe", [MM(pO[0:16, 0:256], PTn, crow, False, True), MM(pD, PTn, ones_b[0:1, 0:1], False, True)],
                     reads=[rg("PTn"), rg("crow"), rg("ones")], writes=[pOr])
                p.op("dve", lambda e, pD=pD: e.reciprocal(out=rd, in_=pD), reads=[pOr], writes=[rg("rd")])
                p.op("dve", TS(OL, pO[0:16, 0:256], rd, None, ALU.mult), reads=[pOr, rg("rd")], writes=[rg("OL")])
                pT2, pT2r = nps()
                pT2b = pT2[:, :].bitcast(BF16)
                p.op("pe", [TR(pT2b[:, kc * 16:(kc + 1) * 16], OL[:, kc * 128:(kc + 1) * 128], ident_b[0:16, 0:16]) for kc in range(2)],
                     reads=[rg("OL"), rg("identb")], writes=[pT2r])
                p.op("act", A_(OLT[:, :, b4, :], pT2b[:, 0:32].rearrange("p (k h) -> p k h", k=2), AF.Copy), reads=[pT2r], writes=[rg("OLT")])

            def emit_T(i):
                b4, g, r4 = steps[i]
                gi = b4 * 16 + g
                cb = gi % 4
                if r4 == 0:
                    if g == 0:
                        acc[b4] = aps()
                    p.raw("pool", lambda e, b4=b4, g=g, cb=cb: e.indirect_dma_start(
                        out=CLb2[cb], out_offset=None, in_=cache_lat[:, :],
                        in_offset=bass.IndirectOffsetOnAxis(ap=idxg[:, b4, g:g + 1], axis=0)),
                        d_gl[cb], reads=[rg("idxg")], writes=[rg("CLb", cb)])
                    p.raw("pool", lambda e, b4=b4, g=g, cb=cb: e.indirect_dma_start(
                        out=KRg[cb], out_offset=None, in_=cache_kr[:, :],
                        in_offset=bass.IndirectOffsetOnAxis(ap=idxg[:, b4, g:g + 1], axis=0)),
                        d_gk[cb], reads=[rg("idxg")], writes=[rg("KRg", cb)])
                tb = i % 2
                pA, pAr = nps()
                pB, pBr = nps()
                pAb = pA[:, :].bitcast(BF16).rearrange("p (k r t) -> p k r t", k=2, r=4)
                pBb = pB[:, :].bitcast(BF16)
                fl = []
                for rr in range(4):
                    r = r4 * 4 + rr
                    for kc in range(2):
                        fl.append(TR(pAb[:, kc, rr, :], CLb[cb][:, r, kc * 128:(kc + 1) * 128], ident_b[:, :]))
                fl.append(TR(pBb[:, 0:128], KRg[cb][:, r4 * 128:(r4 + 1) * 128], ident_b[:, :]))
                p.op("pe", fl, reads=[rg("CLb", cb), rg("KRg", cb), rg("identb")], writes=[pAr, pBr])
                p.op("act", A_(tcT[tb], pAb.rearrange("p k r t -> p k (r t)"), AF.Copy), reads=[pAr], writes=[rg("tcT", tb)])
                p.op("dve", CP(tkr[tb], pBb[:, 0:128]), reads=[pBr], writes=[rg("tkr", tb)])

            def emit_S(i):
                b4, g, r4 = steps[i]
                tb = i % 2
                pS_, pSr_ = nps()
                fl = []
                for rr in range(4):
                    o_ = pS_[:, rr * 16:(rr + 1) * 16]
                    fl.append(MM(o_, tcT[tb][:, 0, rr * 128:(rr + 1) * 128], QL[:, 0, b4, :], True, False))
                    fl.append(MM(o_, tcT[tb][:, 1, rr * 128:(rr + 1) * 128], QL[:, 1, b4, :], False, False))
                    fl.append(MM(o_, tkr[tb][:, :], QRz[:, rr, b4, :], False, True))
                p.op("pe", fl, reads=[rg("tcT", tb), rg("tkr", tb), sar], writes=[pSr_])
                p.op("act", A_(PTs[tb], pS_[:, 0:64], AF.Exp, scale=SCALE), reads=[pSr_], writes=[rg("PTs", tb)])

            def emit_V(i):
                b4, g, r4 = steps[i]
                gi = b4 * 16 + g
                cb = gi % 4
                tb = i % 2
                pO, pOr = acc[b4]
                pD = pO[0:16, 256:257]
                fl = []
                for rr in range(4):
                    r = r4 * 4 + rr
                    first = (g == 0 and r4 == 0 and rr == 0)
                    fl.append(MM(pO[0:16, 0:256], PTs[tb][:, rr * 16:(rr + 1) * 16], CLb[cb][:, r, :], first, False))
                    fl.append(MM(pD, PTs[tb][:, rr * 16:(rr + 1) * 16], ones_b[:, 0:1], first, False))
                p.op("pe", fl, reads=[rg("PTs", tb), rg("CLb", cb), rg("ones")], writes=[pOr])
                if g == 15 and r4 == 1:
                    finalize(b4)

            n_ = len(steps)
            for i in range(n_ + 2):
                if i < n_:
                    emit_T(i)
                if 1 <= i <= n_:
                    emit_S(i - 1)
                if i >= 2:
                    emit_V(i - 2)
            for pair in range(8):
                po, por = nps()
                fl = []
                i = 0
                for e2 in range(2):
                    for kc in range(2):
                        fl.append(MM(po[:, 0:4], uvp[:, kc, 2 * pair + e2, :], OLT[:, kc, :, 2 * pair + e2], i == 0, i == 3))
                        i += 1
                p.op("pe", fl, reads=[rg("auxB"), rg("OLT")], writes=[por])
                p.op("act", A_(OT[:, pair, NP:NT], po[:, 0:4], AF.Copy), reads=[por], writes=[rg("hT", 4)])

        sa_ctr = [0]

        if stage >= 3:
            for j in range(2):
                b_layer(j)

        R1f = R1[:, :].bitcast(F32)
        yos = [R1f[:, i * 4096:(i + 1) * 4096].rearrange("p (c t) -> p c t", c=8) for i in range(2)]

        def fin_norm(ti):
            t0, n = TT[ti]
            yo = yos[ti % 2]
            srcs = [xT[:, c, t0:t0 + n] for c in range(8)]
            t, tr_ = rstd_from(srcs, [rg("xT", c, ti) for c in range(8)], n, 1024.0, sq_eng="pool")
            for c in range(8):
                eng = "dve"
                p.op(eng, STT(yo[:, c, 0:n], xT[:, c, t0:t0 + n], V("fin_g", c), t[:, 0:n], ALU.mult, ALU.mult),
                     reads=[rg("xT", c, ti), tr_, rg("vecs")] + ([] if c == 0 else [rg("yo", ti % 2)]),
                     writes=([rg("yo", ti % 2)] + [r_ for k in range(5) for r_ in HR(k)]) if c == 0 else [rg("yoc", ti % 2, c)])

        def fin_out(ti):
            t0, n = TT[ti]
            yo = yos[ti % 2]
            for bi in range((n + 127) // 128):
                c0 = bi * 128
                nb = min(128, n - c0)
                b = bi % 2
                for half in range(2):
                    pt_, ptr_ = nps()
                    p.op("pe", [TR(pt_[0:nb, jj * 128:(jj + 1) * 128], yo[:, half * 4 + jj, c0:c0 + nb], ident_f[:, :]) for jj in range(4)],
                         reads=[rg("yo", ti % 2), rg("ident")] + [rg("yoc", ti % 2, c) for c in range(1, 8)], writes=[ptr_])
                    if half == 0:
                        p.op("act", A_(io[b][0:nb, 0:512], pt_[0:nb, :], AF.Copy), reads=[ptr_], writes=[rg("io", b)])
                    else:
                        p.op("dve", CP(io[b][0:nb, 512:1024], pt_[0:nb, :]), reads=[ptr_], writes=[rg("io", b)])
                p.dma("sp", y_all[t0 + c0:t0 + c0 + nb, :], io[b][0:nb, :], d_out, reads=[rg("io", b)])
        fin_norm(0)
        for ti in range(5):
            if ti + 1 < 5:
                fin_norm(ti + 1)
            fin_out(ti)
        p.q["sp"].append(lambda e: e.wait_ge(d_out.h, d_out.count))
        p.emit()
    return nc


def _cols(v):
    return np.ascontiguousarray(np.asarray(v, np.float32).reshape(-1, 128).T)


def _prep_shared(inp):
    f = lambda a: np.ascontiguousarray(np.asarray(a, dtype=np.float32))
    d = {}
    pw1 = f(inp["a_pw1_w"]).reshape(2, 8, 128, 2, 8, 128)
    d["w_pw1"] = f(pw1.transpose(0, 4, 2, 3, 1, 5).reshape(2, 8, 128, 2048))
    pw2 = f(inp["a_pw2_w"]).reshape(2, 8, 128, 8, 128)
    d["w_pw2"] = f(pw2.transpose(0, 3, 2, 1, 4).reshape(2, 8, 128, 1024))
    g = f(inp["ffn_w_gate"]).reshape(4, 8, 128, 22, 128)
    u = f(inp["ffn_w_up"]).reshape(4, 8, 128, 22, 128)
    gu = np.stack([g, u], axis=0)
    d["w_gu"] = f(gu.transpose(1, 4, 3, 0, 2, 5).reshape(4, 22, 128, 2048))
    dn = f(inp["ffn_w_down"]).reshape(4, 22, 128, 8, 128)
    d["w_dn"] = f(dn.transpose(0, 3, 2, 1, 4).reshape(4, 8, 128, 2816))
    dq = f(inp["b_w_dq"]).reshape(2, 8, 128, 3, 128)
    d["w_dq"] = f(dq.transpose(0, 3, 2, 1, 4).reshape(2, 3, 128, 1024))
    uq = f(inp["b_w_uq"]).reshape(2, 3, 128, 16, 96)
    rope = uq[..., 64:96]
    swp = np.concatenate([uq[..., 80:96], uq[..., 64:80]], axis=-1)
    rs = np.stack([rope, swp], axis=0).reshape(2, 2, 3, 128, 4, 4 * 32)
    d["w_rs"] = f(rs.transpose(1, 4, 3, 2, 0, 5).reshape(2, 4, 128, 768))
    nope = np.zeros((2, 3, 128, 16, 96), np.float32)
    nope[..., 0:64] = uq[..., 0:64]
    uk = f(inp["w_uk"]).reshape(2, 128, 16, 64)
    ukp = np.zeros((2, 128, 16, 96), np.float32)
    ukp[..., 0:64] = uk
    uv = f(inp["w_uv"]).reshape(2, 128, 16, 64)
    hd = np.zeros((2, 16, 128, 608), np.float32)
    for j in range(2):
        hd[j, :, :, 0:288] = nope[j].transpose(2, 1, 0, 3).reshape(16, 128, 288)
        hd[j, :, :, 288:480] = ukp.transpose(2, 1, 0, 3).reshape(16, 128, 192)
        hd[j, :, :, 480:608] = uv.transpose(2, 1, 0, 3).reshape(16, 128, 128)
    d["w_hd"] = hd
    wo = f(inp["b_w_o"]).reshape(2, 8, 128, 8, 128)
    d["w_o"] = f(wo.transpose(0, 3, 2, 1, 4).reshape(2, 8, 128, 1024))
    dkv = f(inp["w_dkv"]).reshape(8, 128, 288)
    d["w_dkvl"] = f(dkv[..., 0:256].transpose(1, 0, 2).reshape(128, 2048))
    rsw = np.concatenate([dkv[..., 256:288], dkv[..., 272:288], dkv[..., 256:272]], axis=-1)
    d["w_dkvr"] = f(rsw.transpose(1, 0, 2).reshape(128, 512))
    ukf = f(inp["w_uk"])
    d["w_ukT"] = f(ukf.transpose(2, 1, 0).reshape(64, 4096))
    uvp = np.zeros((128, 2, 16, 128), np.float32)
    for h in range(16):
        uvp[:, :, h, (h % 2) * 64:(h % 2) * 64 + 64] = uv[:, :, h, :].transpose(1, 0, 2)
    d["w_uvp"] = f(uvp.reshape(128, 4096))
    vs = []
    for l in range(2):
        pass
    order = [("a_norm_g", 0), ("a_norm_g", 1), ("a_pw1_b", 0), ("a_pw1_b", 1), ("a_dw_b", 0), ("a_dw_b", 1),
             ("a_ln_g", 0), ("a_ln_g", 1), ("a_ln_b", 0), ("a_ln_b", 1), ("a_pw2_b", 0), ("a_pw2_b", 1),
             ("ffn_norm_g", 0), ("ffn_norm_g", 1), ("ffn_norm_g", 2), ("ffn_norm_g", 3)]
    for nm, i in order:
        vs.append(_cols(f(inp[nm])[i]))
    vs.append(_cols(inp["kv_norm_g"]))
    vs.append(_cols(inp["kv_latent_norm_g"]))
    vs.append(_cols(f(inp["b_norm_g"])[0]))
    vs.append(_cols(f(inp["b_norm_g"])[1]))
    vs.append(_cols(f(inp["b_q_norm_g"])[0]))
    vs.append(_cols(f(inp["b_q_norm_g"])[1]))
    vs.append(_cols(inp["final_norm_g"]))
    d["vecs"] = f(np.concatenate(vs, axis=1))
    assert d["vecs"].shape == (128, NV)
    dw = f(inp["a_dw_w"]).reshape(2, 31, 8, 128)
    d["dww"] = f(dw.transpose(3, 0, 2, 1).reshape(128, 2 * 8 * 31))
    d["cache_lat"] = f(inp["cache_latent"]).reshape(5120 * 16, 2048)
    d["cache_kr"] = f(inp["cache_krope"]).reshape(5120 * 16, 256)
    return d


_NC_CACHE = {}


def kernel(**inputs):
    shared = _prep_shared(inputs)
    xp = np.asarray(inputs["x_prompt"], np.float32)
    xs = np.asarray(inputs["x_sample"], np.float32)
    meta = np.asarray(inputs["meta_tokens"], np.float32)
    stc = np.asarray(inputs["state_conv"], np.float32)
    pt = np.asarray(inputs["page_table"], np.int32)
    in_maps = []
    for i in range(8):
        m = dict(shared)
        m["xin"] = np.ascontiguousarray(np.concatenate([meta, xp[i], xs[4 * i:4 * i + 4, 0]], axis=0))
        m["state"] = np.ascontiguousarray(stc[:, 4 * i:4 * i + 4].transpose(0, 2, 1, 3))
        m["ptT"] = np.ascontiguousarray(pt[4 * i:4 * i + 4].T)
        in_maps.append(m)
    if "nc" not in _NC_CACHE:
        _NC_CACHE["nc"] = build_program()
    res = run_bass_kernel_spmd(_NC_CACHE["nc"], in_maps, core_ids=list(range(8)))
    R = res.results
    y_prompt = np.stack([R[i]["y_all"][16:NP] for i in range(8)], 0)
    y_sample = np.concatenate([R[i]["y_all"][NP:NT] for i in range(8)], 0)[:, None, :]
    lat_p = np.stack([R[i]["lat_all"][0:NP] for i in range(8)], 0)
    kr_p = np.stack([R[i]["kr_all"][0:NP] for i in range(8)], 0)
    conv_p = np.stack([R[i]["conv_p"] for i in range(8)], 1)
    lat_s = np.concatenate([R[i]["lat_all"][NP:NT] for i in range(8)], 0)[:, None, :]
    kr_s = np.concatenate([R[i]["kr_all"][NP:NT] for i in range(8)], 0)[:, None, :]
    conv_s = np.concatenate([R[i]["conv_s"] for i in range(8)], 1)
    outs = (y_prompt, y_sample, lat_p, kr_p, conv_p, lat_s, kr_s, conv_s)
    return tuple(np.ascontiguousarray(o, dtype=np.float32) for o in outs)
```

```python
import math
import numpy as np
from contextlib import ExitStack
import concourse.bass as bass
import concourse.mybir as mybir
from concourse.bass_utils import run_bass_kernel_spmd

F32 = mybir.dt.float32
BF16 = mybir.dt.bfloat16
I32 = mybir.dt.int32
ALU = mybir.AluOpType
AF = mybir.ActivationFunctionType
AX = mybir.AxisListType

NT = 2068
NP = 2064
TT = [(0, 512), (512, 512), (1024, 512), (1536, 512), (2048, 20)]
CTL = [(i * 256, 256) for i in range(8)] + [(2048, 20)]
HGS = [(0, 6), (6, 6), (12, 5), (17, 5)]
EPS = 1e-6
SCALE = 1.0 / math.sqrt(96.0)
TWO_PI = 2.0 * math.pi

VOFF = {}
_o = 0
for _name, _n in [("a_norm_g0", 8), ("a_norm_g1", 8), ("a_pw1_b0", 16), ("a_pw1_b1", 16), ("a_dw_b0", 8), ("a_dw_b1", 8),
                  ("a_ln_g0", 8), ("a_ln_g1", 8), ("a_ln_b0", 8), ("a_ln_b1", 8), ("a_pw2_b0", 8), ("a_pw2_b1", 8),
                  ("ffn_g0", 8), ("ffn_g1", 8), ("ffn_g2", 8), ("ffn_g3", 8), ("kv_g", 8), ("kvl_g", 2),
                  ("b_g0", 8), ("b_g1", 8), ("bq_g0", 3), ("bq_g1", 3), ("fin_g", 8)]:
    VOFF[_name] = _o
    _o += _n
NV = _o


class Sem:
    def __init__(self, h, name):
        self.h = h
        self.name = name
        self.count = 0


class Reg:
    __slots__ = ("name", "lw", "rs")

    def __init__(self, name):
        self.name = name
        self.lw = None
        self.rs = []


class Prog:
    ENG = ("pe", "act", "dve", "pool", "sp")

    def __init__(self, nc, stack):
        self.nc = nc
        self.stack = stack
        self.q = {e: [] for e in self.ENG}
        self.sem = {e: self.new_sem("e_" + e) for e in self.ENG}
        self.waited = {e: {} for e in self.ENG}
        self.regs = {}

    def new_sem(self, name):
        return Sem(self.stack.enter_context(self.nc.semaphore(name)), name)

    def sb(self, name, shape, dt):
        return self.stack.enter_context(self.nc.sbuf_tensor("sb_" + name, list(shape), dt))

    def psum(self, name, shape, dt):
        return self.stack.enter_context(self.nc.psum_tensor(name, list(shape), dt))

    def rg(self, *key):
        r = self.regs.get(key)
        if r is None:
            r = Reg(str(key))
            self.regs[key] = r
        return r

    def _waits(self, eng, reads, writes):
        deps = {}

        def add(d):
            if d is None:
                return
            s, v = d
            if deps.get(s, 0) < v:
                deps[s] = v
        for r in reads:
            add(r.lw)
        for w in writes:
            add(w.lw)
            for d in w.rs:
                add(d)
        out = []
        wd = self.waited[eng]
        for s, v in deps.items():
            if eng == "pe" and s is self.sem["pe"]:
                continue
            if wd.get(s, 0) >= v:
                continue
            wd[s] = v
            out.append((s, v))
        return out

    def _commit(self, tok, reads, writes):
        for r in reads:
            r.rs.append(tok)
            if len(r.rs) > 48:
                m = {}
                for s, v in r.rs:
                    if m.get(s, 0) < v:
                        m[s] = v
                r.rs = list(m.items())
        for w in writes:
            w.lw = tok
            w.rs = []

    def op(self, eng, fns, reads=(), writes=()):
        if callable(fns):
            fns = [fns]
        waits = self._waits(eng, reads, writes)
        s = self.sem[eng]
        s.count += 1
        val = s.count
        q = self.q[eng]
        for ws, wv in waits:
            q.append(lambda e, ws=ws, wv=wv: e.wait_ge(ws.h, wv))
        for f in fns[:-1]:
            q.append(f)
        last = fns[-1]
        q.append(lambda e, last=last, s=s: last(e).then_inc(s.h, 1))
        self._commit((s, val), reads, writes)

    def dma(self, queue, out, in_, dsem, reads=(), writes=(), **kw):
        self.raw(queue, lambda e, out=out, in_=in_, kw=kw: e.dma_start(out=out, in_=in_, **kw), dsem, reads, writes)

    def raw(self, queue, fn, dsem, reads=(), writes=()):
        waits = self._waits(queue, reads, writes)
        dsem.count += 16
        val = dsem.count
        q = self.q[queue]
        for ws, wv in waits:
            q.append(lambda e, ws=ws, wv=wv: e.wait_ge(ws.h, wv))
        q.append(lambda e, fn=fn, dsem=dsem: fn(e).then_inc(dsem.h, 16))
        self._commit((dsem, val), reads, writes)

    def final_wait(self, eng, regs):
        for ws, wv in self._waits(eng, regs, ()):
            self.q[eng].append(lambda e, ws=ws, wv=wv: e.wait_ge(ws.h, wv))

    def emit(self):
        with self.nc.Block() as block:
            @block.tensor
            def _(e):
                for f in self.q["pe"]:
                    f(e)

            @block.scalar
            def _(e):
                for f in self.q["act"]:
                    f(e)

            @block.vector
            def _(e):
                for f in self.q["dve"]:
                    f(e)

            @block.gpsimd
            def _(e):
                for f in self.q["pool"]:
                    f(e)

            @block.sync
            def _(e):
                for f in self.q["sp"]:
                    f(e)


def A_(out, in_, func, **kw):
    return lambda e: e.activation(out=out, in_=in_, func=func, **kw)


def TTo(out, a, b, op):
    return lambda e: e.tensor_tensor(out=out, in0=a, in1=b, op=op)


def STT(out, a, s, b, op0, op1):
    return lambda e: e.scalar_tensor_tensor(out=out, in0=a, scalar=s, in1=b, op0=op0, op1=op1)


def TS(out, a, s1, s2, op0, op1=None):
    if op1 is None:
        return lambda e: e.tensor_scalar(out=out, in0=a, scalar1=s1, scalar2=None, op0=op0)
    return lambda e: e.tensor_scalar(out=out, in0=a, scalar1=s1, scalar2=s2, op0=op0, op1=op1)


def CP(out, in_):
    return lambda e: e.tensor_copy(out=out, in_=in_)


def MM(out, l, r, st, sp):
    return lambda e: e.matmul(out, lhsT=l, rhs=r, start=st, stop=sp)


def TR(out, in_, ident):
    return lambda e: e.transpose(out, in_, ident)


def build_program(stage=99, debug=False):
    nc = bass.Bass("TRN2", target_bir_lowering=False)

    def din(name, shape, dt=F32):
        return nc.dram_tensor(name, list(shape), dt, kind="ExternalInput").ap()

    def dout(name, shape):
        return nc.dram_tensor(name, list(shape), F32, kind="ExternalOutput").ap()

    xin = din("xin", [NT, 1024])
    state = din("state", [2, 30, 4, 1024])
    ptT = din("ptT", [128, 4], I32)
    cache_lat = din("cache_lat", [5120 * 16, 2048])
    cache_kr = din("cache_kr", [5120 * 16, 256])
    w_pw1 = din("w_pw1", [2, 8, 128, 2048])
    w_pw2 = din("w_pw2", [2, 8, 128, 1024])
    w_gu = din("w_gu", [4, 22, 128, 2048])
    w_dn = din("w_dn", [4, 8, 128, 2816])
    w_dq = din("w_dq", [2, 3, 128, 1024])
    w_rs = din("w_rs", [2, 4, 128, 768])
    w_hd = din("w_hd", [2, 16, 128, 608])
    w_o = din("w_o", [2, 8, 128, 1024])
    w_dkvl = din("w_dkvl", [128, 2048])
    w_dkvr = din("w_dkvr", [128, 512])
    w_ukT = din("w_ukT", [64, 4096])
    w_uvp = din("w_uvp", [128, 4096])
    vecs_d = din("vecs", [128, NV])
    dww_d = din("dww", [128, 2 * 8 * 31])
    y_all = dout("y_all", [NT, 1024])
    lat_all = dout("lat_all", [NT, 256])
    kr_all = dout("kr_all", [NT, 32])
    conv_p = dout("conv_p", [2, 30, 1024])
    conv_s = dout("conv_s", [2, 4, 30, 1024])

    with ExitStack() as st:
        p = Prog(nc, st)
        rg = p.rg
        xT = p.sb("xT", [128, 8, NT], F32)
        R1 = p.sb("R1", [128, 8 * NT], BF16)
        R2 = p.sb("R2", [128, 8 * 2098], BF16)
        ringt = p.sb("ring", [128, 4, 2048], BF16)
        auxA = p.sb("auxA", [128, 4352], BF16)
        auxB = p.sb("auxB", [128, 2048], F32)
        C4 = p.sb("C4", [128, NT], BF16)
        S4 = p.sb("S4", [128, NT], BF16)
        krT = p.sb("krT", [32, NT], BF16)
        ident_f = p.sb("ident_f", [128, 128], F32)
        ident_b = p.sb("ident_b", [128, 128], BF16)
        ones_b = p.sb("ones_b", [128, 128], BF16)
        tri = p.sb("tri", [128, 128], BF16)
        negm = p.sb("negm", [128, 128], BF16)
        selK = p.sb("selK", [32, 96], BF16)
        selQ = p.sb("selQ", [128, 4, 96], BF16)
        vecs = p.sb("vecs", [128, NV], F32)
        dww = p.sb("dww", [128, 2, 8, 31], F32)
        sqb = [p.sb("sqb%d" % i, [128, 512], BF16) for i in range(2)]
        tA = [p.sb("tA%d" % i, [128, 512], F32) for i in range(2)]
        tB = [p.sb("tB%d" % i, [128, 512], F32) for i in range(2)]
        PT = [p.sb("PT%d" % i, [128, 512], BF16) for i in range(3)]
        iot = p.sb("iot", [128, 2, 1024], F32)
        io = [iot[:, 0, :], iot[:, 1, :]]
        iob = iot[:, :, :].rearrange("p a b -> p (a b)").bitcast(BF16)
        g_tail = p.sb("g_tail", [128, 8, 34], F32)
        bufT = p.sb("bufT", [128, 8, 4, 31], F32)
        cols = p.sb("cols", [128, 8], F32)
        icol = p.sb("icol", [128, 4], I32)
        idx = p.sb("idx", [128, 4], I32)
        idxg = p.sb("idxg", [128, 4, 16], I32)
        Qs = p.sb("Qs", [96, 16, 4], BF16)
        PS = [p.psum("ps%d" % i, [128, 512], F32) for i in range(8)]
        psr = [rg("ps", i) for i in range(8)]
        ps_ctr = [0]

        def nps():
            i = ps_ctr[0] % 6
            ps_ctr[0] += 1
            return PS[i], psr[i]

        aps_ctr = [0]

        def aps():
            i = 6 + aps_ctr[0] % 2
            aps_ctr[0] += 1
            return PS[i], psr[i]

        d_const = p.new_sem("d_const")
        d_const2 = p.new_sem("d_const2")
        d_const3 = p.new_sem("d_const3")
        d_misc2 = p.new_sem("d_misc2")
        d_o = [p.new_sem("d_o%d" % i) for i in range(2)]
        d_io = [p.new_sem("d_io%d" % i) for i in range(2)]
        d_out = p.new_sem("d_out")
        d_ring = [p.new_sem("d_ring%d" % i) for i in range(4)]
        d_misc = p.new_sem("d_misc")
        out_reg = rg("out")

        hT = R1[:, :].rearrange("p (c t) -> p c t", c=8)
        gpad = R2[:, :].rearrange("p (c t) -> p c t", c=8)
        hid = R2[:, 0:6 * NT].rearrange("p (c t) -> p c t", c=6)
        cqn = R2[:, 0:3 * NT].rearrange("p (c t) -> p c t", c=3)
        Qh = R2[:, 3 * NT:4 * NT]
        Kh = R2[:, 4 * NT:5 * NT]
        cT = R2[:, 6 * NT:8 * NT].rearrange("p (c t) -> p c t", c=2)
        diagB = [auxA[:, 0:31 * 128].rearrange("p (k j) -> p k j", k=31),
                 auxB[:, :].bitcast(BF16)[:, 0:31 * 128].rearrange("p (k j) -> p k j", k=31)]
        Rq = auxA[:, 0:NT]
        ybuf = auxB[:, :].rearrange("p (c t) -> p c t", c=8)
        ring_ctr = [0]

        def ring_load(src_ap, n):
            i = ring_ctr[0] % 4
            ring_ctr[0] += 1
            r = rg("ring", i)
            np_ = src_ap.shape[0]
            dst = ringt[0:np_, i, 0:n]
            p.dma("pool", dst, src_ap, d_ring[i], writes=[r])
            return ringt[:, i, :], r

        def V(name, c=0):
            o = VOFF[name] + c
            return vecs[:, o:o + 1]

        p.dma("sp", vecs[:, :], vecs_d, d_const, writes=[rg("vecs")])
        p.dma("sp", dww[:, :, :, :], dww_d.rearrange("p (l c k) -> p l c k", l=2, c=8), d_const2, writes=[rg("dww")])
        p.dma("sp", idx[:, :], ptT, d_const3, writes=[rg("idx")])
        for g in range(16):
            p.op("dve", TS(idxg[:, :, g], idx[:, :], 16, g, ALU.mult, ALU.add), reads=[rg("idx")], writes=[rg("idxg")])
        R2f = R2[:, :].bitcast(F32)
        R2i = R2[:, :].bitcast(I32)
        iota_row = R2f[:, 0:128]
        scr = rg("scr")
        p.op("pool", lambda e: e.iota(iota_row, pattern=[[1, 128]], base=0, channel_multiplier=0,
                                      allow_small_or_imprecise_dtypes=True), writes=[scr])
        p.op("pool", lambda e: e.iota(cols[:, 0:1], pattern=[[0, 1]], base=0, channel_multiplier=1,
                                      allow_small_or_imprecise_dtypes=True), writes=[rg("cols")])
        p.op("pool", lambda e: e.iota(icol[:, 0:1], pattern=[[0, 1]], base=0, channel_multiplier=1), writes=[rg("icol")])
        p.op("dve", TS(ident_f[:, :], iota_row, cols[:, 0:1], None, ALU.is_equal), reads=[scr, rg("cols")], writes=[rg("ident")])
        p.op("dve", CP(ident_b[:, :], ident_f[:, :]), reads=[rg("ident")], writes=[rg("identb")])
        p.op("dve", TS(tri[:, :], iota_row, cols[:, 0:1], None, ALU.is_ge), reads=[scr, rg("cols")], writes=[rg("tri")])
        p.op("dve", TS(negm[:, :], tri[:, :], 30000.0, -30000.0, ALU.mult, ALU.add), reads=[rg("tri")], writes=[rg("negm")])
        p.op("pool", lambda e: e.memset(ones_b[:, :], 1.0), writes=[rg("ones")])
        p.op("pool", lambda e: e.memset(selK[:, :], 0.0), writes=[rg("selK")])
        p.op("pool", lambda e: e.memset(selQ[:, :, :], 0.0), writes=[rg("selQ")])
        p.op("dve", CP(selK[:, 64:96], ident_b[0:32, 0:32]), reads=[rg("identb")], writes=[rg("selK")])
        for i in range(4):
            p.op("dve", CP(selQ[:, i, 64:96], ident_b[:, 32 * i:32 * i + 32]), reads=[rg("identb")], writes=[rg("selQ")])
        p.op("dve", lambda e: e.tensor_single_scalar(out=icol[:, 1:2], in_=icol[:, 0:1], scalar=15, op=ALU.bitwise_and),
             reads=[rg("icol")], writes=[rg("icol")])
        p.op("dve", lambda e: e.tensor_single_scalar(out=icol[:, 2:3], in_=icol[:, 0:1], scalar=16, op=ALU.bitwise_and),
             reads=[rg("icol")], writes=[rg("icol")])
        p.op("dve", CP(cols[:, 1:3], icol[:, 1:3]), reads=[rg("icol")], writes=[rg("cols")])
        p.op("act", A_(cols[:, 1:2], cols[:, 1:2], AF.Exp, scale=-math.log(10000.0) / 16.0), reads=[rg("cols")], writes=[rg("cols")])
        p.op("dve", TS(cols[:, 2:3], cols[:, 2:3], 1.0 / 8.0, -1.0, ALU.mult, ALU.add), reads=[rg("cols")], writes=[rg("cols")])
        p.op("pool", lambda e: e.memset(cols[:, 3:4], EPS), writes=[rg("cols")])
        p.op("pool", lambda e: e.memset(cols[:, 4:5], 0.0), writes=[rg("cols")])
        pos = R2f[:, 0:NT]
        ang = R2f[:, NT:2 * NT]
        kf = R2f[:, 2 * NT:3 * NT]
        ki = R2i[:, 3 * NT:4 * NT]
        p.op("pool", lambda e: e.iota(pos, pattern=[[1, NT]], base=0, channel_multiplier=0,
                                      allow_small_or_imprecise_dtypes=True), reads=[rg("ident"), rg("tri")], writes=[scr])
        p.op("pool", lambda e: e.memset(pos[:, NP:NT], 16384.0), writes=[scr])
        for which, dst in ((0, S4), (1, C4)):
            if which == 0:
                p.op("dve", TS(ang, pos, cols[:, 1:2], None, ALU.mult), reads=[rg("cols")], writes=[scr])
            else:
                p.op("dve", TS(ang, pos, cols[:, 1:2], math.pi / 2.0, ALU.mult, ALU.add), reads=[rg("cols")], writes=[scr])
            p.op("dve", TS(kf, ang, 1.0 / TWO_PI, None, ALU.mult), writes=[scr])
            p.op("dve", CP(ki, kf), writes=[scr])
            p.op("dve", CP(kf, ki), writes=[scr])
            p.op("dve", STT(ang, kf, -TWO_PI, ang, ALU.mult, ALU.add), writes=[scr])
            p.op("dve", TS(ang, ang, 3.14159, -3.14159, ALU.min, ALU.max), writes=[scr])
            p.op("act", A_(ang, ang, AF.Sin), writes=[scr])
            if which == 0:
                p.op("dve", TS(dst[:, :], ang, cols[:, 2:3], None, ALU.mult), reads=[rg("cols")], writes=[scr, rg("tabs")])
            else:
                p.op("dve", CP(dst[:, :], ang), writes=[scr, rg("tabs")])

        def rstd_from(srcs, src_regs, n, D, sq_eng="act"):
            pss, pssr = nps()
            nc_ = len(srcs)
            for c, (s, sr) in enumerate(zip(srcs, src_regs)):
                b = c % 2
                if sq_eng == "act":
                    p.op("act", A_(sqb[b][:, 0:n], s, AF.Square), reads=[sr], writes=[rg("sqb", b)])
                else:
                    p.op(sq_eng, TTo(sqb[b][:, 0:n], s, s, ALU.mult), reads=[sr], writes=[rg("sqb", b)])
                p.op("pe", MM(pss[:, 0:n], ones_b[:, :], sqb[b][:, 0:n], c == 0, c == nc_ - 1),
                     reads=[rg("sqb", b), rg("ones")], writes=[pssr])
            k = rstd_from.ctr % 2
            rstd_from.ctr += 1
            t = tA[k]
            tr_ = rg("tA", k)
            p.op("dve", TS(t[:, 0:n], pss[:, 0:n], 1.0 / D, EPS, ALU.mult, ALU.add), reads=[pssr], writes=[tr_])
            p.op("act", A_(t[:, 0:n], t[:, 0:n], AF.Sqrt), reads=[tr_], writes=[tr_])
            p.op("dve", lambda e, t=t: e.reciprocal(out=t[:, 0:n], in_=t[:, 0:n]), reads=[tr_], writes=[tr_])
            return t, tr_
        rstd_from.ctr = 0

        def HR(ti):
            return [rg("hT", ti)] + [rg("hTc", c, ti) for c in range(8)]

        def norm_tile(gname, ti):
            t0, n = TT[ti]
            srcs = [xT[:, c, t0:t0 + n] for c in range(8)]
            t, tr_ = rstd_from(srcs, [rg("xT", c, ti) for c in range(8)], n, 1024.0, sq_eng="pool")
            for c in range(8):
                eng = "dve"
                if c == 0:
                    rd_, wr_ = [], [rg("hT", ti), rg("hTc", 0, ti)]
                else:
                    rd_, wr_ = [rg("hT", ti)], [rg("hTc", c, ti)]
                p.op(eng, STT(hT[:, c, t0:t0 + n], xT[:, c, t0:t0 + n], V(gname, c), t[:, 0:n], ALU.mult, ALU.mult),
                     reads=[rg("xT", c, ti), tr_, rg("vecs")] + rd_, writes=wr_)

        def first_pass(gname, per_tile):
            norm_tile(gname, 0)
            for ti in range(5):
                if ti + 1 < 5:
                    norm_tile(gname, ti + 1)
                per_tile(ti)

        def ffn(l):
            def gu(Wv, wr, nn, ti):
                t0, n = TT[ti]
                pg, pgr = nps()
                pu, pur = nps()
                p.op("pe", [MM(pg[:, 0:n], Wv[:, 0, k, :], hT[:, k, t0:t0 + n], k == 0, k == 7) for k in range(8)],
                     reads=[wr] + HR(ti), writes=[pgr])
                p.op("pe", [MM(pu[:, 0:n], Wv[:, 1, k, :], hT[:, k, t0:t0 + n], k == 0, k == 7) for k in range(8)],
                     reads=[wr] + HR(ti), writes=[pur])
                b = ti % 2
                p.op("act", A_(tB[b][:, 0:n], pg[:, 0:n], AF.Silu), reads=[pgr], writes=[rg("tB", b)])
                p.op("dve", TTo(hid[:, nn, t0:t0 + n], pu[:, 0:n], tB[b][:, 0:n], ALU.mult),
                     reads=[pur, rg("tB", b)], writes=[rg("hid", nn, ti)])
            KF = 3
            Wf = []
            for nn in range(KF):
                W, wr = ring_load(w_gu[l, nn], 2048)
                Wf.append((W.rearrange("p (g k j) -> p g k j", g=2, k=8), wr))

            def pt(ti):
                for nn in range(KF):
                    gu(Wf[nn][0], Wf[nn][1], nn, ti)
            first_pass("ffn_g%d" % l, pt)
            for gi, (n0, cnt) in enumerate(HGS):
                for nn in range(cnt):
                    if gi == 0 and nn < KF:
                        continue
                    W, wr = ring_load(w_gu[l, n0 + nn], 2048)
                    Wv = W.rearrange("p (g k j) -> p g k j", g=2, k=8)
                    for ti in range(5):
                        gu(Wv, wr, nn, ti)
                for m in range(8):
                    W, wr = ring_load(w_dn[l, m][:, n0 * 128:(n0 + cnt) * 128], cnt * 128)
                    Wv = W[:, 0:cnt * 128].rearrange("p (n j) -> p n j", n=cnt)
                    for ti, (t0, n) in enumerate(TT):
                        po, por = nps()
                        p.op("pe", [MM(po[:, 0:n], Wv[:, nn, :], hid[:, nn, t0:t0 + n], nn == 0, nn == cnt - 1) for nn in range(cnt)],
                             reads=[wr] + [rg("hid", nn, ti) for nn in range(cnt)], writes=[por])
                        p.op("dve", TTo(xT[:, m, t0:t0 + n], po[:, 0:n], xT[:, m, t0:t0 + n], ALU.add),
                             reads=[por], writes=[rg("xT", m, ti)])

        for bi in range(17):
            r0 = bi * 128
            nb = min(128, NT - r0)
            ti = r0 // 512
            b = bi % 2
            p.dma("sp", io[b][0:nb, :], xin[r0:r0 + nb, :], d_io[b], writes=[rg("io", b)])
            for half in range(2):
                pt_, ptr_ = nps()
                p.op("pe", [TR(pt_[:, j * 128:j * 128 + nb], io[b][0:nb, (half * 4 + j) * 128:(half * 4 + j + 1) * 128], ident_f[0:nb, 0:nb])
                            for j in range(4)], reads=[rg("io", b), rg("ident")], writes=[ptr_])
                eng = "act" if half == 0 else "dve"
                src = pt_[:, :].rearrange("p (j t) -> p j t", j=4)[:, :, 0:nb]
                dstv = xT[:, half * 4:half * 4 + 4, r0:r0 + nb]
                if eng == "act":
                    p.op("act", A_(dstv, src, AF.Copy), reads=[ptr_], writes=[rg("xT", half * 4 + j, ti) for j in range(4)])
                else:
                    p.op("dve", CP(dstv, src), reads=[ptr_], writes=[rg("xT", half * 4 + j, ti) for j in range(4)])

        def a_layer(l):
            p.op("pool", lambda e: e.memset(gpad[:, :, 0:30], 0.0),
                 writes=[rg("gpad", c) for c in range(8)] + [scr] + [rg("hid", nn, ti) for nn in range(6) for ti in range(5)])

            def pw1(Wv, wr, c, ti):
                t0, n = TT[ti]
                pa, par = nps()
                pb, pbr = nps()
                p.op("pe", [MM(pa[:, 0:n], Wv[:, 0, k, :], hT[:, k, t0:t0 + n], k == 0, k == 7) for k in range(8)],
                     reads=[wr] + HR(ti), writes=[par])
                p.op("pe", [MM(pb[:, 0:n], Wv[:, 1, k, :], hT[:, k, t0:t0 + n], k == 0, k == 7) for k in range(8)],
                     reads=[wr] + HR(ti), writes=[pbr])
                b = ti % 2
                p.op("act", A_(tB[b][:, 0:n], pb[:, 0:n], AF.Sigmoid, bias=V("a_pw1_b%d" % l, 8 + c)),
                     reads=[pbr, rg("vecs")], writes=[rg("tB", b)])
                p.op("dve", STT(gpad[:, c, 30 + t0:30 + t0 + n], pa[:, 0:n], V("a_pw1_b%d" % l, c), tB[b][:, 0:n], ALU.add, ALU.mult),
                     reads=[par, rg("tB", b)], writes=[rg("gpad", c)])
                if t0 + n > 2034:
                    s0 = max(t0, 2034)
                    p.op("dve", STT(g_tail[:, c, s0 - 2034:t0 + n - 2034], pa[:, s0 - t0:n], V("a_pw1_b%d" % l, c),
                                    tB[b][:, s0 - t0:n], ALU.add, ALU.mult),
                         reads=[par, rg("tB", b)], writes=[rg("g_tail")])
            KF = 3
            Wf = []
            for c in range(KF):
                W, wr = ring_load(w_pw1[l, c], 2048)
                Wf.append((W.rearrange("p (g k j) -> p g k j", g=2, k=8), wr))

            def pt(ti):
                for c in range(KF):
                    pw1(Wf[c][0], Wf[c][1], c, ti)
            first_pass("a_norm_g%d" % l, pt)
            for c in range(KF, 8):
                W, wr = ring_load(w_pw1[l, c], 2048)
                Wv = W.rearrange("p (g k j) -> p g k j", g=2, k=8)
                for ti in range(5):
                    pw1(Wv, wr, c, ti)
            for b4 in range(4):
                b = b4 % 2
                p.dma("sp", io[b][0:30, :], state[l, :, b4, :], d_io[b], writes=[rg("io", b)])
                p.dma("sp", conv_s[l, b4, 0:29, :], io[b][1:30, :], d_o[b], reads=[rg("io", b)])
                pt_, ptr_ = nps()
                p.op("pe", [TR(pt_[:, c * 30:c * 30 + 30], io[b][0:30, c * 128:(c + 1) * 128], ident_f[0:30, 0:30]) for c in range(8)],
                     reads=[rg("io", b), rg("ident")], writes=[ptr_])
                p.op("act", A_(bufT[:, :, b4, 0:30], pt_[:, 0:240].rearrange("p (c t) -> p c t", c=8), AF.Copy),
                     reads=[ptr_], writes=[rg("bufT")])
            p.op("dve", CP(bufT[:, :, :, 30], g_tail[:, :, 30:34]), reads=[rg("g_tail")], writes=[rg("bufT")])
            b = 0
            for half in range(2):
                pt_, ptr_ = nps()
                p.op("pe", [TR(pt_[0:34, j * 128:(j + 1) * 128], g_tail[:, half * 4 + j, :], ident_f[:, :]) for j in range(4)],
                     reads=[rg("g_tail"), rg("ident")], writes=[ptr_])
                p.op("act", A_(io[b][0:34, half * 512:(half + 1) * 512], pt_[0:34, :], AF.Copy), reads=[ptr_], writes=[rg("io", b)])
            p.dma("sp", conv_p[l, :, :], io[b][0:30, :], d_o[b], reads=[rg("io", b)])
            for b4 in range(4):
                p.dma("sp", conv_s[l, b4, 29:30, :], io[b][30 + b4:31 + b4, :], d_o[b], reads=[rg("io", b)])
            tmp = tB[0][:, 0:8 * 31 * 2].rearrange("p (c b k) -> p c b k", c=8, b=2)
            ysr = rg("ys")
            for hb in range(2):
                p.op("dve", TTo(tmp, bufT[:, :, hb * 2:hb * 2 + 2, :],
                                dww[:, l, :, :].unsqueeze(2).broadcast_to([128, 8, 2, 31]), ALU.mult),
                     reads=[rg("bufT"), rg("dww")], writes=[rg("tB", 0)])
                p.op("dve", lambda e, hb=hb: e.tensor_reduce(out=tA[0][:, hb * 16:hb * 16 + 16].rearrange("p (c b) -> p c b", c=8),
                                                              in_=tmp, axis=AX.X, op=ALU.add),
                     reads=[rg("tB", 0)], writes=[rg("tA", 0), ysr])
            for c in range(8):
                db = c % 2
                dg = diagB[db]
                p.op("pool", TTo(dg[:, :, :], ident_b[:, :].unsqueeze(1).broadcast_to([128, 31, 128]),
                                 dww[:, l, c, :].unsqueeze(2).broadcast_to([128, 31, 128]), ALU.mult),
                     reads=[rg("identb"), rg("dww")], writes=[rg("diag", db)] + ([rg("auxB"), rg("clf", 0), rg("clf", 1), rg("krf", 0), rg("krf", 1), rg("Vh", 1)] if db == 1 else [rg("auxA"), rg("Vh", 0)] + [rg("Rq", ti) for ti in range(5)]))
                for ti, (t0, n) in enumerate(TT):
                    pc, pcr = nps()
                    p.op("pe", [MM(pc[:, 0:n], dg[:, k, :], gpad[:, c, t0 + k:t0 + k + n], k == 0, k == 30) for k in range(31)],
                         reads=[rg("diag", db), rg("gpad", c)], writes=[pcr])
                    p.op("act", A_(hT[:, c, t0:t0 + n], pc[:, 0:n], AF.Identity, bias=V("a_dw_b%d" % l, c)),
                         reads=[pcr, rg("vecs")], writes=[rg("yc", c, ti)] + ([rg("hT", ti)] if c == 0 else []))
                for hb in range(2):
                    p.op("dve", TS(hT[:, c, NP + hb * 2:NP + 2 + hb * 2], tA[0][:, hb * 16 + c * 2:hb * 16 + c * 2 + 2],
                                   V("a_dw_b%d" % l, c), None, ALU.add),
                         reads=[ysr, rg("tA", 0), rg("vecs")], writes=[rg("yc", c, 4)])
            def ln_tile(ti):
                t0, n = TT[ti]
                ps1, ps1r = aps()
                ps2, ps2r = aps()
                for c in range(8):
                    b = c % 2
                    p.op("pool", TTo(sqb[b][:, 0:n], hT[:, c, t0:t0 + n], hT[:, c, t0:t0 + n], ALU.mult), reads=[rg("yc", c, ti)], writes=[rg("sqb", b)])
                    p.op("pe", [MM(ps2[:, 0:n], ones_b[:, :], sqb[b][:, 0:n], c == 0, c == 7),
                                MM(ps1[:, 0:n], ones_b[:, :], hT[:, c, t0:t0 + n], c == 0, c == 7)],
                         reads=[rg("sqb", b), rg("ones"), rg("yc", c, ti)], writes=[ps1r, ps2r])
                mean = tA[1][:, 0:n]
                var = tA[0][:, 0:n]
                mr = rg("tA", 1)
                vr = rg("tA", 0)
                p.op("dve", TS(mean, ps1[:, 0:n], 1.0 / 1024.0, None, ALU.mult), reads=[ps1r], writes=[mr])
                p.op("dve", TTo(var, mean, mean, ALU.mult), reads=[mr, ysr], writes=[vr])
                p.op("dve", STT(var, ps2[:, 0:n], 1.0 / 1024.0, var, ALU.mult, ALU.subtract), reads=[ps2r, vr], writes=[vr])
                p.op("act", A_(var, var, AF.Sqrt, bias=cols[:, 3:4]), reads=[vr, rg("cols")], writes=[vr])
                p.op("dve", lambda e, var=var: e.reciprocal(out=var, in_=var), reads=[vr], writes=[vr])
                for c in range(8):
                    b = c % 2
                    eng = "dve"
                    tz = tB[b][:, 0:n]
                    p.op(eng, TTo(tz, hT[:, c, t0:t0 + n], mean, ALU.subtract), reads=[rg("yc", c, ti), mr], writes=[rg("tB", b)])
                    p.op(eng, TTo(tz, tz, var, ALU.mult), reads=[vr, rg("tB", b)], writes=[rg("tB", b)])
                    p.op("act", A_(hT[:, c, t0:t0 + n], tz, AF.Silu, bias=V("a_ln_b%d" % l, c), scale=V("a_ln_g%d" % l, c)),
                         reads=[rg("tB", b), rg("vecs")], writes=[rg("yc", c, ti)])

            def pw2(Wv, wr, m, ti):
                t0, n = TT[ti]
                po, por = nps()
                p.op("pe", [MM(po[:, 0:n], Wv[:, k, :], hT[:, k, t0:t0 + n], k == 0, k == 7) for k in range(8)],
                     reads=[wr, rg("hT", ti)] + [rg("yc", c, ti) for c in range(8)], writes=[por])
                p.op("dve", STT(xT[:, m, t0:t0 + n], po[:, 0:n], V("a_pw2_b%d" % l, m), xT[:, m, t0:t0 + n], ALU.add, ALU.add),
                     reads=[por, rg("vecs")], writes=[rg("xT", m, ti)])
            KF2 = 4
            Wf2 = []
            for m in range(KF2):
                W, wr = ring_load(w_pw2[l, m], 1024)
                Wf2.append((W[:, 0:1024].rearrange("p (k j) -> p k j", k=8), wr))
            ln_tile(0)
            for ti in range(5):
                if ti + 1 < 5:
                    ln_tile(ti + 1)
                for m in range(KF2):
                    pw2(Wf2[m][0], Wf2[m][1], m, ti)
            for m in range(KF2, 8):
                W, wr = ring_load(w_pw2[l, m], 1024)
                Wv = W[:, 0:1024].rearrange("p (k j) -> p k j", k=8)
                for ti in range(5):
                    pw2(Wv, wr, m, ti)
            ffn(l)

        for l in range(2):
            a_layer(l)

        Wl, wlr = ring_load(w_dkvl, 2048)
        Wlv = Wl.rearrange("p (k j) -> p k j", k=8)
        Wr_, wrr = ring_load(w_dkvr, 512)
        Wrv = Wr_[:, 0:512].rearrange("p (k j) -> p k j", k=8)
        auxAf = auxA[:, :].bitcast(F32)
        clfs = [auxB[:, 0:1024].rearrange("p (c t) -> p c t", c=2), auxAf[:, 0:1024].rearrange("p (c t) -> p c t", c=2)]
        krfs = [auxB[0:32, 1024:1536], auxAf[0:32, 1024:1536]]
        def kv_tile(ti):
            t0, n = TT[ti]
            clf = clfs[ti % 2]
            krf = krfs[ti % 2]
            sx = ti % 2
            pl = [nps() for _ in range(2)]
            for kc in range(2):
                p.op("pe", [MM(pl[kc][0][:, 0:n], Wlv[:, k, kc * 128:(kc + 1) * 128], hT[:, k, t0:t0 + n], k == 0, k == 7) for k in range(8)],
                     reads=[wlr] + HR(ti), writes=[pl[kc][1]])
            pr1, pr1r = nps()
            pr2, pr2r = nps()
            p.op("pe", [MM(pr1[0:32, 0:n], Wrv[:, k, 0:32], hT[:, k, t0:t0 + n], k == 0, k == 7) for k in range(8)],
                 reads=[wrr] + HR(ti), writes=[pr1r])
            p.op("pe", [MM(pr2[0:32, 0:n], Wrv[:, k, 32:64], hT[:, k, t0:t0 + n], k == 0, k == 7) for k in range(8)],
                 reads=[wrr] + HR(ti), writes=[pr2r])
            t, tr_ = rstd_from([pl[0][0][:, 0:n], pl[1][0][:, 0:n]], [pl[0][1], pl[1][1]], n, 256.0)
            for kc in range(2):
                p.op("dve", STT(clf[:, kc, 0:n], pl[kc][0][:, 0:n], V("kvl_g", kc), t[:, 0:n], ALU.mult, ALU.mult),
                     reads=[pl[kc][1], tr_, rg("vecs")], writes=[rg("clf", sx)])
            p.op("act", A_(cT[:, :, t0:t0 + n], clf[:, :, 0:n], AF.Copy), reads=[rg("clf", sx)], writes=[rg("cT", ti)])
            t1 = tB[0][0:32, 0:n]
            t2 = tB[1][0:32, 0:n]
            p.op("dve", TTo(t1, pr1[0:32, 0:n], C4[0:32, t0:t0 + n], ALU.mult), reads=[pr1r, rg("tabs")], writes=[rg("tB", 0)])
            p.op("dve", TTo(t2, pr2[0:32, 0:n], S4[0:32, t0:t0 + n], ALU.mult), reads=[pr2r, rg("tabs")], writes=[rg("tB", 1)])
            p.op("dve", TTo(krf[:, 0:n], t1, t2, ALU.add), reads=[rg("tB", 0), rg("tB", 1)], writes=[rg("krf", sx)])
            p.op("act", A_(krT[:, t0:t0 + n], krf[:, 0:n], AF.Copy), reads=[rg("krf", sx)], writes=[rg("krT", ti)])
        def kv_out(ti):
            t0, n = TT[ti]
            clf = clfs[ti % 2]
            krf = krfs[ti % 2]
            sx = ti % 2
            for bi in range((n + 127) // 128):
                c0 = bi * 128
                nb = min(128, n - c0)
                b = bi % 2
                pt_, ptr_ = nps()
                p.op("pe", [TR(pt_[0:nb, 0:128], clf[:, 0, c0:c0 + nb], ident_f[:, :]),
                            TR(pt_[0:nb, 128:256], clf[:, 1, c0:c0 + nb], ident_f[:, :]),
                            TR(pt_[0:nb, 256:288], krf[:, c0:c0 + nb], ident_f[0:32, 0:32])],
                     reads=[rg("clf", sx), rg("krf", sx), rg("ident")], writes=[ptr_])
                p.op("act", A_(io[b][0:nb, 0:288], pt_[0:nb, 0:288], AF.Copy), reads=[ptr_], writes=[rg("io", b)])
                p.dma("sp", lat_all[t0 + c0:t0 + c0 + nb, :], io[b][0:nb, 0:256], d_o[b], reads=[rg("io", b)])
                p.dma("sp", kr_all[t0 + c0:t0 + c0 + nb, :], io[b][0:nb, 256:288], d_o[b], reads=[rg("io", b)])

        def kv_pt(ti):
            kv_tile(ti)
            if ti > 0:
                kv_out(ti - 1)
        first_pass("kv_g", kv_pt)
        kv_out(4)

        Vh = [auxA[:, 2080:2080 + 2176].rearrange("p (k j) -> p k j", k=17),
              auxB[:, :].bitcast(BF16)[:, 0:2176].rearrange("p (k j) -> p k j", k=17)]
        OT = hT
        QhB = [R2[:, 3 * NT:4 * NT], R2[:, 5 * NT:6 * NT]]
        KhB = [R2[:, 4 * NT:5 * NT], iob[:, 0:NT]]

        def b_layer(j):
            l = 2 + j
            Wd = [ring_load(w_dq[j, n3], 1024) for n3 in range(3)]

            def dq_tile(ti):
                t0, n = TT[ti]
                pq = [nps() for _ in range(3)]
                for n3 in range(3):
                    Wv = Wd[n3][0][:, 0:1024].rearrange("p (k j) -> p k j", k=8)
                    p.op("pe", [MM(pq[n3][0][:, 0:n], Wv[:, k, :], hT[:, k, t0:t0 + n], k == 0, k == 7) for k in range(8)],
                         reads=[Wd[n3][1]] + HR(ti), writes=[pq[n3][1]])
                t, tr_ = rstd_from([pq[i][0][:, 0:n] for i in range(3)], [pq[i][1] for i in range(3)], n, 384.0)
                for n3 in range(3):
                    p.op("dve", STT(cqn[:, n3, t0:t0 + n], pq[n3][0][:, 0:n], V("bq_g%d" % j, n3), t[:, 0:n], ALU.mult, ALU.mult),
                         reads=[pq[n3][1], tr_, rg("vecs")], writes=[rg("cqn", ti)])
            first_pass("b_g%d" % j, dq_tile)
            p.op("pool", lambda e: e.memset(Vh[0][:, :, 64:128], 1.0), writes=[rg("Vh", 0), rg("auxA"), rg("auxB"), rg("diag", 0), rg("diag", 1), rg("clf", 0), rg("clf", 1), rg("krf", 0), rg("krf", 1)])
            p.op("pool", lambda e: e.memset(Vh[1][:, :, 0:64], 1.0), writes=[rg("Vh", 1), rg("auxA"), rg("auxB"), rg("diag", 0), rg("diag", 1), rg("clf", 0), rg("clf", 1), rg("krf", 0), rg("krf", 1)] + [rg("ybuf", c) for c in range(8)])
            def rope(q4):
                W, wr = ring_load(w_rs[j, q4], 768)
                Wv = W[:, 0:768].rearrange("p (k s j) -> p k s j", k=3, s=2)
                for ti, (t0, n) in enumerate(TT):
                    pR, pRr = nps()
                    pS, pSr = nps()
                    p.op("pe", [MM(pR[:, 0:n], Wv[:, k, 0, :], cqn[:, k, t0:t0 + n], k == 0, k == 2) for k in range(3)],
                         reads=[wr, rg("cqn", ti)], writes=[pRr])
                    p.op("pe", [MM(pS[:, 0:n], Wv[:, k, 1, :], cqn[:, k, t0:t0 + n], k == 0, k == 2) for k in range(3)],
                         reads=[wr, rg("cqn", ti)], writes=[pSr])
                    p.op("dve", TTo(tB[0][:, 0:n], pR[:, 0:n], C4[:, t0:t0 + n], ALU.mult), reads=[pRr, rg("tabs")], writes=[rg("tB", 0)])
                    p.op("dve", TTo(tB[1][:, 0:n], pS[:, 0:n], S4[:, t0:t0 + n], ALU.mult), reads=[pSr, rg("tabs")], writes=[rg("tB", 1)])
                    p.op("pool", TTo(Rq[:, t0:t0 + n], tB[0][:, 0:n], tB[1][:, 0:n], ALU.add),
                         reads=[rg("tB", 0), rg("tB", 1)], writes=[rg("Rq", ti)])

            def gen(h):
                par = h % 2
                e_ = par
                vo = 64 * e_
                Qh = QhB[par]
                Kh = KhB[par]
                kx = [rg("io", 0), rg("io", 1)] if par == 1 else []
                W, wr = ring_load(w_hd[j, h], 608)
                uqn = W[:, 0:288].rearrange("p (k j) -> p k j", k=3)
                ukp = W[:, 288:480].rearrange("p (k j) -> p k j", k=2)
                uvh = W[:, 480:608].rearrange("p (k j) -> p k j", k=2)
                for ti, (t0, n) in enumerate(TT):
                    pq_, pqr = nps()
                    p.op("pe", [MM(pq_[0:96, 0:n], uqn[:, k, :], cqn[:, k, t0:t0 + n], k == 0, False) for k in range(3)]
                         + [MM(pq_[0:96, 0:n], selQ[:, h % 4, :], Rq[:, t0:t0 + n], False, True)],
                         reads=[wr, rg("cqn", ti), rg("Rq", ti), rg("selQ")], writes=[pqr])
                    p.op("act", A_(Qh[0:96, t0:t0 + n], pq_[0:96, 0:n], AF.Copy), reads=[pqr], writes=[rg("Qh", par, ti)])
                    if ti == 4:
                        p.op("act", A_(Qs[:, h, :], pq_[0:96, 16:20], AF.Copy), reads=[pqr], writes=[rg("Qs")])
                    nk_ = min(n, NP - t0)
                    pk_, pkr = nps()
                    p.op("pe", [MM(pk_[0:96, 0:nk_], ukp[:, k, :], cT[:, k, t0:t0 + nk_], k == 0, False) for k in range(2)]
                         + [MM(pk_[0:96, 0:nk_], selK[:, :], krT[:, t0:t0 + nk_], False, True)],
                         reads=[wr, rg("cT", ti), rg("krT", ti), rg("selK")], writes=[pkr])
                    p.op("dve", CP(Kh[0:96, t0:t0 + nk_], pk_[0:96, 0:nk_]), reads=[pkr], writes=[rg("Kh", par, ti)] + kx)
                for g0 in (0, 8, 16):
                    ng = min(8, 17 - g0)
                    pv, pvr = nps()
                    fl = []
                    for i in range(ng):
                        kb = g0 + i
                        k0 = kb * 128
                        nk = min(128, NP - k0)
                        for k in range(2):
                            fl.append(MM(pv[0:nk, i * 64:(i + 1) * 64], cT[:, k, k0:k0 + nk], uvh[:, k, :], k == 0, k == 1))
                    p.op("pe", fl, reads=[wr] + [rg("cT", ti) for ti in range(5)], writes=[pvr])
                    nkk = 128 if g0 < 16 else 16
                    p.op("dve", CP(Vh[e_][0:nkk, g0:g0 + ng, vo:vo + 64], pv[0:nkk, 0:ng * 64].rearrange("p (i d) -> p i d", i=ng)),
                         reads=[pvr], writes=[rg("Vh", e_)])
                if h % 4 == 3 and h < 15:
                    rope(h // 4 + 1)

            def attn(h):
                par = h % 2
                e_ = par
                pair = h // 2
                vo = 64 * e_
                do = 64 - vo
                Qh = QhB[par]
                Kh = KhB[par]
                kx = [rg("io", 0), rg("io", 1)] if par == 1 else []
                steps = []
                for ti, (q0, n) in enumerate(TT):
                    nq = min(n, NP - q0)
                    nkb = (q0 + nq - 1) // 128 + 1
                    pO, pOr = aps()
                    for kb in range(nkb):
                        steps.append((ti, q0, nq, nkb, kb, pO, pOr))
                LOOK = 2
                pis = {}

                def emit_S(i):
                    ti, q0, nq, nkb, kb, pO, pOr = steps[i]
                    k0 = kb * 128
                    nk = min(128, NP - k0)
                    qs = max(q0, k0)
                    w = q0 + nq - qs
                    pS_, pSr_ = nps()
                    if k0 >= q0:
                        dw = min(128, w)
                        p.op("pe", [MM(pS_[0:nk, 0:w], Kh[0:96, k0:k0 + nk], Qh[0:96, qs:qs + w], True, False),
                                    MM(pS_[0:nk, 0:dw], ident_b[0:nk, 0:nk], negm[0:nk, 0:dw], False, True)],
                             reads=[rg("Kh", par, k0 // 512), rg("Qh", par, ti), rg("negm"), rg("identb")] + kx, writes=[pSr_])
                    else:
                        p.op("pe", MM(pS_[0:nk, 0:w], Kh[0:96, k0:k0 + nk], Qh[0:96, qs:qs + w], True, True),
                             reads=[rg("Kh", par, k0 // 512), rg("Qh", par, ti)] + kx, writes=[pSr_])
                    pi = att_ctr[0] % 3
                    att_ctr[0] += 1
                    pis[i] = pi
                    p.op("act", A_(PT[pi][0:nk, 0:w], pS_[0:nk, 0:w], AF.Exp, scale=SCALE), reads=[pSr_], writes=[rg("PT", pi)])

                def emit_PV(i):
                    ti, q0, nq, nkb, kb, pO, pOr = steps[i]
                    k0 = kb * 128
                    nk = min(128, NP - k0)
                    qs = max(q0, k0)
                    w = q0 + nq - qs
                    pi = pis[i]
                    p.op("pe", MM(pO[:, qs - q0:qs - q0 + w], Vh[e_][0:nk, kb, :], PT[pi][0:nk, 0:w], kb == 0, kb == nkb - 1),
                         reads=[rg("PT", pi), rg("Vh", e_)], writes=[pOr])
                    if kb == nkb - 1:
                        b = ti % 2
                        p.op("dve", lambda e, b=b, pO=pO, nq=nq, do=do: e.reciprocal(out=tA[b][do:do + 64, 0:nq], in_=pO[do:do + 64, 0:nq]),
                             reads=[pOr], writes=[rg("tA", b)])
                        p.op("dve", TTo(OT[vo:vo + 64, pair, q0:q0 + nq], pO[vo:vo + 64, 0:nq], tA[b][do:do + 64, 0:nq], ALU.mult),
                             reads=[pOr, rg("tA", b)], writes=[rg("hT", ti)])

                for i in range(len(steps)):
                    emit_S(i)
                    if i >= LOOK:
                        emit_PV(i - LOOK)
                for i in range(max(0, len(steps) - LOOK), len(steps)):
                    emit_PV(i)

            rope(0)
            gen(0)
            for h in range(16):
                if h + 1 < 16:
                    gen(h + 1)
                attn(h)
            if debug and j == 0:
                allr = list(p.regs.values())
                dd = lambda name, shape, dt=BF16: nc.dram_tensor(name, list(shape), dt, kind="ExternalOutput").ap()
                p.dma("sp", dd("dbg_OT", [128, 8 * NT]), R1[:, :], d_out, reads=allr)
                p.dma("sp", dd("dbg_R2", [128, 8 * 2098]), R2[:, :], d_out, reads=allr)
                p.dma("sp", dd("dbg_auxA", [128, 4352]), auxA[:, :], d_out, reads=allr)
                p.dma("sp", dd("dbg_auxB", [128, 2048], F32), auxB[:, :], d_out, reads=allr)
                p.dma("sp", dd("dbg_tri", [128, 128]), tri[:, :], d_out, reads=allr)
                p.dma("sp", dd("dbg_PT", [128, 512]), PT[0][:, :], d_out, reads=allr)
                p.dma("sp", dd("dbg_tA", [128, 512], F32), tA[0][:, :], d_out, reads=allr)
                p.dma("sp", dd("dbg_xT", [128, 8 * NT], F32), xT[:, :, :].rearrange("p c t -> p (c t)"), d_out, reads=allr)
                p.dma("sp", dd("dbg_C4", [128, NT]), C4[:, :], d_out, reads=allr)
                p.dma("sp", dd("dbg_S4", [128, NT]), S4[:, :], d_out, reads=allr)
            if stage >= 4:
                sample_attention(j)
            else:
                p.op("pool", lambda e: e.memset(OT[:, :, NP:NT], 0.0), writes=[rg("hT", 4)])
            for m in range(8):
                W, wr = ring_load(w_o[j, m], 1024)
                Wv = W[:, 0:1024].rearrange("p (k j) -> p k j", k=8)
                for ti, (t0, n) in enumerate(TT):
                    po, por = nps()
                    p.op("pe", [MM(po[:, 0:n], Wv[:, k, :], OT[:, k, t0:t0 + n], k == 0, k == 7) for k in range(8)],
                         reads=[wr, rg("hT", ti)], writes=[por])
                    p.op("dve", TTo(xT[:, m, t0:t0 + n], po[:, 0:n], xT[:, m, t0:t0 + n], ALU.add),
                         reads=[por], writes=[rg("xT", m, ti)])
            ffn(l)

        att_ctr = [0]

        ukT = auxA[0:64, 0:4096].rearrange("p (h c) -> p h c", h=16)
        uvp = auxB[:, :].bitcast(BF16).rearrange("p (k h j) -> p k h j", k=2, h=16)
        SA = R2[:, 0:6 * NT]
        CLb2 = [SA[:, i * 2048:(i + 1) * 2048] for i in range(4)]
        CLb = [c_.rearrange("p (r c) -> p r c", r=8) for c_ in CLb2]
        KRg = [iob[:, 3072 + i * 256:3072 + (i + 1) * 256] for i in range(4)]
        tcT = [SA[:, 8192 + i * 1536: 8192 + i * 1536 + 1024].rearrange("p (k t) -> p k t", k=2) for i in range(2)]
        tkr = [SA[:, 8192 + i * 1536 + 1024: 8192 + i * 1536 + 1152] for i in range(2)]
        QRz = SA[:, 8192 + 1152:8192 + 1152 + 256].rearrange("p (r b h) -> p r b h", r=4, b=4)
        o0 = 8192 + 3072
        QL = SA[:, o0:o0 + 128].rearrange("p (k b h) -> p k b h", k=2, b=4)
        QR = SA[0:32, o0 + 128:o0 + 192].rearrange("p (b h) -> p b h", b=4)
        OLT = SA[:, o0 + 192:o0 + 320].rearrange("p (k b h) -> p k b h", k=2, b=4)
        PTs = [SA[:, o0 + 320 + i * 64:o0 + 384 + i * 64] for i in range(2)]
        PTn = SA[0:1, o0 + 448:o0 + 464]
        crow = SA[0:1, o0 + 464:o0 + 720]
        OL = SA[0:16, o0 + 720:o0 + 976]
        rd = cols[0:16, 5:6]
        d_gl = [p.new_sem("d_gl%d" % i) for i in range(4)]
        d_gk = [p.new_sem("d_gk%d" % i) for i in range(4)]

        def sample_attention(j):
            sar = rg("sa_small")
            p.dma("pool", auxA[0:64, 0:4096], w_ukT, d_misc, writes=[rg("auxA"), rg("diag", 0)] + [rg("Rq", ti) for ti in range(5)] + [rg("Vh", 0), rg("Vh", 1)])
            p.dma("pool", auxB[:, :].bitcast(BF16), w_uvp, d_misc2, writes=[rg("auxB"), rg("diag", 1), rg("clf", 0), rg("clf", 1), rg("krf", 0), rg("krf", 1)] + [rg("ybuf", c) for c in range(8)])
            sa_regs = ([rg("cqn", ti) for ti in range(5)] + [rg("Qh", pp, ti) for ti in range(5) for pp in range(2)]
                       + [rg("Kh", pp, ti) for ti in range(5) for pp in range(2)] + [rg("io", 1)])
            pq1, pq1r = nps()
            fl = []
            for h in range(16):
                for kc in range(2):
                    c0 = (kc * 16 + h) * 4
                    fl.append(MM(pq1[:, c0:c0 + 4], ukT[:, h, kc * 128:(kc + 1) * 128], Qs[0:64, h, :], True, True))
            p.op("pe", fl, reads=[rg("auxA"), rg("Qs")], writes=[pq1r])
            p.op("act", A_(QL.rearrange("p k b h -> p k h b"), pq1[:, 0:128].rearrange("p (k h b) -> p k h b", k=2, h=16), AF.Copy),
                 reads=[pq1r], writes=[sar] + sa_regs)
            pq2, pq2r = nps()
            p.op("pe", [MM(pq2[0:32, h * 4:h * 4 + 4], ident_b[0:96, 64:96], Qs[0:96, h, :], True, True) for h in range(16)],
                 reads=[rg("identb"), rg("Qs")], writes=[pq2r])
            p.op("pool", lambda e: e.memset(QRz[:, :, :, :], 0.0), reads=[pq1r], writes=[sar] + sa_regs)
            for rr in range(4):
                p.op("dve", CP(QRz[32 * rr:32 * rr + 32, rr, :, :].rearrange("p b h -> p h b"),
                               pq2[0:32, 0:64].rearrange("p (h b) -> p h b", h=16)), reads=[pq2r], writes=[sar])
            steps = [(b4, g, r4) for b4 in range(4) for g in range(16) for r4 in range(2)]
            acc = {}
            st_ = {}

            def finalize(b4):
                pO, pOr = acc[b4]
                pD = pO[0:16, 256:257]
                col = NP + b4
                pS_, pSr_ = nps()
                p.op("pe", [MM(pS_[0:1, 0:16], cT[:, 0, col:col + 1], QL[:, 0, b4, :], True, False),
                            MM(pS_[0:1, 0:16], cT[:, 1, col:col + 1], QL[:, 1, b4, :], False, False),
                            MM(pS_[0:1, 0:16], krT[:, col:col + 1], QRz[0:32, 0, b4, :], False, True)],
                     reads=[rg("cT", 4), rg("krT", 4), sar], writes=[pSr_])
                p.op("act", A_(PTn, pS_[0:1, 0:16], AF.Exp, scale=SCALE), reads=[pSr_], writes=[rg("PTn")])
                pC, pCr = nps()
                pCb = pC[:, :].bitcast(BF16)
                p.op("pe", [TR(pCb[0:1, kc * 128:(kc + 1) * 128], cT[:, kc, col:col + 1], ident_b[:, :]) for kc in range(2)],
                     reads=[rg("cT", 4), rg("identb")], writes=[pCr])
                p.op("dve", CP(crow, pCb[0:1, 0:256]), reads=[pCr], writes=[rg("crow")])
                p.op("pe", [MM(pO[0:16, 0:256], PTn, crow, False, True), MM(pD, PTn, ones_b[0:1, 0:1], False, True)],
                     reads=[rg("PTn"), rg("crow"), rg("ones")], writes=[pOr])
                p.op("dve", lambda e, pD=pD: e.reciprocal(out=rd, in_=pD), reads=[pOr], writes=[rg("rd")])
                p.op("dve", TS(OL, pO[0:16, 0:256], rd, None, ALU.mult), reads=[pOr, rg("rd")], writes=[rg("OL")])
                pT2, pT2r = nps()
                pT2b = pT2[:, :].bitcast(BF16)
                p.op("pe", [TR(pT2b[:, kc * 16:(kc + 1) * 16], OL[:, kc * 128:(kc + 1) * 128], ident_b[0:16, 0:16]) for kc in range(2)],
                     reads=[rg("OL"), rg("identb")], writes=[pT2r])
                p.op("act", A_(OLT[:, :, b4, :], pT2b[:, 0:32].rearrange("p (k h) -> p k h", k=2), AF.Copy), reads=[pT2r], writes=[rg("OLT")])

            def emit_T(i):
                b4, g, r4 = steps[i]
                gi = b4 * 16 + g
                cb = gi % 4
                if r4 == 0:
                    if g == 0:
                        acc[b4] = aps()
                    p.raw("pool", lambda e, b4=b4, g=g, cb=cb: e.indirect_dma_start(
                        out=CLb2[cb], out_offset=None, in_=cache_lat[:, :],
                        in_offset=bass.IndirectOffsetOnAxis(ap=idxg[:, b4, g:g + 1], axis=0)),
                        d_gl[cb], reads=[rg("idxg")], writes=[rg("CLb", cb)])
                    p.raw("pool", lambda e, b4=b4, g=g, cb=cb: e.indirect_dma_start(
                        out=KRg[cb], out_offset=None, in_=cache_kr[:, :],
                        in_offset=bass.IndirectOffsetOnAxis(ap=idxg[:, b4, g:g + 1], axis=0)),
                        d_gk[cb], reads=[rg("idxg")], writes=[rg("KRg", cb)])
                tb = i % 2
                pA, pAr = nps()
                pB, pBr = nps()
                pAb = pA[:, :].bitcast(BF16).rearrange("p (k r t) -> p k r t", k=2, r=4)
                pBb = pB[:, :].bitcast(BF16)
                fl = []
                for rr in range(4):
                    r = r4 * 4 + rr
                    for kc in range(2):
                        fl.append(TR(pAb[:, kc, rr, :], CLb[cb][:, r, kc * 128:(kc + 1) * 128], ident_b[:, :]))
                fl.append(TR(pBb[:, 0:128], KRg[cb][:, r4 * 128:(r4 + 1) * 128], ident_b[:, :]))
                p.op("pe", fl, reads=[rg("CLb", cb), rg("KRg", cb), rg("identb")], writes=[pAr, pBr])
                p.op("act", A_(tcT[tb], pAb.rearrange("p k r t -> p k (r t)"), AF.Copy), reads=[pAr], writes=[rg("tcT", tb)])
                p.op("dve", CP(tkr[tb], pBb[:, 0:128]), reads=[pBr], writes=[rg("tkr", tb)])

            def emit_S(i):
                b4, g, r4 = steps[i]
                tb = i % 2
                pS_, pSr_ = nps()
                fl = []
                for rr in range(4):
                    o_ = pS_[:, rr * 16:(rr + 1) * 16]
                    fl.append(MM(o_, tcT[tb][:, 0, rr * 128:(rr + 1) * 128], QL[:, 0, b4, :], True, False))
                    fl.append(MM(o_, tcT[tb][:, 1, rr * 128:(rr + 1) * 128], QL[:, 1, b4, :], False, False))
                    fl.append(MM(o_, tkr[tb][:, :], QRz[:, rr, b4, :], False, True))
                p.op("pe", fl, reads=[rg("tcT", tb), rg("tkr", tb), sar], writes=[pSr_])
                p.op("act", A_(PTs[tb], pS_[:, 0:64], AF.Exp, scale=SCALE), reads=[pSr_], writes=[rg("PTs", tb)])

            def emit_V(i):
                b4, g, r4 = steps[i]
                gi = b4 * 16 + g
                cb = gi % 4
                tb = i % 2
                pO, pOr = acc[b4]
                pD = pO[0:16, 256:257]
                fl = []
                for rr in range(4):
                    r = r4 * 4 + rr
                    first = (g == 0 and r4 == 0 and rr == 0)
                    fl.append(MM(pO[0:16, 0:256], PTs[tb][:, rr * 16:(rr + 1) * 16], CLb[cb][:, r, :], first, False))
                    fl.append(MM(pD, PTs[tb][:, rr * 16:(rr + 1) * 16], ones_b[:, 0:1], first, False))
                p.op("pe", fl, reads=[rg("PTs", tb), rg("CLb", cb), rg("ones")], writes=[pOr])
                if g == 15 and r4 == 1:
                    finalize(b4)

            n_ = len(steps)
            for i in range(n_ + 2):
                if i < n_:
                    emit_T(i)
                if 1 <= i <= n_:
                    emit_S(i - 1)
                if i >= 2:
                    emit_V(i - 2)
            for pair in range(8):
                po, por = nps()
                fl = []
                i = 0
                for e2 in range(2):
                    for kc in range(2):
                        fl.append(MM(po[:, 0:4], uvp[:, kc, 2 * pair + e2, :], OLT[:, kc, :, 2 * pair + e2], i == 0, i == 3))
                        i += 1
                p.op("pe", fl, reads=[rg("auxB"), rg("OLT")], writes=[por])
                p.op("act", A_(OT[:, pair, NP:NT], po[:, 0:4], AF.Copy), reads=[por], writes=[rg("hT", 4)])

        sa_ctr = [0]

        if stage >= 3:
            for j in range(2):
                b_layer(j)

        R1f = R1[:, :].bitcast(F32)
        yos = [R1f[:, i * 4096:(i + 1) * 4096].rearrange("p (c t) -> p c t", c=8) for i in range(2)]

        def fin_norm(ti):
            t0, n = TT[ti]
            yo = yos[ti % 2]
            srcs = [xT[:, c, t0:t0 + n] for c in range(8)]
            t, tr_ = rstd_from(srcs, [rg("xT", c, ti) for c in range(8)], n, 1024.0, sq_eng="pool")
            for c in range(8):
                eng = "dve"
                p.op(eng, STT(yo[:, c, 0:n], xT[:, c, t0:t0 + n], V("fin_g", c), t[:, 0:n], ALU.mult, ALU.mult),
                     reads=[rg("xT", c, ti), tr_, rg("vecs")] + ([] if c == 0 else [rg("yo", ti % 2)]),
                     writes=([rg("yo", ti % 2)] + [r_ for k in range(5) for r_ in HR(k)]) if c == 0 else [rg("yoc", ti % 2, c)])

        def fin_out(ti):
            t0, n = TT[ti]
            yo = yos[ti % 2]
            for bi in range((n + 127) // 128):
                c0 = bi * 128
                nb = min(128, n - c0)
                b = bi % 2
                for half in range(2):
                    pt_, ptr_ = nps()
                    p.op("pe", [TR(pt_[0:nb, jj * 128:(jj + 1) * 128], yo[:, half * 4 + jj, c0:c0 + nb], ident_f[:, :]) for jj in range(4)],
                         reads=[rg("yo", ti % 2), rg("ident")] + [rg("yoc", ti % 2, c) for c in range(1, 8)], writes=[ptr_])
                    if half == 0:
                        p.op("act", A_(io[b][0:nb, 0:512], pt_[0:nb, :], AF.Copy), reads=[ptr_], writes=[rg("io", b)])
                    else:
                        p.op("dve", CP(io[b][0:nb, 512:1024], pt_[0:nb, :]), reads=[ptr_], writes=[rg("io", b)])
                p.dma("sp", y_all[t0 + c0:t0 + c0 + nb, :], io[b][0:nb, :], d_o[b], reads=[rg("io", b)])
        fin_norm(0)
        for ti in range(5):
            if ti + 1 < 5:
                fin_norm(ti + 1)
            fin_out(ti)
        for dd_ in (d_o[0], d_o[1], d_out):
            if dd_.count > 0:
                p.q["sp"].append(lambda e, dd_=dd_: e.wait_ge(dd_.h, dd_.count))
        p.emit()
    return nc


def _cols(v):
    return np.ascontiguousarray(np.asarray(v, np.float32).reshape(-1, 128).T)


def _prep_shared(inp):
    f = lambda a: np.ascontiguousarray(np.asarray(a, dtype=np.float32))
    d = {}
    pw1 = f(inp["a_pw1_w"]).reshape(2, 8, 128, 2, 8, 128)
    d["w_pw1"] = f(pw1.transpose(0, 4, 2, 3, 1, 5).reshape(2, 8, 128, 2048))
    pw2 = f(inp["a_pw2_w"]).reshape(2, 8, 128, 8, 128)
    d["w_pw2"] = f(pw2.transpose(0, 3, 2, 1, 4).reshape(2, 8, 128, 1024))
    g = f(inp["ffn_w_gate"]).reshape(4, 8, 128, 22, 128)
    u = f(inp["ffn_w_up"]).reshape(4, 8, 128, 22, 128)
    gu = np.stack([g, u], axis=0)
    d["w_gu"] = f(gu.transpose(1, 4, 3, 0, 2, 5).reshape(4, 22, 128, 2048))
    dn = f(inp["ffn_w_down"]).reshape(4, 22, 128, 8, 128)
    d["w_dn"] = f(dn.transpose(0, 3, 2, 1, 4).reshape(4, 8, 128, 2816))
    dq = f(inp["b_w_dq"]).reshape(2, 8, 128, 3, 128)
    d["w_dq"] = f(dq.transpose(0, 3, 2, 1, 4).reshape(2, 3, 128, 1024))
    uq = f(inp["b_w_uq"]).reshape(2, 3, 128, 16, 96)
    rope = uq[..., 64:96]
    swp = np.concatenate([uq[..., 80:96], uq[..., 64:80]], axis=-1)
    rs = np.stack([rope, swp], axis=0).reshape(2, 2, 3, 128, 4, 4 * 32)
    d["w_rs"] = f(rs.transpose(1, 4, 3, 2, 0, 5).reshape(2, 4, 128, 768))
    nope = np.zeros((2, 3, 128, 16, 96), np.float32)
    nope[..., 0:64] = uq[..., 0:64]
    uk = f(inp["w_uk"]).reshape(2, 128, 16, 64)
    ukp = np.zeros((2, 128, 16, 96), np.float32)
    ukp[..., 0:64] = uk
    uv = f(inp["w_uv"]).reshape(2, 128, 16, 64)
    hd = np.zeros((2, 16, 128, 608), np.float32)
    for j in range(2):
        hd[j, :, :, 0:288] = nope[j].transpose(2, 1, 0, 3).reshape(16, 128, 288)
        hd[j, :, :, 288:480] = ukp.transpose(2, 1, 0, 3).reshape(16, 128, 192)
        hd[j, :, :, 480:608] = uv.transpose(2, 1, 0, 3).reshape(16, 128, 128)
    d["w_hd"] = hd
    wo = f(inp["b_w_o"]).reshape(2, 8, 128, 8, 128)
    d["w_o"] = f(wo.transpose(0, 3, 2, 1, 4).reshape(2, 8, 128, 1024))
    dkv = f(inp["w_dkv"]).reshape(8, 128, 288)
    d["w_dkvl"] = f(dkv[..., 0:256].transpose(1, 0, 2).reshape(128, 2048))
    rsw = np.concatenate([dkv[..., 256:288], dkv[..., 272:288], dkv[..., 256:272]], axis=-1)
    d["w_dkvr"] = f(rsw.transpose(1, 0, 2).reshape(128, 512))
    ukf = f(inp["w_uk"])
    d["w_ukT"] = f(ukf.transpose(2, 1, 0).reshape(64, 4096))
    uvp = np.zeros((128, 2, 16, 128), np.float32)
    for h in range(16):
        uvp[:, :, h, (h % 2) * 64:(h % 2) * 64 + 64] = uv[:, :, h, :].transpose(1, 0, 2)
    d["w_uvp"] = f(uvp.reshape(128, 4096))
    vs = []
    for l in range(2):
        pass
    order = [("a_norm_g", 0), ("a_norm_g", 1), ("a_pw1_b", 0), ("a_pw1_b", 1), ("a_dw_b", 0), ("a_dw_b", 1),
             ("a_ln_g", 0), ("a_ln_g", 1), ("a_ln_b", 0), ("a_ln_b", 1), ("a_pw2_b", 0), ("a_pw2_b", 1),
             ("ffn_norm_g", 0), ("ffn_norm_g", 1), ("ffn_norm_g", 2), ("ffn_norm_g", 3)]
    for nm, i in order:
        vs.append(_cols(f(inp[nm])[i]))
    vs.append(_cols(inp["kv_norm_g"]))
    vs.append(_cols(inp["kv_latent_norm_g"]))
    vs.append(_cols(f(inp["b_norm_g"])[0]))
    vs.append(_cols(f(inp["b_norm_g"])[1]))
    vs.append(_cols(f(inp["b_q_norm_g"])[0]))
    vs.append(_cols(f(inp["b_q_norm_g"])[1]))
    vs.append(_cols(inp["final_norm_g"]))
    d["vecs"] = f(np.concatenate(vs, axis=1))
    assert d["vecs"].shape == (128, NV)
    dw = f(inp["a_dw_w"]).reshape(2, 31, 8, 128)
    d["dww"] = f(dw.transpose(3, 0, 2, 1).reshape(128, 2 * 8 * 31))
    d["cache_lat"] = f(inp["cache_latent"]).reshape(5120 * 16, 2048)
    d["cache_kr"] = f(inp["cache_krope"]).reshape(5120 * 16, 256)
    return d


_NC_CACHE = {}


def kernel(**inputs):
    shared = _prep_shared(inputs)
    xp = np.asarray(inputs["x_prompt"], np.float32)
    xs = np.asarray(inputs["x_sample"], np.float32)
    meta = np.asarray(inputs["meta_tokens"], np.float32)
    stc = np.asarray(inputs["state_conv"], np.float32)
    pt = np.asarray(inputs["page_table"], np.int32)
    in_maps = []
    for i in range(8):
        m = dict(shared)
        m["xin"] = np.ascontiguousarray(np.concatenate([meta, xp[i], xs[4 * i:4 * i + 4, 0]], axis=0))
        m["state"] = np.ascontiguousarray(stc[:, 4 * i:4 * i + 4].transpose(0, 2, 1, 3))
        m["ptT"] = np.ascontiguousarray(pt[4 * i:4 * i + 4].T)
        in_maps.append(m)
    if "nc" not in _NC_CACHE:
        _NC_CACHE["nc"] = build_program()
    res = run_bass_kernel_spmd(_NC_CACHE["nc"], in_maps, core_ids=list(range(8)))
    R = res.results
    y_prompt = np.stack([R[i]["y_all"][16:NP] for i in range(8)], 0)
    y_sample = np.concatenate([R[i]["y_all"][NP:NT] for i in range(8)], 0)[:, None, :]
    lat_p = np.stack([R[i]["lat_all"][0:NP] for i in range(8)], 0)
    kr_p = np.stack([R[i]["kr_all"][0:NP] for i in range(8)], 0)
    conv_p = np.stack([R[i]["conv_p"] for i in range(8)], 1)
    lat_s = np.concatenate([R[i]["lat_all"][NP:NT] for i in range(8)], 0)[:, None, :]
    kr_s = np.concatenate([R[i]["kr_all"][NP:NT] for i in range(8)], 0)[:, None, :]
    conv_s = np.concatenate([R[i]["conv_s"] for i in range(8)], 1)
    outs = (y_prompt, y_sample, lat_p, kr_p, conv_p, lat_s, kr_s, conv_s)
    return tuple(np.ascontiguousarray(o, dtype=np.float32) for o in outs)
```

```python
import math
import numpy as np
from contextlib import ExitStack
import concourse.bass as bass
import concourse.mybir as mybir
from concourse.bass_utils import run_bass_kernel_spmd

F32 = mybir.dt.float32
BF16 = mybir.dt.bfloat16
I32 = mybir.dt.int32
ALU = mybir.AluOpType
AF = mybir.ActivationFunctionType
AX = mybir.AxisListType

NT = 2068
NP = 2064
TT = [(0, 512), (512, 512), (1024, 512), (1536, 512), (2048, 20)]
CTL = [(i * 256, 256) for i in range(8)] + [(2048, 20)]
HGS = [(0, 6), (6, 6), (12, 5), (17, 5)]
EPS = 1e-6
SCALE = 1.0 / math.sqrt(96.0)
TWO_PI = 2.0 * math.pi

VOFF = {}
_o = 0
for _name, _n in [("a_norm_g0", 8), ("a_norm_g1", 8), ("a_pw1_b0", 16), ("a_pw1_b1", 16), ("a_dw_b0", 8), ("a_dw_b1", 8),
                  ("a_ln_g0", 8), ("a_ln_g1", 8), ("a_ln_b0", 8), ("a_ln_b1", 8), ("a_pw2_b0", 8), ("a_pw2_b1", 8),
                  ("ffn_g0", 8), ("ffn_g1", 8), ("ffn_g2", 8), ("ffn_g3", 8), ("kv_g", 8), ("kvl_g", 2),
                  ("b_g0", 8), ("b_g1", 8), ("bq_g0", 3), ("bq_g1", 3), ("fin_g", 8)]:
    VOFF[_name] = _o
    _o += _n
NV = _o


class Sem:
    def __init__(self, h, name):
        self.h = h
        self.name = name
        self.count = 0


class Reg:
    __slots__ = ("name", "lw", "rs")

    def __init__(self, name):
        self.name = name
        self.lw = None
        self.rs = []


class Prog:
    ENG = ("pe", "act", "dve", "pool", "sp")

    def __init__(self, nc, stack):
        self.nc = nc
        self.stack = stack
        self.q = {e: [] for e in self.ENG}
        self.sem = {e: self.new_sem("e_" + e) for e in self.ENG}
        self.waited = {e: {} for e in self.ENG}
        self.regs = {}

    def new_sem(self, name):
        return Sem(self.stack.enter_context(self.nc.semaphore(name)), name)

    def sb(self, name, shape, dt):
        return self.stack.enter_context(self.nc.sbuf_tensor("sb_" + name, list(shape), dt))

    def psum(self, name, shape, dt):
        return self.stack.enter_context(self.nc.psum_tensor(name, list(shape), dt))

    def rg(self, *key):
        r = self.regs.get(key)
        if r is None:
            r = Reg(str(key))
            self.regs[key] = r
        return r

    def _waits(self, eng, reads, writes):
        deps = {}

        def add(d):
            if d is None:
                return
            s, v = d
            if deps.get(s, 0) < v:
                deps[s] = v
        for r in reads:
            add(r.lw)
        for w in writes:
            add(w.lw)
            for d in w.rs:
                add(d)
        out = []
        wd = self.waited[eng]
        for s, v in deps.items():
            if eng == "pe" and s is self.sem["pe"]:
                continue
            if wd.get(s, 0) >= v:
                continue
            wd[s] = v
            out.append((s, v))
        return out

    def _commit(self, tok, reads, writes):
        for r in reads:
            r.rs.append(tok)
            if len(r.rs) > 48:
                m = {}
                for s, v in r.rs:
                    if m.get(s, 0) < v:
                        m[s] = v
                r.rs = list(m.items())
        for w in writes:
            w.lw = tok
            w.rs = []

    def op(self, eng, fns, reads=(), writes=()):
        if callable(fns):
            fns = [fns]
        waits = self._waits(eng, reads, writes)
        s = self.sem[eng]
        s.count += 1
        val = s.count
        q = self.q[eng]
        for ws, wv in waits:
            q.append(lambda e, ws=ws, wv=wv: e.wait_ge(ws.h, wv))
        for f in fns[:-1]:
            q.append(f)
        last = fns[-1]
        q.append(lambda e, last=last, s=s: last(e).then_inc(s.h, 1))
        self._commit((s, val), reads, writes)

    def dma(self, queue, out, in_, dsem, reads=(), writes=(), **kw):
        self.raw(queue, lambda e, out=out, in_=in_, kw=kw: e.dma_start(out=out, in_=in_, **kw), dsem, reads, writes)

    def raw(self, queue, fn, dsem, reads=(), writes=()):
        waits = self._waits(queue, reads, writes)
        dsem.count += 16
        val = dsem.count
        q = self.q[queue]
        for ws, wv in waits:
            q.append(lambda e, ws=ws, wv=wv: e.wait_ge(ws.h, wv))
        q.append(lambda e, fn=fn, dsem=dsem: fn(e).then_inc(dsem.h, 16))
        self._commit((dsem, val), reads, writes)

    def final_wait(self, eng, regs):
        for ws, wv in self._waits(eng, regs, ()):
            self.q[eng].append(lambda e, ws=ws, wv=wv: e.wait_ge(ws.h, wv))

    def emit(self):
        with self.nc.Block() as block:
            @block.tensor
            def _(e):
                for f in self.q["pe"]:
                    f(e)

            @block.scalar
            def _(e):
                for f in self.q["act"]:
                    f(e)

            @block.vector
            def _(e):
                for f in self.q["dve"]:
                    f(e)

            @block.gpsimd
            def _(e):
                for f in self.q["pool"]:
                    f(e)

            @block.sync
            def _(e):
                for f in self.q["sp"]:
                    f(e)


def A_(out, in_, func, **kw):
    return lambda e: e.activation(out=out, in_=in_, func=func, **kw)


def TTo(out, a, b, op):
    return lambda e: e.tensor_tensor(out=out, in0=a, in1=b, op=op)


def STT(out, a, s, b, op0, op1):
    return lambda e: e.scalar_tensor_tensor(out=out, in0=a, scalar=s, in1=b, op0=op0, op1=op1)


def TS(out, a, s1, s2, op0, op1=None):
    if op1 is None:
        return lambda e: e.tensor_scalar(out=out, in0=a, scalar1=s1, scalar2=None, op0=op0)
    return lambda e: e.tensor_scalar(out=out, in0=a, scalar1=s1, scalar2=s2, op0=op0, op1=op1)


def CP(out, in_):
    return lambda e: e.tensor_copy(out=out, in_=in_)


def MM(out, l, r, st, sp):
    return lambda e: e.matmul(out, lhsT=l, rhs=r, start=st, stop=sp)


def TR(out, in_, ident):
    return lambda e: e.transpose(out, in_, ident)


def build_program(stage=99, debug=False):
    nc = bass.Bass("TRN2", target_bir_lowering=False)

    def din(name, shape, dt=F32):
        return nc.dram_tensor(name, list(shape), dt, kind="ExternalInput").ap()

    def dout(name, shape):
        return nc.dram_tensor(name, list(shape), F32, kind="ExternalOutput").ap()

    xin = din("xin", [NT, 1024])
    state = din("state", [2, 30, 4, 1024])
    ptT = din("ptT", [128, 4], I32)
    cache_lat = din("cache_lat", [5120 * 16, 2048])
    cache_kr = din("cache_kr", [5120 * 16, 256])
    w_pw1 = din("w_pw1", [2, 8, 128, 2048])
    w_pw2 = din("w_pw2", [2, 8, 128, 1024])
    w_gu = din("w_gu", [4, 22, 128, 2048])
    w_dn = din("w_dn", [4, 8, 128, 2816])
    w_dq = din("w_dq", [2, 3, 128, 1024])
    w_rs = din("w_rs", [2, 4, 128, 768])
    w_hd = din("w_hd", [2, 16, 128, 608])
    w_o = din("w_o", [2, 8, 128, 1024])
    w_dkvl = din("w_dkvl", [128, 2048])
    w_dkvr = din("w_dkvr", [128, 512])
    w_ukT = din("w_ukT", [64, 4096])
    w_uvp = din("w_uvp", [128, 4096])
    vecs_d = din("vecs", [128, NV])
    dww_d = din("dww", [128, 2 * 8 * 31])
    y_all = dout("y_all", [NT, 1024])
    lat_all = dout("lat_all", [NT, 256])
    kr_all = dout("kr_all", [NT, 32])
    conv_p = dout("conv_p", [2, 30, 1024])
    conv_s = dout("conv_s", [2, 4, 30, 1024])

    with ExitStack() as st:
        p = Prog(nc, st)
        rg = p.rg
        xT = p.sb("xT", [128, 8, NT], F32)
        R1 = p.sb("R1", [128, 8 * NT], BF16)
        R2 = p.sb("R2", [128, 8 * 2098], BF16)
        ringt = p.sb("ring", [128, 4, 2048], BF16)
        auxA = p.sb("auxA", [128, 4352], BF16)
        auxB = p.sb("auxB", [128, 2048], F32)
        C4 = p.sb("C4", [128, NT], BF16)
        S4 = p.sb("S4", [128, NT], BF16)
        krT = p.sb("krT", [32, NT], BF16)
        ident_f = p.sb("ident_f", [128, 128], F32)
        ident_b = p.sb("ident_b", [128, 128], BF16)
        ones_b = p.sb("ones_b", [128, 128], BF16)
        tri = p.sb("tri", [128, 128], BF16)
        negm = p.sb("negm", [128, 128], BF16)
        selK = p.sb("selK", [32, 96], BF16)
        selQ = p.sb("selQ", [128, 4, 96], BF16)
        vecs = p.sb("vecs", [128, NV], F32)
        dww = p.sb("dww", [128, 2, 8, 31], F32)
        sqb = [p.sb("sqb%d" % i, [128, 512], BF16) for i in range(2)]
        tA = [p.sb("tA%d" % i, [128, 512], F32) for i in range(2)]
        tB = [p.sb("tB%d" % i, [128, 512], F32) for i in range(2)]
        PT = [p.sb("PT%d" % i, [128, 512], BF16) for i in range(3)]
        iot = p.sb("iot", [128, 2, 1024], F32)
        io = [iot[:, 0, :], iot[:, 1, :]]
        iob = iot[:, :, :].rearrange("p a b -> p (a b)").bitcast(BF16)
        g_tail = p.sb("g_tail", [128, 8, 34], F32)
        bufT = p.sb("bufT", [128, 8, 4, 31], F32)
        cols = p.sb("cols", [128, 8], F32)
        icol = p.sb("icol", [128, 4], I32)
        idx = p.sb("idx", [128, 4], I32)
        idxg = p.sb("idxg", [128, 4, 16], I32)
        Qs = p.sb("Qs", [96, 16, 4], BF16)
        PS = [p.psum("ps%d" % i, [128, 512], F32) for i in range(8)]
        psr = [rg("ps", i) for i in range(8)]
        ps_ctr = [0]

        def nps():
            i = ps_ctr[0] % 6
            ps_ctr[0] += 1
            return PS[i], psr[i]

        aps_ctr = [0]

        def aps():
            i = 6 + aps_ctr[0] % 2
            aps_ctr[0] += 1
            return PS[i], psr[i]

        d_const = p.new_sem("d_const")
        d_const2 = p.new_sem("d_const2")
        d_const3 = p.new_sem("d_const3")
        d_misc2 = p.new_sem("d_misc2")
        d_o = [p.new_sem("d_o%d" % i) for i in range(2)]
        d_io = [p.new_sem("d_io%d" % i) for i in range(2)]
        d_out = p.new_sem("d_out")
        d_ring = [p.new_sem("d_ring%d" % i) for i in range(4)]
        d_misc = p.new_sem("d_misc")
        out_reg = rg("out")

        hT = R1[:, :].rearrange("p (c t) -> p c t", c=8)
        gpad = R2[:, :].rearrange("p (c t) -> p c t", c=8)
        hid = R2[:, 0:6 * NT].rearrange("p (c t) -> p c t", c=6)
        cqn = R2[:, 0:3 * NT].rearrange("p (c t) -> p c t", c=3)
        Qh = R2[:, 3 * NT:4 * NT]
        Kh = R2[:, 4 * NT:5 * NT]
        cT = R2[:, 6 * NT:8 * NT].rearrange("p (c t) -> p c t", c=2)
        diagB = [auxA[:, 0:31 * 128].rearrange("p (k j) -> p k j", k=31),
                 auxB[:, :].bitcast(BF16)[:, 0:31 * 128].rearrange("p (k j) -> p k j", k=31)]
        Rq = auxA[:, 0:NT]
        ybuf = auxB[:, :].rearrange("p (c t) -> p c t", c=8)
        ring_ctr = [0]

        def ring_load(src_ap, n):
            i = ring_ctr[0] % 4
            ring_ctr[0] += 1
            r = rg("ring", i)
            np_ = src_ap.shape[0]
            dst = ringt[0:np_, i, 0:n]
            p.dma("pool", dst, src_ap, d_ring[i], writes=[r])
            return ringt[:, i, :], r

        def V(name, c=0):
            o = VOFF[name] + c
            return vecs[:, o:o + 1]

        p.dma("sp", vecs[:, :], vecs_d, d_const, writes=[rg("vecs")])
        p.dma("sp", dww[:, :, :, :], dww_d.rearrange("p (l c k) -> p l c k", l=2, c=8), d_const2, writes=[rg("dww")])
        p.dma("sp", idx[:, :], ptT, d_const3, writes=[rg("idx")])
        for g in range(16):
            p.op("dve", TS(idxg[:, :, g], idx[:, :], 16, g, ALU.mult, ALU.add), reads=[rg("idx")], writes=[rg("idxg")])
        R2f = R2[:, :].bitcast(F32)
        R2i = R2[:, :].bitcast(I32)
        iota_row = R2f[:, 0:128]
        scr = rg("scr")
        p.op("pool", lambda e: e.iota(iota_row, pattern=[[1, 128]], base=0, channel_multiplier=0,
                                      allow_small_or_imprecise_dtypes=True), writes=[scr])
        p.op("pool", lambda e: e.iota(cols[:, 0:1], pattern=[[0, 1]], base=0, channel_multiplier=1,
                                      allow_small_or_imprecise_dtypes=True), writes=[rg("cols")])
        p.op("pool", lambda e: e.iota(icol[:, 0:1], pattern=[[0, 1]], base=0, channel_multiplier=1), writes=[rg("icol")])
        p.op("dve", TS(ident_f[:, :], iota_row, cols[:, 0:1], None, ALU.is_equal), reads=[scr, rg("cols")], writes=[rg("ident")])
        p.op("dve", CP(ident_b[:, :], ident_f[:, :]), reads=[rg("ident")], writes=[rg("identb")])
        p.op("dve", TS(tri[:, :], iota_row, cols[:, 0:1], None, ALU.is_ge), reads=[scr, rg("cols")], writes=[rg("tri")])
        p.op("dve", TS(negm[:, :], tri[:, :], 30000.0, -30000.0, ALU.mult, ALU.add), reads=[rg("tri")], writes=[rg("negm")])
        p.op("pool", lambda e: e.memset(ones_b[:, :], 1.0), writes=[rg("ones")])
        p.op("pool", lambda e: e.memset(selK[:, :], 0.0), writes=[rg("selK")])
        p.op("pool", lambda e: e.memset(selQ[:, :, :], 0.0), writes=[rg("selQ")])
        p.op("dve", CP(selK[:, 64:96], ident_b[0:32, 0:32]), reads=[rg("identb")], writes=[rg("selK")])
        for i in range(4):
            p.op("dve", CP(selQ[:, i, 64:96], ident_b[:, 32 * i:32 * i + 32]), reads=[rg("identb")], writes=[rg("selQ")])
        p.op("dve", lambda e: e.tensor_single_scalar(out=icol[:, 1:2], in_=icol[:, 0:1], scalar=15, op=ALU.bitwise_and),
             reads=[rg("icol")], writes=[rg("icol")])
        p.op("dve", lambda e: e.tensor_single_scalar(out=icol[:, 2:3], in_=icol[:, 0:1], scalar=16, op=ALU.bitwise_and),
             reads=[rg("icol")], writes=[rg("icol")])
        p.op("dve", CP(cols[:, 1:3], icol[:, 1:3]), reads=[rg("icol")], writes=[rg("cols")])
        p.op("act", A_(cols[:, 1:2], cols[:, 1:2], AF.Exp, scale=-math.log(10000.0) / 16.0), reads=[rg("cols")], writes=[rg("cols")])
        p.op("dve", TS(cols[:, 2:3], cols[:, 2:3], 1.0 / 8.0, -1.0, ALU.mult, ALU.add), reads=[rg("cols")], writes=[rg("cols")])
        p.op("pool", lambda e: e.memset(cols[:, 3:4], EPS), writes=[rg("cols")])
        p.op("pool", lambda e: e.memset(cols[:, 4:5], 0.0), writes=[rg("cols")])
        pos = R2f[:, 0:NT]
        ang = R2f[:, NT:2 * NT]
        kf = R2f[:, 2 * NT:3 * NT]
        ki = R2i[:, 3 * NT:4 * NT]
        p.op("pool", lambda e: e.iota(pos, pattern=[[1, NT]], base=0, channel_multiplier=0,
                                      allow_small_or_imprecise_dtypes=True), reads=[rg("ident"), rg("tri")], writes=[scr])
        p.op("pool", lambda e: e.memset(pos[:, NP:NT], 16384.0), writes=[scr])
        for which, dst in ((0, S4), (1, C4)):
            if which == 0:
                p.op("dve", TS(ang, pos, cols[:, 1:2], None, ALU.mult), reads=[rg("cols")], writes=[scr])
            else:
                p.op("dve", TS(ang, pos, cols[:, 1:2], math.pi / 2.0, ALU.mult, ALU.add), reads=[rg("cols")], writes=[scr])
            p.op("dve", TS(kf, ang, 1.0 / TWO_PI, None, ALU.mult), writes=[scr])
            p.op("dve", CP(ki, kf), writes=[scr])
            p.op("dve", CP(kf, ki), writes=[scr])
            p.op("dve", STT(ang, kf, -TWO_PI, ang, ALU.mult, ALU.add), writes=[scr])
            p.op("dve", TS(ang, ang, 3.14159, -3.14159, ALU.min, ALU.max), writes=[scr])
            p.op("act", A_(ang, ang, AF.Sin), writes=[scr])
            if which == 0:
                p.op("dve", TS(dst[:, :], ang, cols[:, 2:3], None, ALU.mult), reads=[rg("cols")], writes=[scr, rg("tabs")])
            else:
                p.op("dve", CP(dst[:, :], ang), writes=[scr, rg("tabs")])

        def rstd_from(srcs, src_regs, n, D, sq_eng="act"):
            pss, pssr = nps()
            nc_ = len(srcs)
            for c, (s, sr) in enumerate(zip(srcs, src_regs)):
                b = c % 2
                if sq_eng == "act":
                    p.op("act", A_(sqb[b][:, 0:n], s, AF.Square), reads=[sr], writes=[rg("sqb", b)])
                else:
                    p.op(sq_eng, TTo(sqb[b][:, 0:n], s, s, ALU.mult), reads=[sr], writes=[rg("sqb", b)])
                p.op("pe", MM(pss[:, 0:n], ones_b[:, :], sqb[b][:, 0:n], c == 0, c == nc_ - 1),
                     reads=[rg("sqb", b), rg("ones")], writes=[pssr])
            k = rstd_from.ctr % 2
            rstd_from.ctr += 1
            t = tA[k]
            tr_ = rg("tA", k)
            p.op("act", A_(t[:, 0:n], pss[:, 0:n], AF.Ln, bias=cols[:, 3:4], scale=1.0 / D), reads=[pssr, rg("cols")], writes=[tr_])
            p.op("act", A_(t[:, 0:n], t[:, 0:n], AF.Exp, scale=-0.5), reads=[tr_], writes=[tr_])
            return t, tr_
        rstd_from.ctr = 0

        def HR(ti):
            return [rg("hT", ti)] + [rg("hTc", c, ti) for c in range(8)]

        def norm_tile(gname, ti):
            t0, n = TT[ti]
            srcs = [xT[:, c, t0:t0 + n] for c in range(8)]
            t, tr_ = rstd_from(srcs, [rg("xT", c, ti) for c in range(8)], n, 1024.0, sq_eng="pool")
            for c in range(8):
                eng = "dve"
                if c == 0:
                    rd_, wr_ = [], [rg("hT", ti), rg("hTc", 0, ti)]
                else:
                    rd_, wr_ = [rg("hT", ti)], [rg("hTc", c, ti)]
                p.op(eng, STT(hT[:, c, t0:t0 + n], xT[:, c, t0:t0 + n], V(gname, c), t[:, 0:n], ALU.mult, ALU.mult),
                     reads=[rg("xT", c, ti), tr_, rg("vecs")] + rd_, writes=wr_)

        def first_pass(gname, per_tile):
            norm_tile(gname, 0)
            for ti in range(5):
                if ti + 1 < 5:
                    norm_tile(gname, ti + 1)
                per_tile(ti)

        def ffn(l):
            def gu(Wv, wr, nn, ti):
                t0, n = TT[ti]
                pg, pgr = nps()
                pu, pur = nps()
                p.op("pe", [MM(pg[:, 0:n], Wv[:, 0, k, :], hT[:, k, t0:t0 + n], k == 0, k == 7) for k in range(8)],
                     reads=[wr] + HR(ti), writes=[pgr])
                p.op("pe", [MM(pu[:, 0:n], Wv[:, 1, k, :], hT[:, k, t0:t0 + n], k == 0, k == 7) for k in range(8)],
                     reads=[wr] + HR(ti), writes=[pur])
                b = ti % 2
                p.op("act", A_(tB[b][:, 0:n], pg[:, 0:n], AF.Silu), reads=[pgr], writes=[rg("tB", b)])
                p.op("dve", TTo(hid[:, nn, t0:t0 + n], pu[:, 0:n], tB[b][:, 0:n], ALU.mult),
                     reads=[pur, rg("tB", b)], writes=[rg("hid", nn, ti)])
            KF = 3
            Wf = []
            for nn in range(KF):
                W, wr = ring_load(w_gu[l, nn], 2048)
                Wf.append((W.rearrange("p (g k j) -> p g k j", g=2, k=8), wr))

            def pt(ti):
                for nn in range(KF):
                    gu(Wf[nn][0], Wf[nn][1], nn, ti)
            first_pass("ffn_g%d" % l, pt)
            for gi, (n0, cnt) in enumerate(HGS):
                for nn in range(cnt):
                    if gi == 0 and nn < KF:
                        continue
                    W, wr = ring_load(w_gu[l, n0 + nn], 2048)
                    Wv = W.rearrange("p (g k j) -> p g k j", g=2, k=8)
                    for ti in range(5):
                        gu(Wv, wr, nn, ti)
                for m in range(8):
                    W, wr = ring_load(w_dn[l, m][:, n0 * 128:(n0 + cnt) * 128], cnt * 128)
                    Wv = W[:, 0:cnt * 128].rearrange("p (n j) -> p n j", n=cnt)
                    for ti, (t0, n) in enumerate(TT):
                        po, por = nps()
                        p.op("pe", [MM(po[:, 0:n], Wv[:, nn, :], hid[:, nn, t0:t0 + n], nn == 0, nn == cnt - 1) for nn in range(cnt)],
                             reads=[wr] + [rg("hid", nn, ti) for nn in range(cnt)], writes=[por])
                        p.op("dve", TTo(xT[:, m, t0:t0 + n], po[:, 0:n], xT[:, m, t0:t0 + n], ALU.add),
                             reads=[por], writes=[rg("xT", m, ti)])

        for bi in range(17):
            r0 = bi * 128
            nb = min(128, NT - r0)
            ti = r0 // 512
            b = bi % 2
            p.dma("sp", io[b][0:nb, :], xin[r0:r0 + nb, :], d_io[b], writes=[rg("io", b)])
            for half in range(2):
                pt_, ptr_ = nps()
                p.op("pe", [TR(pt_[:, j * 128:j * 128 + nb], io[b][0:nb, (half * 4 + j) * 128:(half * 4 + j + 1) * 128], ident_f[0:nb, 0:nb])
                            for j in range(4)], reads=[rg("io", b), rg("ident")], writes=[ptr_])
                eng = "act" if half == 0 else "dve"
                src = pt_[:, :].rearrange("p (j t) -> p j t", j=4)[:, :, 0:nb]
                dstv = xT[:, half * 4:half * 4 + 4, r0:r0 + nb]
                if eng == "act":
                    p.op("act", A_(dstv, src, AF.Copy), reads=[ptr_], writes=[rg("xT", half * 4 + j, ti) for j in range(4)])
                else:
                    p.op("dve", CP(dstv, src), reads=[ptr_], writes=[rg("xT", half * 4 + j, ti) for j in range(4)])

        def a_layer(l):
            p.op("pool", lambda e: e.memset(gpad[:, :, 0:30], 0.0),
                 writes=[rg("gpad", c) for c in range(8)] + [scr] + [rg("hid", nn, ti) for nn in range(6) for ti in range(5)])

            def pw1(Wv, wr, c, ti):
                t0, n = TT[ti]
                pa, par = nps()
                pb, pbr = nps()
                p.op("pe", [MM(pa[:, 0:n], Wv[:, 0, k, :], hT[:, k, t0:t0 + n], k == 0, k == 7) for k in range(8)],
                     reads=[wr] + HR(ti), writes=[par])
                p.op("pe", [MM(pb[:, 0:n], Wv[:, 1, k, :], hT[:, k, t0:t0 + n], k == 0, k == 7) for k in range(8)],
                     reads=[wr] + HR(ti), writes=[pbr])
                b = ti % 2
                p.op("act", A_(tB[b][:, 0:n], pb[:, 0:n], AF.Sigmoid, bias=V("a_pw1_b%d" % l, 8 + c)),
                     reads=[pbr, rg("vecs")], writes=[rg("tB", b)])
                p.op("dve", STT(gpad[:, c, 30 + t0:30 + t0 + n], pa[:, 0:n], V("a_pw1_b%d" % l, c), tB[b][:, 0:n], ALU.add, ALU.mult),
                     reads=[par, rg("tB", b)], writes=[rg("gpad", c)])
                if t0 + n > 2034:
                    s0 = max(t0, 2034)
                    p.op("dve", STT(g_tail[:, c, s0 - 2034:t0 + n - 2034], pa[:, s0 - t0:n], V("a_pw1_b%d" % l, c),
                                    tB[b][:, s0 - t0:n], ALU.add, ALU.mult),
                         reads=[par, rg("tB", b)], writes=[rg("g_tail")])
            KF = 3
            Wf = []
            for c in range(KF):
                W, wr = ring_load(w_pw1[l, c], 2048)
                Wf.append((W.rearrange("p (g k j) -> p g k j", g=2, k=8), wr))

            def pt(ti):
                for c in range(KF):
                    pw1(Wf[c][0], Wf[c][1], c, ti)
            first_pass("a_norm_g%d" % l, pt)
            for c in range(KF, 8):
                W, wr = ring_load(w_pw1[l, c], 2048)
                Wv = W.rearrange("p (g k j) -> p g k j", g=2, k=8)
                for ti in range(5):
                    pw1(Wv, wr, c, ti)
            for b4 in range(4):
                b = b4 % 2
                p.dma("sp", io[b][0:30, :], state[l, :, b4, :], d_io[b], writes=[rg("io", b)])
                p.dma("sp", conv_s[l, b4, 0:29, :], io[b][1:30, :], d_o[b], reads=[rg("io", b)])
                pt_, ptr_ = nps()
                p.op("pe", [TR(pt_[:, c * 30:c * 30 + 30], io[b][0:30, c * 128:(c + 1) * 128], ident_f[0:30, 0:30]) for c in range(8)],
                     reads=[rg("io", b), rg("ident")], writes=[ptr_])
                p.op("act", A_(bufT[:, :, b4, 0:30], pt_[:, 0:240].rearrange("p (c t) -> p c t", c=8), AF.Copy),
                     reads=[ptr_], writes=[rg("bufT")])
            p.op("dve", CP(bufT[:, :, :, 30], g_tail[:, :, 30:34]), reads=[rg("g_tail")], writes=[rg("bufT")])
            b = 0
            for half in range(2):
                pt_, ptr_ = nps()
                p.op("pe", [TR(pt_[0:34, j * 128:(j + 1) * 128], g_tail[:, half * 4 + j, :], ident_f[:, :]) for j in range(4)],
                     reads=[rg("g_tail"), rg("ident")], writes=[ptr_])
                p.op("act", A_(io[b][0:34, half * 512:(half + 1) * 512], pt_[0:34, :], AF.Copy), reads=[ptr_], writes=[rg("io", b)])
            p.dma("sp", conv_p[l, :, :], io[b][0:30, :], d_o[b], reads=[rg("io", b)])
            for b4 in range(4):
                p.dma("sp", conv_s[l, b4, 29:30, :], io[b][30 + b4:31 + b4, :], d_o[b], reads=[rg("io", b)])
            tmp = tB[0][:, 0:8 * 31 * 2].rearrange("p (c b k) -> p c b k", c=8, b=2)
            ysr = rg("ys")
            for hb in range(2):
                p.op("dve", TTo(tmp, bufT[:, :, hb * 2:hb * 2 + 2, :],
                                dww[:, l, :, :].unsqueeze(2).broadcast_to([128, 8, 2, 31]), ALU.mult),
                     reads=[rg("bufT"), rg("dww")], writes=[rg("tB", 0)])
                p.op("dve", lambda e, hb=hb: e.tensor_reduce(out=tA[0][:, hb * 16:hb * 16 + 16].rearrange("p (c b) -> p c b", c=8),
                                                              in_=tmp, axis=AX.X, op=ALU.add),
                     reads=[rg("tB", 0)], writes=[rg("tA", 0), ysr])
            seen_t = set()

            def evac(c, ti, pc, pcr):
                t0, n = TT[ti]
                extra = []
                if ti not in seen_t:
                    seen_t.add(ti)
                    extra = [rg("hT", ti)]
                p.op("act", A_(hT[:, c, t0:t0 + n], pc[:, 0:n], AF.Identity, bias=V("a_dw_b%d" % l, c)),
                     reads=[pcr, rg("vecs")], writes=[rg("yc", c, ti)] + extra)
                if ti == 4:
                    for hb in range(2):
                        p.op("dve", TS(hT[:, c, NP + hb * 2:NP + 2 + hb * 2], tA[0][:, hb * 16 + c * 2:hb * 16 + c * 2 + 2],
                                       V("a_dw_b%d" % l, c), None, ALU.add),
                             reads=[ysr, rg("tA", 0), rg("vecs")], writes=[rg("yc", c, 4)])
            PEC = list(range(8))
            DVC = []
            dsteps = [(c, ti) for c in DVC for ti in range(5)]
            dacc = {}

            def dve_step(j):
                c, ti = dsteps[j]
                t0, n = TT[ti]
                acc, accr = aps()
                dacc[j] = (acc, accr)
                p.op("dve", TS(acc[:, 0:n], gpad[:, c, t0:t0 + n], dww[:, l, c, 0:1], None, ALU.mult),
                     reads=[rg("gpad", c), rg("dww")], writes=[accr])
                for k in range(1, 31):
                    p.op("dve", STT(acc[:, 0:n], gpad[:, c, t0 + k:t0 + k + n], dww[:, l, c, k:k + 1], acc[:, 0:n], ALU.mult, ALU.add),
                         reads=[rg("gpad", c), rg("dww")], writes=[accr])
            psteps = [(c, ti) for c in PEC for ti in range(5)]
            nd = 0
            for i, (c, ti) in enumerate(psteps):
                t0, n = TT[ti]
                db = c % 2
                dg = diagB[db]
                if ti == 0:
                    p.op("pool", TTo(dg[:, :, :], ident_b[:, :].unsqueeze(1).broadcast_to([128, 31, 128]),
                                     dww[:, l, c, :].unsqueeze(2).broadcast_to([128, 31, 128]), ALU.mult),
                         reads=[rg("identb"), rg("dww")], writes=[rg("diag", db)] + ([rg("auxB"), rg("clf", 0), rg("clf", 1), rg("krf", 0), rg("krf", 1), rg("Vh", 1)] if db == 1 else [rg("auxA"), rg("Vh", 0)] + [rg("Rq", t_) for t_ in range(5)]))
                if i % 3 == 0 and i // 3 < len(dsteps):
                    dve_step(i // 3)
                pc, pcr = nps()
                p.op("pe", [MM(pc[:, 0:n], dg[:, k, :], gpad[:, c, t0 + k:t0 + k + n], k == 0, k == 30) for k in range(31)],
                     reads=[rg("diag", db), rg("gpad", c)], writes=[pcr])
                evac(c, ti, pc, pcr)
                if i % 3 == 2 and i // 3 < len(dsteps):
                    j = i // 3
                    evac(dsteps[j][0], dsteps[j][1], dacc[j][0], dacc[j][1])
            def ln_tile(ti):
                t0, n = TT[ti]
                ps1, ps1r = aps()
                ps2, ps2r = aps()
                for c in range(8):
                    b = c % 2
                    p.op("pool", TTo(sqb[b][:, 0:n], hT[:, c, t0:t0 + n], hT[:, c, t0:t0 + n], ALU.mult), reads=[rg("yc", c, ti)], writes=[rg("sqb", b)])
                    p.op("pe", [MM(ps2[:, 0:n], ones_b[:, :], sqb[b][:, 0:n], c == 0, c == 7),
                                MM(ps1[:, 0:n], ones_b[:, :], hT[:, c, t0:t0 + n], c == 0, c == 7)],
                         reads=[rg("sqb", b), rg("ones"), rg("yc", c, ti)], writes=[ps1r, ps2r])
                mean = tA[1][:, 0:n]
                var = tA[0][:, 0:n]
                mr = rg("tA", 1)
                vr = rg("tA", 0)
                p.op("dve", TS(mean, ps1[:, 0:n], 1.0 / 1024.0, None, ALU.mult), reads=[ps1r], writes=[mr])
                p.op("dve", TTo(var, mean, mean, ALU.mult), reads=[mr, ysr], writes=[vr])
                p.op("dve", STT(var, ps2[:, 0:n], 1.0 / 1024.0, var, ALU.mult, ALU.subtract), reads=[ps2r, vr], writes=[vr])
                p.op("act", A_(var, var, AF.Ln, bias=cols[:, 3:4]), reads=[vr, rg("cols")], writes=[vr])
                p.op("act", A_(var, var, AF.Exp, scale=-0.5), reads=[vr], writes=[vr])
                for c in range(8):
                    b = c % 2
                    eng = "dve"
                    tz = tB[b][:, 0:n]
                    p.op(eng, TTo(tz, hT[:, c, t0:t0 + n], mean, ALU.subtract), reads=[rg("yc", c, ti), mr], writes=[rg("tB", b)])
                    p.op(eng, TTo(tz, tz, var, ALU.mult), reads=[vr, rg("tB", b)], writes=[rg("tB", b)])
                    p.op("act", A_(hT[:, c, t0:t0 + n], tz, AF.Silu, bias=V("a_ln_b%d" % l, c), scale=V("a_ln_g%d" % l, c)),
                         reads=[rg("tB", b), rg("vecs")], writes=[rg("yc", c, ti)])

            def pw2(Wv, wr, m, ti):
                t0, n = TT[ti]
                po, por = nps()
                p.op("pe", [MM(po[:, 0:n], Wv[:, k, :], hT[:, k, t0:t0 + n], k == 0, k == 7) for k in range(8)],
                     reads=[wr, rg("hT", ti)] + [rg("yc", c, ti) for c in range(8)], writes=[por])
                p.op("dve", STT(xT[:, m, t0:t0 + n], po[:, 0:n], V("a_pw2_b%d" % l, m), xT[:, m, t0:t0 + n], ALU.add, ALU.add),
                     reads=[por, rg("vecs")], writes=[rg("xT", m, ti)])
            KF2 = 4
            Wf2 = []
            for m in range(KF2):
                W, wr = ring_load(w_pw2[l, m], 1024)
                Wf2.append((W[:, 0:1024].rearrange("p (k j) -> p k j", k=8), wr))
            ln_tile(0)
            for ti in range(5):
                if ti + 1 < 5:
                    ln_tile(ti + 1)
                for m in range(KF2):
                    pw2(Wf2[m][0], Wf2[m][1], m, ti)
            for m in range(KF2, 8):
                W, wr = ring_load(w_pw2[l, m], 1024)
                Wv = W[:, 0:1024].rearrange("p (k j) -> p k j", k=8)
                for ti in range(5):
                    pw2(Wv, wr, m, ti)
            ffn(l)

        for l in range(2):
            a_layer(l)

        Wl, wlr = ring_load(w_dkvl, 2048)
        Wlv = Wl.rearrange("p (k j) -> p k j", k=8)
        Wr_, wrr = ring_load(w_dkvr, 512)
        Wrv = Wr_[:, 0:512].rearrange("p (k j) -> p k j", k=8)
        auxAf = auxA[:, :].bitcast(F32)
        clfs = [auxB[:, 0:1024].rearrange("p (c t) -> p c t", c=2), auxAf[:, 0:1024].rearrange("p (c t) -> p c t", c=2)]
        krfs = [auxB[0:32, 1024:1536], auxAf[0:32, 1024:1536]]
        def kv_tile(ti):
            t0, n = TT[ti]
            clf = clfs[ti % 2]
            krf = krfs[ti % 2]
            sx = ti % 2
            pl = [nps() for _ in range(2)]
            for kc in range(2):
                p.op("pe", [MM(pl[kc][0][:, 0:n], Wlv[:, k, kc * 128:(kc + 1) * 128], hT[:, k, t0:t0 + n], k == 0, k == 7) for k in range(8)],
                     reads=[wlr] + HR(ti), writes=[pl[kc][1]])
            pr1, pr1r = nps()
            pr2, pr2r = nps()
            p.op("pe", [MM(pr1[0:32, 0:n], Wrv[:, k, 0:32], hT[:, k, t0:t0 + n], k == 0, k == 7) for k in range(8)],
                 reads=[wrr] + HR(ti), writes=[pr1r])
            p.op("pe", [MM(pr2[0:32, 0:n], Wrv[:, k, 32:64], hT[:, k, t0:t0 + n], k == 0, k == 7) for k in range(8)],
                 reads=[wrr] + HR(ti), writes=[pr2r])
            t, tr_ = rstd_from([pl[0][0][:, 0:n], pl[1][0][:, 0:n]], [pl[0][1], pl[1][1]], n, 256.0)
            for kc in range(2):
                p.op("dve", STT(clf[:, kc, 0:n], pl[kc][0][:, 0:n], V("kvl_g", kc), t[:, 0:n], ALU.mult, ALU.mult),
                     reads=[pl[kc][1], tr_, rg("vecs")], writes=[rg("clf", sx)])
            p.op("act", A_(cT[:, :, t0:t0 + n], clf[:, :, 0:n], AF.Copy), reads=[rg("clf", sx)], writes=[rg("cT", ti)])
            t1 = tB[0][0:32, 0:n]
            t2 = tB[1][0:32, 0:n]
            p.op("dve", TTo(t1, pr1[0:32, 0:n], C4[0:32, t0:t0 + n], ALU.mult), reads=[pr1r, rg("tabs")], writes=[rg("tB", 0)])
            p.op("dve", TTo(t2, pr2[0:32, 0:n], S4[0:32, t0:t0 + n], ALU.mult), reads=[pr2r, rg("tabs")], writes=[rg("tB", 1)])
            p.op("dve", TTo(krf[:, 0:n], t1, t2, ALU.add), reads=[rg("tB", 0), rg("tB", 1)], writes=[rg("krf", sx)])
            p.op("act", A_(krT[:, t0:t0 + n], krf[:, 0:n], AF.Copy), reads=[rg("krf", sx)], writes=[rg("krT", ti)])
        def kv_out(ti):
            t0, n = TT[ti]
            clf = clfs[ti % 2]
            krf = krfs[ti % 2]
            sx = ti % 2
            for bi in range((n + 127) // 128):
                c0 = bi * 128
                nb = min(128, n - c0)
                b = bi % 2
                pt_, ptr_ = nps()
                p.op("pe", [TR(pt_[0:nb, 0:128], clf[:, 0, c0:c0 + nb], ident_f[:, :]),
                            TR(pt_[0:nb, 128:256], clf[:, 1, c0:c0 + nb], ident_f[:, :]),
                            TR(pt_[0:nb, 256:288], krf[:, c0:c0 + nb], ident_f[0:32, 0:32])],
                     reads=[rg("clf", sx), rg("krf", sx), rg("ident")], writes=[ptr_])
                p.op("act", A_(io[b][0:nb, 0:288], pt_[0:nb, 0:288], AF.Copy), reads=[ptr_], writes=[rg("io", b)])
                p.dma("sp", lat_all[t0 + c0:t0 + c0 + nb, :], io[b][0:nb, 0:256], d_o[b], reads=[rg("io", b)])
                p.dma("sp", kr_all[t0 + c0:t0 + c0 + nb, :], io[b][0:nb, 256:288], d_o[b], reads=[rg("io", b)])

        def kv_pt(ti):
            kv_tile(ti)
            if ti > 0:
                kv_out(ti - 1)
        first_pass("kv_g", kv_pt)
        kv_out(4)

        Vh = [auxA[:, 2080:2080 + 2176].rearrange("p (k j) -> p k j", k=17),
              auxB[:, :].bitcast(BF16)[:, 0:2176].rearrange("p (k j) -> p k j", k=17)]
        OT = hT
        QhB = [R2[:, 3 * NT:4 * NT], R2[:, 5 * NT:6 * NT]]
        KhB = [R2[:, 4 * NT:5 * NT], iob[:, 0:NT]]

        def b_layer(j):
            l = 2 + j
            Wd = [ring_load(w_dq[j, n3], 1024) for n3 in range(3)]

            def dq_tile(ti):
                t0, n = TT[ti]
                pq = [nps() for _ in range(3)]
                for n3 in range(3):
                    Wv = Wd[n3][0][:, 0:1024].rearrange("p (k j) -> p k j", k=8)
                    p.op("pe", [MM(pq[n3][0][:, 0:n], Wv[:, k, :], hT[:, k, t0:t0 + n], k == 0, k == 7) for k in range(8)],
                         reads=[Wd[n3][1]] + HR(ti), writes=[pq[n3][1]])
                t, tr_ = rstd_from([pq[i][0][:, 0:n] for i in range(3)], [pq[i][1] for i in range(3)], n, 384.0)
                for n3 in range(3):
                    p.op("dve", STT(cqn[:, n3, t0:t0 + n], pq[n3][0][:, 0:n], V("bq_g%d" % j, n3), t[:, 0:n], ALU.mult, ALU.mult),
                         reads=[pq[n3][1], tr_, rg("vecs")], writes=[rg("cqn", ti)])
            first_pass("b_g%d" % j, dq_tile)
            p.op("pool", lambda e: e.memset(Vh[0][:, :, 64:128], 1.0), writes=[rg("Vh", 0), rg("auxA"), rg("auxB"), rg("diag", 0), rg("diag", 1), rg("clf", 0), rg("clf", 1), rg("krf", 0), rg("krf", 1)])
            p.op("pool", lambda e: e.memset(Vh[1][:, :, 0:64], 1.0), writes=[rg("Vh", 1), rg("auxA"), rg("auxB"), rg("diag", 0), rg("diag", 1), rg("clf", 0), rg("clf", 1), rg("krf", 0), rg("krf", 1)] + [rg("ybuf", c) for c in range(8)])
            def rope(q4):
                W, wr = ring_load(w_rs[j, q4], 768)
                Wv = W[:, 0:768].rearrange("p (k s j) -> p k s j", k=3, s=2)
                for ti, (t0, n) in enumerate(TT):
                    pR, pRr = nps()
                    pS, pSr = nps()
                    p.op("pe", [MM(pR[:, 0:n], Wv[:, k, 0, :], cqn[:, k, t0:t0 + n], k == 0, k == 2) for k in range(3)],
                         reads=[wr, rg("cqn", ti)], writes=[pRr])
                    p.op("pe", [MM(pS[:, 0:n], Wv[:, k, 1, :], cqn[:, k, t0:t0 + n], k == 0, k == 2) for k in range(3)],
                         reads=[wr, rg("cqn", ti)], writes=[pSr])
                    p.op("dve", TTo(tB[0][:, 0:n], pR[:, 0:n], C4[:, t0:t0 + n], ALU.mult), reads=[pRr, rg("tabs")], writes=[rg("tB", 0)])
                    p.op("dve", TTo(tB[1][:, 0:n], pS[:, 0:n], S4[:, t0:t0 + n], ALU.mult), reads=[pSr, rg("tabs")], writes=[rg("tB", 1)])
                    p.op("pool", TTo(Rq[:, t0:t0 + n], tB[0][:, 0:n], tB[1][:, 0:n], ALU.add),
                         reads=[rg("tB", 0), rg("tB", 1)], writes=[rg("Rq", ti)])

            def gen(h):
                par = h % 2
                e_ = par
                vo = 64 * e_
                Qh = QhB[par]
                Kh = KhB[par]
                kx = [rg("io", 0), rg("io", 1)] if par == 1 else []
                W, wr = ring_load(w_hd[j, h], 608)
                uqn = W[:, 0:288].rearrange("p (k j) -> p k j", k=3)
                ukp = W[:, 288:480].rearrange("p (k j) -> p k j", k=2)
                uvh = W[:, 480:608].rearrange("p (k j) -> p k j", k=2)
                for ti, (t0, n) in enumerate(TT):
                    pq_, pqr = nps()
                    p.op("pe", [MM(pq_[0:96, 0:n], uqn[:, k, :], cqn[:, k, t0:t0 + n], k == 0, False) for k in range(3)]
                         + [MM(pq_[0:96, 0:n], selQ[:, h % 4, :], Rq[:, t0:t0 + n], False, True)],
                         reads=[wr, rg("cqn", ti), rg("Rq", ti), rg("selQ")], writes=[pqr])
                    p.op("act", A_(Qh[0:96, t0:t0 + n], pq_[0:96, 0:n], AF.Copy), reads=[pqr], writes=[rg("Qh", par, ti)])
                    if ti == 4:
                        p.op("act", A_(Qs[:, h, :], pq_[0:96, 16:20], AF.Copy), reads=[pqr], writes=[rg("Qs")])
                    nk_ = min(n, NP - t0)
                    pk_, pkr = nps()
                    p.op("pe", [MM(pk_[0:96, 0:nk_], ukp[:, k, :], cT[:, k, t0:t0 + nk_], k == 0, False) for k in range(2)]
                         + [MM(pk_[0:96, 0:nk_], selK[:, :], krT[:, t0:t0 + nk_], False, True)],
                         reads=[wr, rg("cT", ti), rg("krT", ti), rg("selK")], writes=[pkr])
                    p.op("dve", CP(Kh[0:96, t0:t0 + nk_], pk_[0:96, 0:nk_]), reads=[pkr], writes=[rg("Kh", par, ti)] + kx)
                for g0 in (0, 8, 16):
                    ng = min(8, 17 - g0)
                    pv, pvr = nps()
                    fl = []
                    for i in range(ng):
                        kb = g0 + i
                        k0 = kb * 128
                        nk = min(128, NP - k0)
                        for k in range(2):
                            fl.append(MM(pv[0:nk, i * 64:(i + 1) * 64], cT[:, k, k0:k0 + nk], uvh[:, k, :], k == 0, k == 1))
                    p.op("pe", fl, reads=[wr] + [rg("cT", ti) for ti in range(5)], writes=[pvr])
                    nkk = 128 if g0 < 16 else 16
                    p.op("dve", CP(Vh[e_][0:nkk, g0:g0 + ng, vo:vo + 64], pv[0:nkk, 0:ng * 64].rearrange("p (i d) -> p i d", i=ng)),
                         reads=[pvr], writes=[rg("Vh", e_)])
                if h % 4 == 3 and h < 15:
                    rope(h // 4 + 1)

            def attn(h):
                par = h % 2
                e_ = par
                pair = h // 2
                vo = 64 * e_
                do = 64 - vo
                Qh = QhB[par]
                Kh = KhB[par]
                kx = [rg("io", 0), rg("io", 1)] if par == 1 else []
                steps = []
                for ti, (q0, n) in enumerate(TT):
                    nq = min(n, NP - q0)
                    nkb = (q0 + nq - 1) // 128 + 1
                    pO, pOr = aps()
                    for kb in range(nkb):
                        steps.append((ti, q0, nq, nkb, kb, pO, pOr))
                LOOK = 2
                pis = {}

                def emit_S(i):
                    ti, q0, nq, nkb, kb, pO, pOr = steps[i]
                    k0 = kb * 128
                    nk = min(128, NP - k0)
                    qs = max(q0, k0)
                    w = q0 + nq - qs
                    pS_, pSr_ = nps()
                    if k0 >= q0:
                        dw = min(128, w)
                        p.op("pe", [MM(pS_[0:nk, 0:w], Kh[0:96, k0:k0 + nk], Qh[0:96, qs:qs + w], True, False),
                                    MM(pS_[0:nk, 0:dw], ident_b[0:nk, 0:nk], negm[0:nk, 0:dw], False, True)],
                             reads=[rg("Kh", par, k0 // 512), rg("Qh", par, ti), rg("negm"), rg("identb")] + kx, writes=[pSr_])
                    else:
                        p.op("pe", MM(pS_[0:nk, 0:w], Kh[0:96, k0:k0 + nk], Qh[0:96, qs:qs + w], True, True),
                             reads=[rg("Kh", par, k0 // 512), rg("Qh", par, ti)] + kx, writes=[pSr_])
                    pi = att_ctr[0] % 3
                    att_ctr[0] += 1
                    pis[i] = pi
                    p.op("act", A_(PT[pi][0:nk, 0:w], pS_[0:nk, 0:w], AF.Exp, scale=SCALE), reads=[pSr_], writes=[rg("PT", pi)])

                def emit_PV(i):
                    ti, q0, nq, nkb, kb, pO, pOr = steps[i]
                    k0 = kb * 128
                    nk = min(128, NP - k0)
                    qs = max(q0, k0)
                    w = q0 + nq - qs
                    pi = pis[i]
                    p.op("pe", MM(pO[:, qs - q0:qs - q0 + w], Vh[e_][0:nk, kb, :], PT[pi][0:nk, 0:w], kb == 0, kb == nkb - 1),
                         reads=[rg("PT", pi), rg("Vh", e_)], writes=[pOr])
                    if kb == nkb - 1:
                        b = ti % 2
                        p.op("dve", lambda e, b=b, pO=pO, nq=nq, do=do: e.reciprocal(out=tA[b][do:do + 64, 0:nq], in_=pO[do:do + 64, 0:nq]),
                             reads=[pOr], writes=[rg("tA", b)])
                        p.op("dve", TTo(OT[vo:vo + 64, pair, q0:q0 + nq], pO[vo:vo + 64, 0:nq], tA[b][do:do + 64, 0:nq], ALU.mult),
                             reads=[pOr, rg("tA", b)], writes=[rg("hT", ti)])

                for i in range(len(steps)):
                    emit_S(i)
                    if i >= LOOK:
                        emit_PV(i - LOOK)
                for i in range(max(0, len(steps) - LOOK), len(steps)):
                    emit_PV(i)

            rope(0)
            gen(0)
            for h in range(16):
                if h + 1 < 16:
                    gen(h + 1)
                attn(h)
            if debug and j == 0:
                allr = list(p.regs.values())
                dd = lambda name, shape, dt=BF16: nc.dram_tensor(name, list(shape), dt, kind="ExternalOutput").ap()
                p.dma("sp", dd("dbg_OT", [128, 8 * NT]), R1[:, :], d_out, reads=allr)
                p.dma("sp", dd("dbg_R2", [128, 8 * 2098]), R2[:, :], d_out, reads=allr)
                p.dma("sp", dd("dbg_auxA", [128, 4352]), auxA[:, :], d_out, reads=allr)
                p.dma("sp", dd("dbg_auxB", [128, 2048], F32), auxB[:, :], d_out, reads=allr)
                p.dma("sp", dd("dbg_tri", [128, 128]), tri[:, :], d_out, reads=allr)
                p.dma("sp", dd("dbg_PT", [128, 512]), PT[0][:, :], d_out, reads=allr)
                p.dma("sp", dd("dbg_tA", [128, 512], F32), tA[0][:, :], d_out, reads=allr)
                p.dma("sp", dd("dbg_xT", [128, 8 * NT], F32), xT[:, :, :].rearrange("p c t -> p (c t)"), d_out, reads=allr)
                p.dma("sp", dd("dbg_C4", [128, NT]), C4[:, :], d_out, reads=allr)
                p.dma("sp", dd("dbg_S4", [128, NT]), S4[:, :], d_out, reads=allr)
            if stage >= 4:
                sample_attention(j)
            else:
                p.op("pool", lambda e: e.memset(OT[:, :, NP:NT], 0.0), writes=[rg("hT", 4)])
            for m in range(8):
                W, wr = ring_load(w_o[j, m], 1024)
                Wv = W[:, 0:1024].rearrange("p (k j) -> p k j", k=8)
                for ti, (t0, n) in enumerate(TT):
                    po, por = nps()
                    p.op("pe", [MM(po[:, 0:n], Wv[:, k, :], OT[:, k, t0:t0 + n], k == 0, k == 7) for k in range(8)],
                         reads=[wr, rg("hT", ti)], writes=[por])
                    p.op("dve", TTo(xT[:, m, t0:t0 + n], po[:, 0:n], xT[:, m, t0:t0 + n], ALU.add),
                         reads=[por], writes=[rg("xT", m, ti)])
            ffn(l)

        att_ctr = [0]

        ukT = auxA[0:64, 0:4096].rearrange("p (h c) -> p h c", h=16)
        uvp = auxB[:, :].bitcast(BF16).rearrange("p (k h j) -> p k h j", k=2, h=16)
        SA = R2[:, 0:6 * NT]
        CLb2 = [SA[:, i * 2048:(i + 1) * 2048] for i in range(4)]
        CLb = [c_.rearrange("p (r c) -> p r c", r=8) for c_ in CLb2]
        KRg = [iob[:, 3072 + i * 256:3072 + (i + 1) * 256] for i in range(4)]
        tcT = [SA[:, 8192 + i * 1536: 8192 + i * 1536 + 1024].rearrange("p (k t) -> p k t", k=2) for i in range(2)]
        tkr = [SA[:, 8192 + i * 1536 + 1024: 8192 + i * 1536 + 1152] for i in range(2)]
        QRz = SA[:, 8192 + 1152:8192 + 1152 + 256].rearrange("p (r b h) -> p r b h", r=4, b=4)
        o0 = 8192 + 3072
        QL = SA[:, o0:o0 + 128].rearrange("p (k b h) -> p k b h", k=2, b=4)
        QR = SA[0:32, o0 + 128:o0 + 192].rearrange("p (b h) -> p b h", b=4)
        OLT = SA[:, o0 + 192:o0 + 320].rearrange("p (k b h) -> p k b h", k=2, b=4)
        PTs = [SA[:, o0 + 320 + i * 64:o0 + 384 + i * 64] for i in range(2)]
        PTn = SA[0:1, o0 + 448:o0 + 464]
        crow = SA[0:1, o0 + 464:o0 + 720]
        OL = SA[0:16, o0 + 720:o0 + 976]
        rd = cols[0:16, 5:6]
        d_gl = [p.new_sem("d_gl%d" % i) for i in range(4)]
        d_gk = [p.new_sem("d_gk%d" % i) for i in range(4)]

        def sample_attention(j):
            sar = rg("sa_small")
            p.dma("pool", auxA[0:64, 0:4096], w_ukT, d_misc, writes=[rg("auxA"), rg("diag", 0)] + [rg("Rq", ti) for ti in range(5)] + [rg("Vh", 0), rg("Vh", 1)])
            p.dma("pool", auxB[:, :].bitcast(BF16), w_uvp, d_misc2, writes=[rg("auxB"), rg("diag", 1), rg("clf", 0), rg("clf", 1), rg("krf", 0), rg("krf", 1)] + [rg("ybuf", c) for c in range(8)])
            sa_regs = ([rg("cqn", ti) for ti in range(5)] + [rg("Qh", pp, ti) for ti in range(5) for pp in range(2)]
                       + [rg("Kh", pp, ti) for ti in range(5) for pp in range(2)] + [rg("io", 1)])
            pq1, pq1r = nps()
            fl = []
            for h in range(16):
                for kc in range(2):
                    c0 = (kc * 16 + h) * 4
                    fl.append(MM(pq1[:, c0:c0 + 4], ukT[:, h, kc * 128:(kc + 1) * 128], Qs[0:64, h, :], True, True))
            p.op("pe", fl, reads=[rg("auxA"), rg("Qs")], writes=[pq1r])
            p.op("act", A_(QL.rearrange("p k b h -> p k h b"), pq1[:, 0:128].rearrange("p (k h b) -> p k h b", k=2, h=16), AF.Copy),
                 reads=[pq1r], writes=[sar] + sa_regs)
            pq2, pq2r = nps()
            p.op("pe", [MM(pq2[0:32, h * 4:h * 4 + 4], ident_b[0:96, 64:96], Qs[0:96, h, :], True, True) for h in range(16)],
                 reads=[rg("identb"), rg("Qs")], writes=[pq2r])
            p.op("pool", lambda e: e.memset(QRz[:, :, :, :], 0.0), reads=[pq1r], writes=[sar] + sa_regs)
            for rr in range(4):
                p.op("dve", CP(QRz[32 * rr:32 * rr + 32, rr, :, :].rearrange("p b h -> p h b"),
                               pq2[0:32, 0:64].rearrange("p (h b) -> p h b", h=16)), reads=[pq2r], writes=[sar])
            steps = [(b4, g, r4) for b4 in range(4) for g in range(16) for r4 in range(2)]
            acc = {}
            st_ = {}

            def finalize(b4):
                pO, pOr = acc[b4]
                pD = PS[7][0:16, b4:b4 + 1]
                col = NP + b4
                pS_, pSr_ = nps()
                p.op("pe", [MM(pS_[0:1, 0:16], cT[:, 0, col:col + 1], QL[:, 0, b4, :], True, False),
                            MM(pS_[0:1, 0:16], cT[:, 1, col:col + 1], QL[:, 1, b4, :], False, False),
                            MM(pS_[0:1, 0:16], krT[:, col:col + 1], QRz[0:32, 0, b4, :], False, True)],
                     reads=[rg("cT", 4), rg("krT", 4), sar], writes=[pSr_])
                p.op("act", A_(PTn, pS_[0:1, 0:16], AF.Exp, scale=SCALE), reads=[pSr_], writes=[rg("PTn")])
                pC, pCr = nps()
                pCb = pC[:, :].bitcast(BF16)
                p.op("pe", [TR(pCb[0:1, kc * 128:(kc + 1) * 128], cT[:, kc, col:col + 1], ident_b[:, :]) for kc in range(2)],
                     reads=[rg("cT", 4), rg("identb")], writes=[pCr])
                p.op("dve", CP(crow, pCb[0:1, 0:256]), reads=[pCr], writes=[rg("crow")])
                p.op("pe", [MM(pO[0:16, 0:256], PTn, crow, False, True), MM(pD, PTn, ones_b[0:1, 0:1], False, True)],
                     reads=[rg("PTn"), rg("crow"), rg("ones")], writes=[pOr, psr[7]])
                p.op("dve", lambda e, pD=pD: e.reciprocal(out=rd, in_=pD), reads=[pOr, psr[7]], writes=[rg("rd")])
                p.op("dve", TS(OL, pO[0:16, 0:256], rd, None, ALU.mult), reads=[pOr, rg("rd")], writes=[rg("OL")])
                pT2, pT2r = nps()
                pT2b = pT2[:, :].bitcast(BF16)
                p.op("pe", [TR(pT2b[:, kc * 16:(kc + 1) * 16], OL[:, kc * 128:(kc + 1) * 128], ident_b[0:16, 0:16]) for kc in range(2)],
                     reads=[rg("OL"), rg("identb")], writes=[pT2r])
                p.op("act", A_(OLT[:, :, b4, :], pT2b[:, 0:32].rearrange("p (k h) -> p k h", k=2), AF.Copy), reads=[pT2r], writes=[rg("OLT")])

            def emit_T(i):
                b4, g, r4 = steps[i]
                gi = b4 * 16 + g
                cb = gi % 4
                if r4 == 0:
                    if g == 0:
                        acc[b4] = (PS[6], psr[6])
                    p.raw("pool", lambda e, b4=b4, g=g, cb=cb: e.indirect_dma_start(
                        out=CLb2[cb], out_offset=None, in_=cache_lat[:, :],
                        in_offset=bass.IndirectOffsetOnAxis(ap=idxg[:, b4, g:g + 1], axis=0)),
                        d_gl[cb], reads=[rg("idxg")], writes=[rg("CLb", cb)])
                    p.raw("pool", lambda e, b4=b4, g=g, cb=cb: e.indirect_dma_start(
                        out=KRg[cb], out_offset=None, in_=cache_kr[:, :],
                        in_offset=bass.IndirectOffsetOnAxis(ap=idxg[:, b4, g:g + 1], axis=0)),
                        d_gk[cb], reads=[rg("idxg")], writes=[rg("KRg", cb)])
                tb = i % 2
                pA, pAr = nps()
                pB, pBr = nps()
                pAb = pA[:, :].bitcast(BF16).rearrange("p (k r t) -> p k r t", k=2, r=4)
                pBb = pB[:, :].bitcast(BF16)
                fl = []
                for rr in range(4):
                    r = r4 * 4 + rr
                    for kc in range(2):
                        fl.append(TR(pAb[:, kc, rr, :], CLb[cb][:, r, kc * 128:(kc + 1) * 128], ident_b[:, :]))
                fl.append(TR(pBb[:, 0:128], KRg[cb][:, r4 * 128:(r4 + 1) * 128], ident_b[:, :]))
                p.op("pe", fl, reads=[rg("CLb", cb), rg("KRg", cb), rg("identb")], writes=[pAr, pBr])
                p.op("act", A_(tcT[tb], pAb.rearrange("p k r t -> p k (r t)"), AF.Copy), reads=[pAr], writes=[rg("tcT", tb)])
                p.op("dve", CP(tkr[tb], pBb[:, 0:128]), reads=[pBr], writes=[rg("tkr", tb)])

            def emit_S(i):
                b4, g, r4 = steps[i]
                tb = i % 2
                pS_, pSr_ = nps()
                fl = []
                for rr in range(4):
                    o_ = pS_[:, rr * 16:(rr + 1) * 16]
                    fl.append(MM(o_, tcT[tb][:, 0, rr * 128:(rr + 1) * 128], QL[:, 0, b4, :], True, False))
                    fl.append(MM(o_, tcT[tb][:, 1, rr * 128:(rr + 1) * 128], QL[:, 1, b4, :], False, False))
                    fl.append(MM(o_, tkr[tb][:, :], QRz[:, rr, b4, :], False, True))
                p.op("pe", fl, reads=[rg("tcT", tb), rg("tkr", tb), sar], writes=[pSr_])
                p.op("act", A_(PTs[tb], pS_[:, 0:64], AF.Exp, scale=SCALE), reads=[pSr_], writes=[rg("PTs", tb)])

            def emit_V(i):
                b4, g, r4 = steps[i]
                gi = b4 * 16 + g
                cb = gi % 4
                tb = i % 2
                pO, pOr = acc[b4]
                pD = PS[7][0:16, b4:b4 + 1]
                fl = []
                for rr in range(4):
                    r = r4 * 4 + rr
                    first = (g == 0 and r4 == 0 and rr == 0)
                    fl.append(MM(pO[0:16, 0:256], PTs[tb][:, rr * 16:(rr + 1) * 16], CLb[cb][:, r, :], first, False))
                    fl.append(MM(pD, PTs[tb][:, rr * 16:(rr + 1) * 16], ones_b[:, 0:1], first, False))
                p.op("pe", fl, reads=[rg("PTs", tb), rg("CLb", cb), rg("ones")], writes=[pOr, psr[7]])
                if g == 15 and r4 == 1:
                    finalize(b4)

            n_ = len(steps)
            for i in range(n_ + 2):
                if i < n_:
                    emit_T(i)
                if 1 <= i <= n_:
                    emit_S(i - 1)
                if i >= 2:
                    emit_V(i - 2)
            for pair in range(8):
                po, por = nps()
                fl = []
                i = 0
                for e2 in range(2):
                    for kc in range(2):
                        fl.append(MM(po[:, 0:4], uvp[:, kc, 2 * pair + e2, :], OLT[:, kc, :, 2 * pair + e2], i == 0, i == 3))
                        i += 1
                p.op("pe", fl, reads=[rg("auxB"), rg("OLT")], writes=[por])
                p.op("act", A_(OT[:, pair, NP:NT], po[:, 0:4], AF.Copy), reads=[por], writes=[rg("hT", 4)])

        sa_ctr = [0]

        if stage >= 3:
            for j in range(2):
                b_layer(j)

        R1f = R1[:, :].bitcast(F32)
        yos = [R1f[:, i * 4096:(i + 1) * 4096].rearrange("p (c t) -> p c t", c=8) for i in range(2)]

        def fin_norm(ti):
            t0, n = TT[ti]
            yo = yos[ti % 2]
            srcs = [xT[:, c, t0:t0 + n] for c in range(8)]
            t, tr_ = rstd_from(srcs, [rg("xT", c, ti) for c in range(8)], n, 1024.0, sq_eng="pool")
            for c in range(8):
                eng = "dve"
                p.op(eng, STT(yo[:, c, 0:n], xT[:, c, t0:t0 + n], V("fin_g", c), t[:, 0:n], ALU.mult, ALU.mult),
                     reads=[rg("xT", c, ti), tr_, rg("vecs")] + ([] if c == 0 else [rg("yo", ti % 2)]),
                     writes=([rg("yo", ti % 2)] + [r_ for k in range(5) for r_ in HR(k)]) if c == 0 else [rg("yoc", ti % 2, c)])

        def fin_out(ti):
            t0, n = TT[ti]
            yo = yos[ti % 2]
            for bi in range((n + 127) // 128):
                c0 = bi * 128
                nb = min(128, n - c0)
                b = bi % 2
                for half in range(2):
                    pt_, ptr_ = nps()
                    p.op("pe", [TR(pt_[0:nb, jj * 128:(jj + 1) * 128], yo[:, half * 4 + jj, c0:c0 + nb], ident_f[:, :]) for jj in range(4)],
                         reads=[rg("yo", ti % 2), rg("ident")] + [rg("yoc", ti % 2, c) for c in range(1, 8)], writes=[ptr_])
                    if half == 0:
                        p.op("act", A_(io[b][0:nb, 0:512], pt_[0:nb, :], AF.Copy), reads=[ptr_], writes=[rg("io", b)])
                    else:
                        p.op("dve", CP(io[b][0:nb, 512:1024], pt_[0:nb, :]), reads=[ptr_], writes=[rg("io", b)])
                p.dma("sp", y_all[t0 + c0:t0 + c0 + nb, :], io[b][0:nb, :], d_o[b], reads=[rg("io", b)])
        fin_norm(0)
        for ti in range(5):
            if ti + 1 < 5:
                fin_norm(ti + 1)
            fin_out(ti)
        for dd_ in (d_o[0], d_o[1], d_out):
            if dd_.count > 0:
                p.q["sp"].append(lambda e, dd_=dd_: e.wait_ge(dd_.h, dd_.count))
        p.emit()
    return nc


def _cols(v):
    return np.ascontiguousarray(np.asarray(v, np.float32).reshape(-1, 128).T)


def _prep_shared(inp):
    f = lambda a: np.ascontiguousarray(np.asarray(a, dtype=np.float32))
    d = {}
    pw1 = f(inp["a_pw1_w"]).reshape(2, 8, 128, 2, 8, 128)
    d["w_pw1"] = f(pw1.transpose(0, 4, 2, 3, 1, 5).reshape(2, 8, 128, 2048))
    pw2 = f(inp["a_pw2_w"]).reshape(2, 8, 128, 8, 128)
    d["w_pw2"] = f(pw2.transpose(0, 3, 2, 1, 4).reshape(2, 8, 128, 1024))
    g = f(inp["ffn_w_gate"]).reshape(4, 8, 128, 22, 128)
    u = f(inp["ffn_w_up"]).reshape(4, 8, 128, 22, 128)
    gu = np.stack([g, u], axis=0)
    d["w_gu"] = f(gu.transpose(1, 4, 3, 0, 2, 5).reshape(4, 22, 128, 2048))
    dn = f(inp["ffn_w_down"]).reshape(4, 22, 128, 8, 128)
    d["w_dn"] = f(dn.transpose(0, 3, 2, 1, 4).reshape(4, 8, 128, 2816))
    dq = f(inp["b_w_dq"]).reshape(2, 8, 128, 3, 128)
    d["w_dq"] = f(dq.transpose(0, 3, 2, 1, 4).reshape(2, 3, 128, 1024))
    uq = f(inp["b_w_uq"]).reshape(2, 3, 128, 16, 96)
    rope = uq[..., 64:96]
    swp = np.concatenate([uq[..., 80:96], uq[..., 64:80]], axis=-1)
    rs = np.stack([rope, swp], axis=0).reshape(2, 2, 3, 128, 4, 4 * 32)
    d["w_rs"] = f(rs.transpose(1, 4, 3, 2, 0, 5).reshape(2, 4, 128, 768))
    nope = np.zeros((2, 3, 128, 16, 96), np.float32)
    nope[..., 0:64] = uq[..., 0:64]
    uk = f(inp["w_uk"]).reshape(2, 128, 16, 64)
    ukp = np.zeros((2, 128, 16, 96), np.float32)
    ukp[..., 0:64] = uk
    uv = f(inp["w_uv"]).reshape(2, 128, 16, 64)
    hd = np.zeros((2, 16, 128, 608), np.float32)
    for j in range(2):
        hd[j, :, :, 0:288] = nope[j].transpose(2, 1, 0, 3).reshape(16, 128, 288)
        hd[j, :, :, 288:480] = ukp.transpose(2, 1, 0, 3).reshape(16, 128, 192)
        hd[j, :, :, 480:608] = uv.transpose(2, 1, 0, 3).reshape(16, 128, 128)
    d["w_hd"] = hd
    wo = f(inp["b_w_o"]).reshape(2, 8, 128, 8, 128)
    d["w_o"] = f(wo.transpose(0, 3, 2, 1, 4).reshape(2, 8, 128, 1024))
    dkv = f(inp["w_dkv"]).reshape(8, 128, 288)
    d["w_dkvl"] = f(dkv[..., 0:256].transpose(1, 0, 2).reshape(128, 2048))
    rsw = np.concatenate([dkv[..., 256:288], dkv[..., 272:288], dkv[..., 256:272]], axis=-1)
    d["w_dkvr"] = f(rsw.transpose(1, 0, 2).reshape(128, 512))
    ukf = f(inp["w_uk"])
    d["w_ukT"] = f(ukf.transpose(2, 1, 0).reshape(64, 4096))
    uvp = np.zeros((128, 2, 16, 128), np.float32)
    for h in range(16):
        uvp[:, :, h, (h % 2) * 64:(h % 2) * 64 + 64] = uv[:, :, h, :].transpose(1, 0, 2)
    d["w_uvp"] = f(uvp.reshape(128, 4096))
    vs = []
    for l in range(2):
        pass
    order = [("a_norm_g", 0), ("a_norm_g", 1), ("a_pw1_b", 0), ("a_pw1_b", 1), ("a_dw_b", 0), ("a_dw_b", 1),
             ("a_ln_g", 0), ("a_ln_g", 1), ("a_ln_b", 0), ("a_ln_b", 1), ("a_pw2_b", 0), ("a_pw2_b", 1),
             ("ffn_norm_g", 0), ("ffn_norm_g", 1), ("ffn_norm_g", 2), ("ffn_norm_g", 3)]
    for nm, i in order:
        vs.append(_cols(f(inp[nm])[i]))
    vs.append(_cols(inp["kv_norm_g"]))
    vs.append(_cols(inp["kv_latent_norm_g"]))
    vs.append(_cols(f(inp["b_norm_g"])[0]))
    vs.append(_cols(f(inp["b_norm_g"])[1]))
    vs.append(_cols(f(inp["b_q_norm_g"])[0]))
    vs.append(_cols(f(inp["b_q_norm_g"])[1]))
    vs.append(_cols(inp["final_norm_g"]))
    d["vecs"] = f(np.concatenate(vs, axis=1))
    assert d["vecs"].shape == (128, NV)
    dw = f(inp["a_dw_w"]).reshape(2, 31, 8, 128)
    d["dww"] = f(dw.transpose(3, 0, 2, 1).reshape(128, 2 * 8 * 31))
    d["cache_lat"] = f(inp["cache_latent"]).reshape(5120 * 16, 2048)
    d["cache_kr"] = f(inp["cache_krope"]).reshape(5120 * 16, 256)
    return d


_NC_CACHE = {}


def kernel(**inputs):
    shared = _prep_shared(inputs)
    xp = np.asarray(inputs["x_prompt"], np.float32)
    xs = np.asarray(inputs["x_sample"], np.float32)
    meta = np.asarray(inputs["meta_tokens"], np.float32)
    stc = np.asarray(inputs["state_conv"], np.float32)
    pt = np.asarray(inputs["page_table"], np.int32)
    in_maps = []
    for i in range(8):
        m = dict(shared)
        m["xin"] = np.ascontiguousarray(np.concatenate([meta, xp[i], xs[4 * i:4 * i + 4, 0]], axis=0))
        m["state"] = np.ascontiguousarray(stc[:, 4 * i:4 * i + 4].transpose(0, 2, 1, 3))
        m["ptT"] = np.ascontiguousarray(pt[4 * i:4 * i + 4].T)
        in_maps.append(m)
    if "nc" not in _NC_CACHE:
        _NC_CACHE["nc"] = build_program()
    res = run_bass_kernel_spmd(_NC_CACHE["nc"], in_maps, core_ids=list(range(8)))
    R = res.results
    y_prompt = np.stack([R[i]["y_all"][16:NP] for i in range(8)], 0)
    y_sample = np.concatenate([R[i]["y_all"][NP:NT] for i in range(8)], 0)[:, None, :]
    lat_p = np.stack([R[i]["lat_all"][0:NP] for i in range(8)], 0)
    kr_p = np.stack([R[i]["kr_all"][0:NP] for i in range(8)], 0)
    conv_p = np.stack([R[i]["conv_p"] for i in range(8)], 1)
    lat_s = np.concatenate([R[i]["lat_all"][NP:NT] for i in range(8)], 0)[:, None, :]
    kr_s = np.concatenate([R[i]["kr_all"][NP:NT] for i in range(8)], 0)[:, None, :]
    conv_s = np.concatenate([R[i]["conv_s"] for i in range(8)], 1)
    outs = (y_prompt, y_sample, lat_p, kr_p, conv_p, lat_s, kr_s, conv_s)
    return tuple(np.ascontiguousarray(o, dtype=np.float32) for o in outs)
```

```python
import math
import numpy as np
from contextlib import ExitStack
import concourse.bass as bass
import concourse.mybir as mybir
from concourse.bass_utils import run_bass_kernel_spmd

F32 = mybir.dt.float32
BF16 = mybir.dt.bfloat16
I32 = mybir.dt.int32
ALU = mybir.AluOpType
AF = mybir.ActivationFunctionType
AX = mybir.AxisListType

NT = 2068
NP = 2064
TT = [(0, 512), (512, 512), (1024, 512), (1536, 512), (2048, 20)]
CTL = [(i * 256, 256) for i in range(8)] + [(2048, 20)]
HGS = [(0, 6), (6, 6), (12, 5), (17, 5)]
EPS = 1e-6
SCALE = 1.0 / math.sqrt(96.0)
TWO_PI = 2.0 * math.pi

VOFF = {}
_o = 0
for _name, _n in [("a_norm_g0", 8), ("a_norm_g1", 8), ("a_pw1_b0", 16), ("a_pw1_b1", 16), ("a_dw_b0", 8), ("a_dw_b1", 8),
                  ("a_ln_g0", 8), ("a_ln_g1", 8), ("a_ln_b0", 8), ("a_ln_b1", 8), ("a_pw2_b0", 8), ("a_pw2_b1", 8),
                  ("ffn_g0", 8), ("ffn_g1", 8), ("ffn_g2", 8), ("ffn_g3", 8), ("kv_g", 8), ("kvl_g", 2),
                  ("b_g0", 8), ("b_g1", 8), ("bq_g0", 3), ("bq_g1", 3), ("fin_g", 8)]:
    VOFF[_name] = _o
    _o += _n
NV = _o


class Sem:
    def __init__(self, h, name):
        self.h = h
        self.name = name
        self.count = 0


class Reg:
    __slots__ = ("name", "lw", "rs")

    def __init__(self, name):
        self.name = name
        self.lw = None
        self.rs = []


class Prog:
    ENG = ("pe", "act", "dve", "pool", "sp")

    def __init__(self, nc, stack):
        self.nc = nc
        self.stack = stack
        self.q = {e: [] for e in self.ENG}
        self.sem = {e: self.new_sem("e_" + e) for e in self.ENG}
        self.waited = {e: {} for e in self.ENG}
        self.regs = {}

    def new_sem(self, name):
        return Sem(self.stack.enter_context(self.nc.semaphore(name)), name)

    def sb(self, name, shape, dt):
        return self.stack.enter_context(self.nc.sbuf_tensor("sb_" + name, list(shape), dt))

    def psum(self, name, shape, dt):
        return self.stack.enter_context(self.nc.psum_tensor(name, list(shape), dt))

    def rg(self, *key):
        r = self.regs.get(key)
        if r is None:
            r = Reg(str(key))
            self.regs[key] = r
        return r

    def _waits(self, eng, reads, writes):
        deps = {}

        def add(d):
            if d is None:
                return
            s, v = d
            if deps.get(s, 0) < v:
                deps[s] = v
        for r in reads:
            add(r.lw)
        for w in writes:
            add(w.lw)
            for d in w.rs:
                add(d)
        out = []
        wd = self.waited[eng]
        for s, v in deps.items():
            if eng == "pe" and s is self.sem["pe"]:
                continue
            if wd.get(s, 0) >= v:
                continue
            wd[s] = v
            out.append((s, v))
        return out

    def _commit(self, tok, reads, writes):
        for r in reads:
            r.rs.append(tok)
            if len(r.rs) > 48:
                m = {}
                for s, v in r.rs:
                    if m.get(s, 0) < v:
                        m[s] = v
                r.rs = list(m.items())
        for w in writes:
            w.lw = tok
            w.rs = []

    def op(self, eng, fns, reads=(), writes=()):
        if callable(fns):
            fns = [fns]
        waits = self._waits(eng, reads, writes)
        s = self.sem[eng]
        s.count += 1
        val = s.count
        q = self.q[eng]
        for ws, wv in waits:
            q.append(lambda e, ws=ws, wv=wv: e.wait_ge(ws.h, wv))
        for f in fns[:-1]:
            q.append(f)
        last = fns[-1]
        q.append(lambda e, last=last, s=s: last(e).then_inc(s.h, 1))
        self._commit((s, val), reads, writes)

    def dma(self, queue, out, in_, dsem, reads=(), writes=(), **kw):
        self.raw(queue, lambda e, out=out, in_=in_, kw=kw: e.dma_start(out=out, in_=in_, **kw), dsem, reads, writes)

    def raw(self, queue, fn, dsem, reads=(), writes=()):
        waits = self._waits(queue, reads, writes)
        dsem.count += 16
        val = dsem.count
        q = self.q[queue]
        for ws, wv in waits:
            q.append(lambda e, ws=ws, wv=wv: e.wait_ge(ws.h, wv))
        q.append(lambda e, fn=fn, dsem=dsem: fn(e).then_inc(dsem.h, 16))
        self._commit((dsem, val), reads, writes)

    def final_wait(self, eng, regs):
        for ws, wv in self._waits(eng, regs, ()):
            self.q[eng].append(lambda e, ws=ws, wv=wv: e.wait_ge(ws.h, wv))

    def emit(self):
        with self.nc.Block() as block:
            @block.tensor
            def _(e):
                for f in self.q["pe"]:
                    f(e)

            @block.scalar
            def _(e):
                for f in self.q["act"]:
                    f(e)

            @block.vector
            def _(e):
                for f in self.q["dve"]:
                    f(e)

            @block.gpsimd
            def _(e):
                for f in self.q["pool"]:
                    f(e)

            @block.sync
            def _(e):
                for f in self.q["sp"]:
                    f(e)


def A_(out, in_, func, **kw):
    return lambda e: e.activation(out=out, in_=in_, func=func, **kw)


def TTo(out, a, b, op):
    return lambda e: e.tensor_tensor(out=out, in0=a, in1=b, op=op)


def STT(out, a, s, b, op0, op1):
    return lambda e: e.scalar_tensor_tensor(out=out, in0=a, scalar=s, in1=b, op0=op0, op1=op1)


def TS(out, a, s1, s2, op0, op1=None):
    if op1 is None:
        return lambda e: e.tensor_scalar(out=out, in0=a, scalar1=s1, scalar2=None, op0=op0)
    return lambda e: e.tensor_scalar(out=out, in0=a, scalar1=s1, scalar2=s2, op0=op0, op1=op1)


def CP(out, in_):
    return lambda e: e.tensor_copy(out=out, in_=in_)


def MM(out, l, r, st, sp):
    return lambda e: e.matmul(out, lhsT=l, rhs=r, start=st, stop=sp)


def TR(out, in_, ident):
    return lambda e: e.transpose(out, in_, ident)


def build_program(stage=99, debug=False):
    nc = bass.Bass("TRN2", target_bir_lowering=False)

    def din(name, shape, dt=F32):
        return nc.dram_tensor(name, list(shape), dt, kind="ExternalInput").ap()

    def dout(name, shape):
        return nc.dram_tensor(name, list(shape), F32, kind="ExternalOutput").ap()

    xin = din("xin", [NT, 1024])
    state = din("state", [2, 30, 4, 1024])
    ptT = din("ptT", [128, 4], I32)
    cache_lat = din("cache_lat", [5120 * 16, 2048])
    cache_kr = din("cache_kr", [5120 * 16, 256])
    w_pw1 = din("w_pw1", [2, 8, 128, 2048])
    w_pw2 = din("w_pw2", [2, 8, 128, 1024])
    w_gu = din("w_gu", [4, 22, 128, 2048])
    w_dn = din("w_dn", [4, 8, 128, 2816])
    w_dq = din("w_dq", [2, 3, 128, 1024])
    w_rs = din("w_rs", [2, 4, 128, 768])
    w_hd = din("w_hd", [2, 16, 128, 608])
    w_o = din("w_o", [2, 8, 128, 1024])
    w_dkvl = din("w_dkvl", [128, 2048])
    w_dkvr = din("w_dkvr", [128, 512])
    w_ukT = din("w_ukT", [64, 4096])
    w_uvp = din("w_uvp", [128, 4096])
    vecs_d = din("vecs", [128, NV])
    dww_d = din("dww", [128, 2 * 8 * 31])
    y_all = dout("y_all", [NT, 1024])
    lat_all = dout("lat_all", [NT, 256])
    kr_all = dout("kr_all", [NT, 32])
    conv_p = dout("conv_p", [2, 30, 1024])
    conv_s = dout("conv_s", [2, 4, 30, 1024])

    with ExitStack() as st:
        p = Prog(nc, st)
        rg = p.rg
        xT = p.sb("xT", [128, 8, NT], F32)
        R1 = p.sb("R1", [128, 8 * NT], BF16)
        R2 = p.sb("R2", [128, 8 * 2098], BF16)
        ringt = p.sb("ring", [128, 4, 2048], BF16)
        auxA = p.sb("auxA", [128, 4352], BF16)
        auxB = p.sb("auxB", [128, 2048], F32)
        C4 = p.sb("C4", [128, NT], BF16)
        S4 = p.sb("S4", [128, NT], BF16)
        krT = p.sb("krT", [32, NT], BF16)
        ident_f = p.sb("ident_f", [128, 128], F32)
        ident_b = p.sb("ident_b", [128, 128], BF16)
        ones_b = p.sb("ones_b", [128, 128], BF16)
        tri = p.sb("tri", [128, 128], BF16)
        negm = p.sb("negm", [128, 128], BF16)
        selK = p.sb("selK", [32, 96], BF16)
        selQ = p.sb("selQ", [128, 4, 96], BF16)
        vecs = p.sb("vecs", [128, NV], F32)
        dww = p.sb("dww", [128, 2, 8, 31], F32)
        sqb = [p.sb("sqb%d" % i, [128, 512], BF16) for i in range(2)]
        tA = [p.sb("tA%d" % i, [128, 512], F32) for i in range(2)]
        tB = [p.sb("tB%d" % i, [128, 512], F32) for i in range(2)]
        PT = [p.sb("PT%d" % i, [128, 512], BF16) for i in range(4)]
        iot = p.sb("iot", [128, 2, 1024], F32)
        io = [iot[:, 0, :], iot[:, 1, :]]
        iob = iot[:, :, :].rearrange("p a b -> p (a b)").bitcast(BF16)
        g_tail = p.sb("g_tail", [128, 8, 34], F32)
        bufT = p.sb("bufT", [128, 8, 4, 31], F32)
        PT.append(bufT[:, :, :, :].rearrange("p a b c -> p (a b c)").bitcast(BF16)[:, 0:512])
        cols = p.sb("cols", [128, 8], F32)
        icol = p.sb("icol", [128, 4], I32)
        idx = p.sb("idx", [128, 4], I32)
        idxg = p.sb("idxg", [128, 4, 16], I32)
        Qs = p.sb("Qs", [96, 16, 4], BF16)
        PS = [p.psum("ps%d" % i, [128, 512], F32) for i in range(8)]
        psr = [rg("ps", i) for i in range(8)]
        ps_ctr = [0]

        def nps():
            i = ps_ctr[0] % 6
            ps_ctr[0] += 1
            return PS[i], psr[i]

        aps_ctr = [0]

        def aps():
            i = 6 + aps_ctr[0] % 2
            aps_ctr[0] += 1
            return PS[i], psr[i]

        d_const = p.new_sem("d_const")
        d_const2 = p.new_sem("d_const2")
        d_const3 = p.new_sem("d_const3")
        d_misc2 = p.new_sem("d_misc2")
        d_o = [p.new_sem("d_o%d" % i) for i in range(2)]
        d_io = [p.new_sem("d_io%d" % i) for i in range(2)]
        d_out = p.new_sem("d_out")
        d_ring = [p.new_sem("d_ring%d" % i) for i in range(4)]
        d_misc = p.new_sem("d_misc")
        out_reg = rg("out")

        hT = R1[:, :].rearrange("p (c t) -> p c t", c=8)
        gpad = R2[:, :].rearrange("p (c t) -> p c t", c=8)
        hid = R2[:, 0:6 * NT].rearrange("p (c t) -> p c t", c=6)
        cqn = R2[:, 0:3 * NT].rearrange("p (c t) -> p c t", c=3)
        Qh = R2[:, 3 * NT:4 * NT]
        Kh = R2[:, 4 * NT:5 * NT]
        cT = R2[:, 6 * NT:8 * NT].rearrange("p (c t) -> p c t", c=2)
        diagB = [auxA[:, 0:31 * 128].rearrange("p (k j) -> p k j", k=31),
                 auxB[:, :].bitcast(BF16)[:, 0:31 * 128].rearrange("p (k j) -> p k j", k=31)]
        Rq = auxA[:, 0:NT]
        ybuf = auxB[:, :].rearrange("p (c t) -> p c t", c=8)
        ring_ctr = [0]

        def ring_load(src_ap, n):
            i = ring_ctr[0] % 4
            ring_ctr[0] += 1
            r = rg("ring", i)
            np_ = src_ap.shape[0]
            dst = ringt[0:np_, i, 0:n]
            p.dma("pool", dst, src_ap, d_ring[i], writes=[r])
            return ringt[:, i, :], r

        def V(name, c=0):
            o = VOFF[name] + c
            return vecs[:, o:o + 1]

        p.dma("sp", vecs[:, :], vecs_d, d_const, writes=[rg("vecs")])
        p.dma("sp", dww[:, :, :, :], dww_d.rearrange("p (l c k) -> p l c k", l=2, c=8), d_const2, writes=[rg("dww")])
        p.dma("sp", idx[:, :], ptT, d_const3, writes=[rg("idx")])
        for g in range(16):
            p.op("dve", TS(idxg[:, :, g], idx[:, :], 16, g, ALU.mult, ALU.add), reads=[rg("idx")], writes=[rg("idxg")])
        R2f = R2[:, :].bitcast(F32)
        R2i = R2[:, :].bitcast(I32)
        iota_row = R2f[:, 0:128]
        scr = rg("scr")
        p.op("pool", lambda e: e.iota(iota_row, pattern=[[1, 128]], base=0, channel_multiplier=0,
                                      allow_small_or_imprecise_dtypes=True), writes=[scr])
        p.op("pool", lambda e: e.iota(cols[:, 0:1], pattern=[[0, 1]], base=0, channel_multiplier=1,
                                      allow_small_or_imprecise_dtypes=True), writes=[rg("cols")])
        p.op("pool", lambda e: e.iota(icol[:, 0:1], pattern=[[0, 1]], base=0, channel_multiplier=1), writes=[rg("icol")])
        p.op("dve", TS(ident_f[:, :], iota_row, cols[:, 0:1], None, ALU.is_equal), reads=[scr, rg("cols")], writes=[rg("ident")])
        p.op("dve", CP(ident_b[:, :], ident_f[:, :]), reads=[rg("ident")], writes=[rg("identb")])
        p.op("dve", TS(tri[:, :], iota_row, cols[:, 0:1], None, ALU.is_ge), reads=[scr, rg("cols")], writes=[rg("tri")])
        p.op("dve", TS(negm[:, :], tri[:, :], 30000.0, -30000.0, ALU.mult, ALU.add), reads=[rg("tri")], writes=[rg("negm")])
        p.op("pool", lambda e: e.memset(ones_b[:, :], 1.0), writes=[rg("ones")])
        p.op("pool", lambda e: e.memset(selK[:, :], 0.0), writes=[rg("selK")])
        p.op("pool", lambda e: e.memset(selQ[:, :, :], 0.0), writes=[rg("selQ")])
        p.op("dve", CP(selK[:, 64:96], ident_b[0:32, 0:32]), reads=[rg("identb")], writes=[rg("selK")])
        for i in range(4):
            p.op("dve", CP(selQ[:, i, 64:96], ident_b[:, 32 * i:32 * i + 32]), reads=[rg("identb")], writes=[rg("selQ")])
        p.op("dve", lambda e: e.tensor_single_scalar(out=icol[:, 1:2], in_=icol[:, 0:1], scalar=15, op=ALU.bitwise_and),
             reads=[rg("icol")], writes=[rg("icol")])
        p.op("dve", lambda e: e.tensor_single_scalar(out=icol[:, 2:3], in_=icol[:, 0:1], scalar=16, op=ALU.bitwise_and),
             reads=[rg("icol")], writes=[rg("icol")])
        p.op("dve", CP(cols[:, 1:3], icol[:, 1:3]), reads=[rg("icol")], writes=[rg("cols")])
        p.op("act", A_(cols[:, 1:2], cols[:, 1:2], AF.Exp, scale=-math.log(10000.0) / 16.0), reads=[rg("cols")], writes=[rg("cols")])
        p.op("dve", TS(cols[:, 2:3], cols[:, 2:3], 1.0 / 8.0, -1.0, ALU.mult, ALU.add), reads=[rg("cols")], writes=[rg("cols")])
        p.op("pool", lambda e: e.memset(cols[:, 3:4], EPS), writes=[rg("cols")])
        p.op("pool", lambda e: e.memset(cols[:, 4:5], 0.0), writes=[rg("cols")])
        pos = R2f[:, 0:NT]
        ang = R2f[:, NT:2 * NT]
        kf = R2f[:, 2 * NT:3 * NT]
        ki = R2i[:, 3 * NT:4 * NT]
        p.op("pool", lambda e: e.iota(pos, pattern=[[1, NT]], base=0, channel_multiplier=0,
                                      allow_small_or_imprecise_dtypes=True), reads=[rg("ident"), rg("tri")], writes=[scr])
        p.op("pool", lambda e: e.memset(pos[:, NP:NT], 16384.0), writes=[scr])
        for which, dst in ((0, S4), (1, C4)):
            if which == 0:
                p.op("dve", TS(ang, pos, cols[:, 1:2], None, ALU.mult), reads=[rg("cols")], writes=[scr])
            else:
                p.op("dve", TS(ang, pos, cols[:, 1:2], math.pi / 2.0, ALU.mult, ALU.add), reads=[rg("cols")], writes=[scr])
            p.op("dve", TS(kf, ang, 1.0 / TWO_PI, None, ALU.mult), writes=[scr])
            p.op("dve", CP(ki, kf), writes=[scr])
            p.op("dve", CP(kf, ki), writes=[scr])
            p.op("dve", STT(ang, kf, -TWO_PI, ang, ALU.mult, ALU.add), writes=[scr])
            p.op("dve", TS(ang, ang, 3.14159, -3.14159, ALU.min, ALU.max), writes=[scr])
            p.op("act", A_(ang, ang, AF.Sin), writes=[scr])
            if which == 0:
                p.op("dve", TS(dst[:, :], ang, cols[:, 2:3], None, ALU.mult), reads=[rg("cols")], writes=[scr, rg("tabs")])
            else:
                p.op("dve", CP(dst[:, :], ang), writes=[scr, rg("tabs")])

        def rstd_from(srcs, src_regs, n, D, sq_eng="act"):
            pss, pssr = nps()
            nc_ = len(srcs)
            for c, (s, sr) in enumerate(zip(srcs, src_regs)):
                b = c % 2
                if sq_eng == "act":
                    p.op("act", A_(sqb[b][:, 0:n], s, AF.Square), reads=[sr], writes=[rg("sqb", b)])
                else:
                    p.op(sq_eng, TTo(sqb[b][:, 0:n], s, s, ALU.mult), reads=[sr], writes=[rg("sqb", b)])
                p.op("pe", MM(pss[:, 0:n], ones_b[:, :], sqb[b][:, 0:n], c == 0, c == nc_ - 1),
                     reads=[rg("sqb", b), rg("ones")], writes=[pssr])
            k = rstd_from.ctr % 2
            rstd_from.ctr += 1
            t = tA[k]
            tr_ = rg("tA", k)
            p.op("act", A_(t[:, 0:n], pss[:, 0:n], AF.Ln, bias=cols[:, 3:4], scale=1.0 / D), reads=[pssr, rg("cols")], writes=[tr_])
            p.op("act", A_(t[:, 0:n], t[:, 0:n], AF.Exp, scale=-0.5), reads=[tr_], writes=[tr_])
            return t, tr_
        rstd_from.ctr = 0

        def HR(ti):
            return [rg("hT", ti)] + [rg("hTc", c, ti) for c in range(8)]

        def norm_tile(gname, ti):
            t0, n = TT[ti]
            srcs = [xT[:, c, t0:t0 + n] for c in range(8)]
            t, tr_ = rstd_from(srcs, [rg("xT", c, ti) for c in range(8)], n, 1024.0, sq_eng="pool")
            for c in range(8):
                eng = "dve"
                if c == 0:
                    rd_, wr_ = [], [rg("hT", ti), rg("hTc", 0, ti)]
                else:
                    rd_, wr_ = [rg("hT", ti)], [rg("hTc", c, ti)]
                p.op(eng, STT(hT[:, c, t0:t0 + n], xT[:, c, t0:t0 + n], V(gname, c), t[:, 0:n], ALU.mult, ALU.mult),
                     reads=[rg("xT", c, ti), tr_, rg("vecs")] + rd_, writes=wr_)

        def first_pass(gname, per_tile):
            norm_tile(gname, 0)
            for ti in range(5):
                if ti + 1 < 5:
                    norm_tile(gname, ti + 1)
                per_tile(ti)

        def ffn(l):
            def gu(Wv, wr, nn, ti):
                t0, n = TT[ti]
                pg, pgr = nps()
                pu, pur = nps()
                p.op("pe", [MM(pg[:, 0:n], Wv[:, 0, k, :], hT[:, k, t0:t0 + n], k == 0, k == 7) for k in range(8)],
                     reads=[wr] + HR(ti), writes=[pgr])
                p.op("pe", [MM(pu[:, 0:n], Wv[:, 1, k, :], hT[:, k, t0:t0 + n], k == 0, k == 7) for k in range(8)],
                     reads=[wr] + HR(ti), writes=[pur])
                b = ti % 2
                p.op("act", A_(tB[b][:, 0:n], pg[:, 0:n], AF.Silu), reads=[pgr], writes=[rg("tB", b)])
                p.op("dve", TTo(hid[:, nn, t0:t0 + n], pu[:, 0:n], tB[b][:, 0:n], ALU.mult),
                     reads=[pur, rg("tB", b)], writes=[rg("hid", nn, ti)])
            KF = 3
            Wf = []
            for nn in range(KF):
                W, wr = ring_load(w_gu[l, nn], 2048)
                Wf.append((W.rearrange("p (g k j) -> p g k j", g=2, k=8), wr))

            def pt(ti):
                for nn in range(KF):
                    gu(Wf[nn][0], Wf[nn][1], nn, ti)
            first_pass("ffn_g%d" % l, pt)
            for gi, (n0, cnt) in enumerate(HGS):
                for nn in range(cnt):
                    if gi == 0 and nn < KF:
                        continue
                    W, wr = ring_load(w_gu[l, n0 + nn], 2048)
                    Wv = W.rearrange("p (g k j) -> p g k j", g=2, k=8)
                    for ti in range(5):
                        gu(Wv, wr, nn, ti)
                for m in range(8):
                    W, wr = ring_load(w_dn[l, m][:, n0 * 128:(n0 + cnt) * 128], cnt * 128)
                    Wv = W[:, 0:cnt * 128].rearrange("p (n j) -> p n j", n=cnt)
                    for ti, (t0, n) in enumerate(TT):
                        po, por = nps()
                        p.op("pe", [MM(po[:, 0:n], Wv[:, nn, :], hid[:, nn, t0:t0 + n], nn == 0, nn == cnt - 1) for nn in range(cnt)],
                             reads=[wr] + [rg("hid", nn, ti) for nn in range(cnt)], writes=[por])
                        p.op("dve", TTo(xT[:, m, t0:t0 + n], po[:, 0:n], xT[:, m, t0:t0 + n], ALU.add),
                             reads=[por], writes=[rg("xT", m, ti)])

        for bi in range(17):
            r0 = bi * 128
            nb = min(128, NT - r0)
            ti = r0 // 512
            b = bi % 2
            p.dma("sp", io[b][0:nb, :], xin[r0:r0 + nb, :], d_io[b], writes=[rg("io", b)])
            for half in range(2):
                pt_, ptr_ = nps()
                p.op("pe", [TR(pt_[:, j * 128:j * 128 + nb], io[b][0:nb, (half * 4 + j) * 128:(half * 4 + j + 1) * 128], ident_f[0:nb, 0:nb])
                            for j in range(4)], reads=[rg("io", b), rg("ident")], writes=[ptr_])
                eng = "act" if half == 0 else "dve"
                src = pt_[:, :].rearrange("p (j t) -> p j t", j=4)[:, :, 0:nb]
                dstv = xT[:, half * 4:half * 4 + 4, r0:r0 + nb]
                if eng == "act":
                    p.op("act", A_(dstv, src, AF.Copy), reads=[ptr_], writes=[rg("xT", half * 4 + j, ti) for j in range(4)])
                else:
                    p.op("dve", CP(dstv, src), reads=[ptr_], writes=[rg("xT", half * 4 + j, ti) for j in range(4)])

        def a_layer(l):
            p.op("pool", lambda e: e.memset(gpad[:, :, 0:30], 0.0),
                 writes=[rg("gpad", c) for c in range(8)] + [scr] + [rg("hid", nn, ti) for nn in range(6) for ti in range(5)])

            def pw1(Wv, wr, c, ti):
                t0, n = TT[ti]
                pa, par = nps()
                pb, pbr = nps()
                p.op("pe", [MM(pa[:, 0:n], Wv[:, 0, k, :], hT[:, k, t0:t0 + n], k == 0, k == 7) for k in range(8)],
                     reads=[wr] + HR(ti), writes=[par])
                p.op("pe", [MM(pb[:, 0:n], Wv[:, 1, k, :], hT[:, k, t0:t0 + n], k == 0, k == 7) for k in range(8)],
                     reads=[wr] + HR(ti), writes=[pbr])
                b = ti % 2
                p.op("act", A_(tB[b][:, 0:n], pb[:, 0:n], AF.Sigmoid, bias=V("a_pw1_b%d" % l, 8 + c)),
                     reads=[pbr, rg("vecs")], writes=[rg("tB", b)])
                p.op("dve", STT(gpad[:, c, 30 + t0:30 + t0 + n], pa[:, 0:n], V("a_pw1_b%d" % l, c), tB[b][:, 0:n], ALU.add, ALU.mult),
                     reads=[par, rg("tB", b)], writes=[rg("gpad", c)])
                if t0 + n > 2034:
                    s0 = max(t0, 2034)
                    p.op("dve", STT(g_tail[:, c, s0 - 2034:t0 + n - 2034], pa[:, s0 - t0:n], V("a_pw1_b%d" % l, c),
                                    tB[b][:, s0 - t0:n], ALU.add, ALU.mult),
                         reads=[par, rg("tB", b)], writes=[rg("g_tail")])
            KF = 3
            Wf = []
            for c in range(KF):
                W, wr = ring_load(w_pw1[l, c], 2048)
                Wf.append((W.rearrange("p (g k j) -> p g k j", g=2, k=8), wr))

            def pt(ti):
                for c in range(KF):
                    pw1(Wf[c][0], Wf[c][1], c, ti)
            first_pass("a_norm_g%d" % l, pt)
            for c in range(KF, 8):
                W, wr = ring_load(w_pw1[l, c], 2048)
                Wv = W.rearrange("p (g k j) -> p g k j", g=2, k=8)
                for ti in range(5):
                    pw1(Wv, wr, c, ti)
            for b4 in range(4):
                b = b4 % 2
                p.dma("sp", io[b][0:30, :], state[l, :, b4, :], d_io[b], writes=[rg("io", b)])
                p.dma("sp", conv_s[l, b4, 0:29, :], io[b][1:30, :], d_o[b], reads=[rg("io", b)])
                pt_, ptr_ = nps()
                p.op("pe", [TR(pt_[:, c * 30:c * 30 + 30], io[b][0:30, c * 128:(c + 1) * 128], ident_f[0:30, 0:30]) for c in range(8)],
                     reads=[rg("io", b), rg("ident")], writes=[ptr_])
                p.op("act", A_(bufT[:, :, b4, 0:30], pt_[:, 0:240].rearrange("p (c t) -> p c t", c=8), AF.Copy),
                     reads=[ptr_], writes=[rg("bufT")])
            p.op("dve", CP(bufT[:, :, :, 30], g_tail[:, :, 30:34]), reads=[rg("g_tail")], writes=[rg("bufT")])
            b = 0
            for half in range(2):
                pt_, ptr_ = nps()
                p.op("pe", [TR(pt_[0:34, j * 128:(j + 1) * 128], g_tail[:, half * 4 + j, :], ident_f[:, :]) for j in range(4)],
                     reads=[rg("g_tail"), rg("ident")], writes=[ptr_])
                p.op("act", A_(io[b][0:34, half * 512:(half + 1) * 512], pt_[0:34, :], AF.Copy), reads=[ptr_], writes=[rg("io", b)])
            p.dma("sp", conv_p[l, :, :], io[b][0:30, :], d_o[b], reads=[rg("io", b)])
            for b4 in range(4):
                p.dma("sp", conv_s[l, b4, 29:30, :], io[b][30 + b4:31 + b4, :], d_o[b], reads=[rg("io", b)])
            tmp = tB[0][:, 0:8 * 31 * 2].rearrange("p (c b k) -> p c b k", c=8, b=2)
            ysr = rg("ys")
            for hb in range(2):
                p.op("dve", TTo(tmp, bufT[:, :, hb * 2:hb * 2 + 2, :],
                                dww[:, l, :, :].unsqueeze(2).broadcast_to([128, 8, 2, 31]), ALU.mult),
                     reads=[rg("bufT"), rg("dww")], writes=[rg("tB", 0)])
                p.op("dve", lambda e, hb=hb: e.tensor_reduce(out=tA[0][:, hb * 16:hb * 16 + 16].rearrange("p (c b) -> p c b", c=8),
                                                              in_=tmp, axis=AX.X, op=ALU.add),
                     reads=[rg("tB", 0)], writes=[rg("tA", 0), ysr])
            seen_t = set()

            def evac(c, ti, pc, pcr):
                t0, n = TT[ti]
                extra = []
                if ti not in seen_t:
                    seen_t.add(ti)
                    extra = [rg("hT", ti)]
                p.op("act", A_(hT[:, c, t0:t0 + n], pc[:, 0:n], AF.Identity, bias=V("a_dw_b%d" % l, c)),
                     reads=[pcr, rg("vecs")], writes=[rg("yc", c, ti)] + extra)
                if ti == 4:
                    for hb in range(2):
                        p.op("dve", TS(hT[:, c, NP + hb * 2:NP + 2 + hb * 2], tA[0][:, hb * 16 + c * 2:hb * 16 + c * 2 + 2],
                                       V("a_dw_b%d" % l, c), None, ALU.add),
                             reads=[ysr, rg("tA", 0), rg("vecs")], writes=[rg("yc", c, 4)])
            PEC = list(range(8))
            DVC = []
            dsteps = [(c, ti) for c in DVC for ti in range(5)]
            dacc = {}

            def dve_step(j):
                c, ti = dsteps[j]
                t0, n = TT[ti]
                acc, accr = aps()
                dacc[j] = (acc, accr)
                p.op("dve", TS(acc[:, 0:n], gpad[:, c, t0:t0 + n], dww[:, l, c, 0:1], None, ALU.mult),
                     reads=[rg("gpad", c), rg("dww")], writes=[accr])
                for k in range(1, 31):
                    p.op("dve", STT(acc[:, 0:n], gpad[:, c, t0 + k:t0 + k + n], dww[:, l, c, k:k + 1], acc[:, 0:n], ALU.mult, ALU.add),
                         reads=[rg("gpad", c), rg("dww")], writes=[accr])
            psteps = [(c, ti) for c in PEC for ti in range(5)]
            nd = 0
            for i, (c, ti) in enumerate(psteps):
                t0, n = TT[ti]
                db = c % 2
                dg = diagB[db]
                if ti == 0:
                    p.op("pool", TTo(dg[:, :, :], ident_b[:, :].unsqueeze(1).broadcast_to([128, 31, 128]),
                                     dww[:, l, c, :].unsqueeze(2).broadcast_to([128, 31, 128]), ALU.mult),
                         reads=[rg("identb"), rg("dww")], writes=[rg("diag", db)] + ([rg("auxB"), rg("clf", 0), rg("clf", 1), rg("krf", 0), rg("krf", 1), rg("Vh", 1)] if db == 1 else [rg("auxA"), rg("Vh", 0)] + [rg("Rq", t_) for t_ in range(5)]))
                if i % 3 == 0 and i // 3 < len(dsteps):
                    dve_step(i // 3)
                pc, pcr = nps()
                p.op("pe", [MM(pc[:, 0:n], dg[:, k, :], gpad[:, c, t0 + k:t0 + k + n], k == 0, k == 30) for k in range(31)],
                     reads=[rg("diag", db), rg("gpad", c)], writes=[pcr])
                evac(c, ti, pc, pcr)
                if i % 3 == 2 and i // 3 < len(dsteps):
                    j = i // 3
                    evac(dsteps[j][0], dsteps[j][1], dacc[j][0], dacc[j][1])
            def ln_tile(ti):
                t0, n = TT[ti]
                ps1, ps1r = aps()
                ps2, ps2r = aps()
                for c in range(8):
                    b = c % 2
                    p.op("pool", TTo(sqb[b][:, 0:n], hT[:, c, t0:t0 + n], hT[:, c, t0:t0 + n], ALU.mult), reads=[rg("yc", c, ti)], writes=[rg("sqb", b)])
                    p.op("pe", [MM(ps2[:, 0:n], ones_b[:, :], sqb[b][:, 0:n], c == 0, c == 7),
                                MM(ps1[:, 0:n], ones_b[:, :], hT[:, c, t0:t0 + n], c == 0, c == 7)],
                         reads=[rg("sqb", b), rg("ones"), rg("yc", c, ti)], writes=[ps1r, ps2r])
                mean = tA[1][:, 0:n]
                var = tA[0][:, 0:n]
                mr = rg("tA", 1)
                vr = rg("tA", 0)
                p.op("dve", TS(mean, ps1[:, 0:n], 1.0 / 1024.0, None, ALU.mult), reads=[ps1r], writes=[mr])
                p.op("dve", TTo(var, mean, mean, ALU.mult), reads=[mr, ysr], writes=[vr])
                p.op("dve", STT(var, ps2[:, 0:n], 1.0 / 1024.0, var, ALU.mult, ALU.subtract), reads=[ps2r, vr], writes=[vr])
                p.op("act", A_(var, var, AF.Ln, bias=cols[:, 3:4]), reads=[vr, rg("cols")], writes=[vr])
                p.op("act", A_(var, var, AF.Exp, scale=-0.5), reads=[vr], writes=[vr])
                for c in range(8):
                    b = c % 2
                    eng = "dve"
                    tz = tB[b][:, 0:n]
                    p.op(eng, TTo(tz, hT[:, c, t0:t0 + n], mean, ALU.subtract), reads=[rg("yc", c, ti), mr], writes=[rg("tB", b)])
                    p.op(eng, TTo(tz, tz, var, ALU.mult), reads=[vr, rg("tB", b)], writes=[rg("tB", b)])
                    p.op("act", A_(hT[:, c, t0:t0 + n], tz, AF.Silu, bias=V("a_ln_b%d" % l, c), scale=V("a_ln_g%d" % l, c)),
                         reads=[rg("tB", b), rg("vecs")], writes=[rg("yc", c, ti)])

            def pw2(Wv, wr, m, ti):
                t0, n = TT[ti]
                po, por = nps()
                p.op("pe", [MM(po[:, 0:n], Wv[:, k, :], hT[:, k, t0:t0 + n], k == 0, k == 7) for k in range(8)],
                     reads=[wr, rg("hT", ti)] + [rg("yc", c, ti) for c in range(8)], writes=[por])
                p.op("dve", STT(xT[:, m, t0:t0 + n], po[:, 0:n], V("a_pw2_b%d" % l, m), xT[:, m, t0:t0 + n], ALU.add, ALU.add),
                     reads=[por, rg("vecs")], writes=[rg("xT", m, ti)])
            KF2 = 4
            Wf2 = []
            for m in range(KF2):
                W, wr = ring_load(w_pw2[l, m], 1024)
                Wf2.append((W[:, 0:1024].rearrange("p (k j) -> p k j", k=8), wr))
            ln_tile(0)
            for ti in range(5):
                if ti + 1 < 5:
                    ln_tile(ti + 1)
                for m in range(KF2):
                    pw2(Wf2[m][0], Wf2[m][1], m, ti)
            for m in range(KF2, 8):
                W, wr = ring_load(w_pw2[l, m], 1024)
                Wv = W[:, 0:1024].rearrange("p (k j) -> p k j", k=8)
                for ti in range(5):
                    pw2(Wv, wr, m, ti)
            ffn(l)

        for l in range(2):
            a_layer(l)

        Wl, wlr = ring_load(w_dkvl, 2048)
        Wlv = Wl.rearrange("p (k j) -> p k j", k=8)
        Wr_, wrr = ring_load(w_dkvr, 512)
        Wrv = Wr_[:, 0:512].rearrange("p (k j) -> p k j", k=8)
        auxAf = auxA[:, :].bitcast(F32)
        clfs = [auxB[:, 0:1024].rearrange("p (c t) -> p c t", c=2), auxAf[:, 0:1024].rearrange("p (c t) -> p c t", c=2)]
        krfs = [auxB[0:32, 1024:1536], auxAf[0:32, 1024:1536]]
        def kv_tile(ti):
            t0, n = TT[ti]
            clf = clfs[ti % 2]
            krf = krfs[ti % 2]
            sx = ti % 2
            pl = [nps() for _ in range(2)]
            for kc in range(2):
                p.op("pe", [MM(pl[kc][0][:, 0:n], Wlv[:, k, kc * 128:(kc + 1) * 128], hT[:, k, t0:t0 + n], k == 0, k == 7) for k in range(8)],
                     reads=[wlr] + HR(ti), writes=[pl[kc][1]])
            pr1, pr1r = nps()
            pr2, pr2r = nps()
            p.op("pe", [MM(pr1[0:32, 0:n], Wrv[:, k, 0:32], hT[:, k, t0:t0 + n], k == 0, k == 7) for k in range(8)],
                 reads=[wrr] + HR(ti), writes=[pr1r])
            p.op("pe", [MM(pr2[0:32, 0:n], Wrv[:, k, 32:64], hT[:, k, t0:t0 + n], k == 0, k == 7) for k in range(8)],
                 reads=[wrr] + HR(ti), writes=[pr2r])
            t, tr_ = rstd_from([pl[0][0][:, 0:n], pl[1][0][:, 0:n]], [pl[0][1], pl[1][1]], n, 256.0)
            for kc in range(2):
                p.op("dve", STT(clf[:, kc, 0:n], pl[kc][0][:, 0:n], V("kvl_g", kc), t[:, 0:n], ALU.mult, ALU.mult),
                     reads=[pl[kc][1], tr_, rg("vecs")], writes=[rg("clf", sx)])
            p.op("act", A_(cT[:, :, t0:t0 + n], clf[:, :, 0:n], AF.Copy), reads=[rg("clf", sx)], writes=[rg("cT", ti)])
            t1 = tB[0][0:32, 0:n]
            t2 = tB[1][0:32, 0:n]
            p.op("dve", TTo(t1, pr1[0:32, 0:n], C4[0:32, t0:t0 + n], ALU.mult), reads=[pr1r, rg("tabs")], writes=[rg("tB", 0)])
            p.op("dve", TTo(t2, pr2[0:32, 0:n], S4[0:32, t0:t0 + n], ALU.mult), reads=[pr2r, rg("tabs")], writes=[rg("tB", 1)])
            p.op("dve", TTo(krf[:, 0:n], t1, t2, ALU.add), reads=[rg("tB", 0), rg("tB", 1)], writes=[rg("krf", sx)])
            p.op("act", A_(krT[:, t0:t0 + n], krf[:, 0:n], AF.Copy), reads=[rg("krf", sx)], writes=[rg("krT", ti)])
        def kv_out(ti):
            t0, n = TT[ti]
            clf = clfs[ti % 2]
            krf = krfs[ti % 2]
            sx = ti % 2
            for bi in range((n + 127) // 128):
                c0 = bi * 128
                nb = min(128, n - c0)
                b = bi % 2
                pt_, ptr_ = nps()
                p.op("pe", [TR(pt_[0:nb, 0:128], clf[:, 0, c0:c0 + nb], ident_f[:, :]),
                            TR(pt_[0:nb, 128:256], clf[:, 1, c0:c0 + nb], ident_f[:, :]),
                            TR(pt_[0:nb, 256:288], krf[:, c0:c0 + nb], ident_f[0:32, 0:32])],
                     reads=[rg("clf", sx), rg("krf", sx), rg("ident")], writes=[ptr_])
                p.op("act", A_(io[b][0:nb, 0:288], pt_[0:nb, 0:288], AF.Copy), reads=[ptr_], writes=[rg("io", b)])
                p.dma("sp", lat_all[t0 + c0:t0 + c0 + nb, :], io[b][0:nb, 0:256], d_o[b], reads=[rg("io", b)])
                p.dma("sp", kr_all[t0 + c0:t0 + c0 + nb, :], io[b][0:nb, 256:288], d_o[b], reads=[rg("io", b)])

        def kv_pt(ti):
            kv_tile(ti)
            if ti > 0:
                kv_out(ti - 1)
        first_pass("kv_g", kv_pt)
        kv_out(4)

        Vh = [auxA[:, 2080:2080 + 2176].rearrange("p (k j) -> p k j", k=17),
              auxB[:, :].bitcast(BF16)[:, 0:2176].rearrange("p (k j) -> p k j", k=17)]
        OT = hT
        QhB = [R2[:, 3 * NT:4 * NT], R2[:, 5 * NT:6 * NT]]
        KhB = [R2[:, 4 * NT:5 * NT], iob[:, 0:NT]]

        def b_layer(j):
            l = 2 + j
            Wd = [ring_load(w_dq[j, n3], 1024) for n3 in range(3)]

            def dq_tile(ti):
                t0, n = TT[ti]
                pq = [nps() for _ in range(3)]
                for n3 in range(3):
                    Wv = Wd[n3][0][:, 0:1024].rearrange("p (k j) -> p k j", k=8)
                    p.op("pe", [MM(pq[n3][0][:, 0:n], Wv[:, k, :], hT[:, k, t0:t0 + n], k == 0, k == 7) for k in range(8)],
                         reads=[Wd[n3][1]] + HR(ti), writes=[pq[n3][1]])
                t, tr_ = rstd_from([pq[i][0][:, 0:n] for i in range(3)], [pq[i][1] for i in range(3)], n, 384.0)
                for n3 in range(3):
                    p.op("dve", STT(cqn[:, n3, t0:t0 + n], pq[n3][0][:, 0:n], V("bq_g%d" % j, n3), t[:, 0:n], ALU.mult, ALU.mult),
                         reads=[pq[n3][1], tr_, rg("vecs")], writes=[rg("cqn", ti)])
            first_pass("b_g%d" % j, dq_tile)
            p.op("pool", lambda e: e.memset(Vh[0][:, :, 64:128], 1.0), writes=[rg("Vh", 0), rg("auxA"), rg("auxB"), rg("diag", 0), rg("diag", 1), rg("clf", 0), rg("clf", 1), rg("krf", 0), rg("krf", 1)])
            p.op("pool", lambda e: e.memset(Vh[1][:, :, 0:64], 1.0), writes=[rg("Vh", 1), rg("auxA"), rg("auxB"), rg("diag", 0), rg("diag", 1), rg("clf", 0), rg("clf", 1), rg("krf", 0), rg("krf", 1)] + [rg("ybuf", c) for c in range(8)])
            def rope(q4):
                W, wr = ring_load(w_rs[j, q4], 768)
                Wv = W[:, 0:768].rearrange("p (k s j) -> p k s j", k=3, s=2)
                for ti, (t0, n) in enumerate(TT):
                    pR, pRr = nps()
                    pS, pSr = nps()
                    p.op("pe", [MM(pR[:, 0:n], Wv[:, k, 0, :], cqn[:, k, t0:t0 + n], k == 0, k == 2) for k in range(3)],
                         reads=[wr, rg("cqn", ti)], writes=[pRr])
                    p.op("pe", [MM(pS[:, 0:n], Wv[:, k, 1, :], cqn[:, k, t0:t0 + n], k == 0, k == 2) for k in range(3)],
                         reads=[wr, rg("cqn", ti)], writes=[pSr])
                    p.op("dve", TTo(tB[0][:, 0:n], pR[:, 0:n], C4[:, t0:t0 + n], ALU.mult), reads=[pRr, rg("tabs")], writes=[rg("tB", 0)])
                    p.op("dve", TTo(tB[1][:, 0:n], pS[:, 0:n], S4[:, t0:t0 + n], ALU.mult), reads=[pSr, rg("tabs")], writes=[rg("tB", 1)])
                    p.op("pool", TTo(Rq[:, t0:t0 + n], tB[0][:, 0:n], tB[1][:, 0:n], ALU.add),
                         reads=[rg("tB", 0), rg("tB", 1)], writes=[rg("Rq", ti)])

            def gen(h):
                par = h % 2
                e_ = par
                vo = 64 * e_
                Qh = QhB[par]
                Kh = KhB[par]
                kx = [rg("io", 0), rg("io", 1)] if par == 1 else []
                W, wr = ring_load(w_hd[j, h], 608)
                uqn = W[:, 0:288].rearrange("p (k j) -> p k j", k=3)
                ukp = W[:, 288:480].rearrange("p (k j) -> p k j", k=2)
                uvh = W[:, 480:608].rearrange("p (k j) -> p k j", k=2)
                for ti, (t0, n) in enumerate(TT):
                    pq_, pqr = nps()
                    p.op("pe", [MM(pq_[0:96, 0:n], uqn[:, k, :], cqn[:, k, t0:t0 + n], k == 0, False) for k in range(3)]
                         + [MM(pq_[0:96, 0:n], selQ[:, h % 4, :], Rq[:, t0:t0 + n], False, True)],
                         reads=[wr, rg("cqn", ti), rg("Rq", ti), rg("selQ")], writes=[pqr])
                    p.op("act", A_(Qh[0:96, t0:t0 + n], pq_[0:96, 0:n], AF.Copy), reads=[pqr], writes=[rg("Qh", par, ti)])
                    if ti == 4:
                        p.op("act", A_(Qs[:, h, :], pq_[0:96, 16:20], AF.Copy), reads=[pqr], writes=[rg("Qs")])
                    nk_ = min(n, NP - t0)
                    pk_, pkr = nps()
                    p.op("pe", [MM(pk_[0:96, 0:nk_], ukp[:, k, :], cT[:, k, t0:t0 + nk_], k == 0, False) for k in range(2)]
                         + [MM(pk_[0:96, 0:nk_], selK[:, :], krT[:, t0:t0 + nk_], False, True)],
                         reads=[wr, rg("cT", ti), rg("krT", ti), rg("selK")], writes=[pkr])
                    p.op("dve", CP(Kh[0:96, t0:t0 + nk_], pk_[0:96, 0:nk_]), reads=[pkr], writes=[rg("Kh", par, ti)] + kx)
                for g0 in (0, 8, 16):
                    ng = min(8, 17 - g0)
                    pv, pvr = nps()
                    fl = []
                    for i in range(ng):
                        kb = g0 + i
                        k0 = kb * 128
                        nk = min(128, NP - k0)
                        for k in range(2):
                            fl.append(MM(pv[0:nk, i * 64:(i + 1) * 64], cT[:, k, k0:k0 + nk], uvh[:, k, :], k == 0, k == 1))
                    p.op("pe", fl, reads=[wr] + [rg("cT", ti) for ti in range(5)], writes=[pvr])
                    nkk = 128 if g0 < 16 else 16
                    p.op("dve", CP(Vh[e_][0:nkk, g0:g0 + ng, vo:vo + 64], pv[0:nkk, 0:ng * 64].rearrange("p (i d) -> p i d", i=ng)),
                         reads=[pvr], writes=[rg("Vh", e_)])
                if h % 4 == 3 and h < 15:
                    rope(h // 4 + 1)

            def attn(h):
                par = h % 2
                e_ = par
                pair = h // 2
                vo = 64 * e_
                do = 64 - vo
                Qh = QhB[par]
                Kh = KhB[par]
                kx = [rg("io", 0), rg("io", 1)] if par == 1 else []
                steps = []
                for ti, (q0, n) in enumerate(TT):
                    nq = min(n, NP - q0)
                    nkb = (q0 + nq - 1) // 128 + 1
                    pO, pOr = aps()
                    for kb in range(nkb):
                        steps.append((ti, q0, nq, nkb, kb, pO, pOr))
                LOOK = 4
                pis = {}

                def emit_S(i):
                    ti, q0, nq, nkb, kb, pO, pOr = steps[i]
                    k0 = kb * 128
                    nk = min(128, NP - k0)
                    qs = max(q0, k0)
                    w = q0 + nq - qs
                    pS_, pSr_ = nps()
                    if k0 >= q0:
                        dw = min(128, w)
                        p.op("pe", [MM(pS_[0:nk, 0:w], Kh[0:96, k0:k0 + nk], Qh[0:96, qs:qs + w], True, False),
                                    MM(pS_[0:nk, 0:dw], ident_b[0:nk, 0:nk], negm[0:nk, 0:dw], False, True)],
                             reads=[rg("Kh", par, k0 // 512), rg("Qh", par, ti), rg("negm"), rg("identb")] + kx, writes=[pSr_])
                    else:
                        p.op("pe", MM(pS_[0:nk, 0:w], Kh[0:96, k0:k0 + nk], Qh[0:96, qs:qs + w], True, True),
                             reads=[rg("Kh", par, k0 // 512), rg("Qh", par, ti)] + kx, writes=[pSr_])
                    pi = att_ctr[0] % 5
                    att_ctr[0] += 1
                    pis[i] = pi
                    p.op("act", A_(PT[pi][0:nk, 0:w], pS_[0:nk, 0:w], AF.Exp, scale=SCALE), reads=[pSr_], writes=[rg("PT", pi)])

                def emit_PV(i):
                    ti, q0, nq, nkb, kb, pO, pOr = steps[i]
                    k0 = kb * 128
                    nk = min(128, NP - k0)
                    qs = max(q0, k0)
                    w = q0 + nq - qs
                    pi = pis[i]
                    p.op("pe", MM(pO[:, qs - q0:qs - q0 + w], Vh[e_][0:nk, kb, :], PT[pi][0:nk, 0:w], kb == 0, kb == nkb - 1),
                         reads=[rg("PT", pi), rg("Vh", e_)], writes=[pOr])
                    if kb == nkb - 1:
                        b = ti % 2
                        p.op("dve", lambda e, b=b, pO=pO, nq=nq, do=do: e.reciprocal(out=tA[b][do:do + 64, 0:nq], in_=pO[do:do + 64, 0:nq]),
                             reads=[pOr], writes=[rg("tA", b)])
                        p.op("dve", TTo(OT[vo:vo + 64, pair, q0:q0 + nq], pO[vo:vo + 64, 0:nq], tA[b][do:do + 64, 0:nq], ALU.mult),
                             reads=[pOr, rg("tA", b)], writes=[rg("hT", ti)])

                for i in range(len(steps)):
                    emit_S(i)
                    if i >= LOOK:
                        emit_PV(i - LOOK)
                for i in range(max(0, len(steps) - LOOK), len(steps)):
                    emit_PV(i)

            rope(0)
            gen(0)
            for h in range(16):
                if h + 1 < 16:
                    gen(h + 1)
                attn(h)
            if debug and j == 0:
                allr = list(p.regs.values())
                dd = lambda name, shape, dt=BF16: nc.dram_tensor(name, list(shape), dt, kind="ExternalOutput").ap()
                p.dma("sp", dd("dbg_OT", [128, 8 * NT]), R1[:, :], d_out, reads=allr)
                p.dma("sp", dd("dbg_R2", [128, 8 * 2098]), R2[:, :], d_out, reads=allr)
                p.dma("sp", dd("dbg_auxA", [128, 4352]), auxA[:, :], d_out, reads=allr)
                p.dma("sp", dd("dbg_auxB", [128, 2048], F32), auxB[:, :], d_out, reads=allr)
                p.dma("sp", dd("dbg_tri", [128, 128]), tri[:, :], d_out, reads=allr)
                p.dma("sp", dd("dbg_PT", [128, 512]), PT[0][:, :], d_out, reads=allr)
                p.dma("sp", dd("dbg_tA", [128, 512], F32), tA[0][:, :], d_out, reads=allr)
                p.dma("sp", dd("dbg_xT", [128, 8 * NT], F32), xT[:, :, :].rearrange("p c t -> p (c t)"), d_out, reads=allr)
                p.dma("sp", dd("dbg_C4", [128, NT]), C4[:, :], d_out, reads=allr)
                p.dma("sp", dd("dbg_S4", [128, NT]), S4[:, :], d_out, reads=allr)
            if stage >= 4:
                sample_attention(j)
            else:
                p.op("pool", lambda e: e.memset(OT[:, :, NP:NT], 0.0), writes=[rg("hT", 4)])
            for m in range(8):
                W, wr = ring_load(w_o[j, m], 1024)
                Wv = W[:, 0:1024].rearrange("p (k j) -> p k j", k=8)
                for ti, (t0, n) in enumerate(TT):
                    po, por = nps()
                    p.op("pe", [MM(po[:, 0:n], Wv[:, k, :], OT[:, k, t0:t0 + n], k == 0, k == 7) for k in range(8)],
                         reads=[wr, rg("hT", ti)], writes=[por])
                    p.op("dve", TTo(xT[:, m, t0:t0 + n], po[:, 0:n], xT[:, m, t0:t0 + n], ALU.add),
                         reads=[por], writes=[rg("xT", m, ti)])
            ffn(l)

        att_ctr = [0]

        ukT = auxA[0:64, 0:4096].rearrange("p (h c) -> p h c", h=16)
        uvp = auxB[:, :].bitcast(BF16).rearrange("p (k h j) -> p k h j", k=2, h=16)
        SA = R2[:, 0:6 * NT]
        CLb2 = [SA[:, i * 2048:(i + 1) * 2048] for i in range(4)]
        CLb = [c_.rearrange("p (r c) -> p r c", r=8) for c_ in CLb2]
        KRg = [iob[:, 3072 + i * 256:3072 + (i + 1) * 256] for i in range(4)]
        tcT = [SA[:, 8192 + i * 1536: 8192 + i * 1536 + 1024].rearrange("p (k t) -> p k t", k=2) for i in range(2)]
        tkr = [SA[:, 8192 + i * 1536 + 1024: 8192 + i * 1536 + 1152] for i in range(2)]
        QRz = SA[:, 8192 + 1152:8192 + 1152 + 256].rearrange("p (r b h) -> p r b h", r=4, b=4)
        o0 = 8192 + 3072
        QL = SA[:, o0:o0 + 128].rearrange("p (k b h) -> p k b h", k=2, b=4)
        QR = SA[0:32, o0 + 128:o0 + 192].rearrange("p (b h) -> p b h", b=4)
        OLT = SA[:, o0 + 192:o0 + 320].rearrange("p (k b h) -> p k b h", k=2, b=4)
        PTs = [SA[:, o0 + 320 + i * 64:o0 + 384 + i * 64] for i in range(2)]
        PTn = SA[0:1, o0 + 448:o0 + 464]
        crow = SA[0:1, o0 + 464:o0 + 720]
        OL = SA[0:16, o0 + 720:o0 + 976]
        rd = cols[0:16, 5:6]
        d_gl = [p.new_sem("d_gl%d" % i) for i in range(4)]
        d_gk = [p.new_sem("d_gk%d" % i) for i in range(4)]

        def sample_attention(j):
            sar = rg("sa_small")
            p.dma("pool", auxA[0:64, 0:4096], w_ukT, d_misc, writes=[rg("auxA"), rg("diag", 0)] + [rg("Rq", ti) for ti in range(5)] + [rg("Vh", 0), rg("Vh", 1)])
            p.dma("pool", auxB[:, :].bitcast(BF16), w_uvp, d_misc2, writes=[rg("auxB"), rg("diag", 1), rg("clf", 0), rg("clf", 1), rg("krf", 0), rg("krf", 1)] + [rg("ybuf", c) for c in range(8)])
            sa_regs = ([rg("cqn", ti) for ti in range(5)] + [rg("Qh", pp, ti) for ti in range(5) for pp in range(2)]
                       + [rg("Kh", pp, ti) for ti in range(5) for pp in range(2)] + [rg("io", 1)])
            pq1, pq1r = nps()
            fl = []
            for h in range(16):
                for kc in range(2):
                    c0 = (kc * 16 + h) * 4
                    fl.append(MM(pq1[:, c0:c0 + 4], ukT[:, h, kc * 128:(kc + 1) * 128], Qs[0:64, h, :], True, True))
            p.op("pe", fl, reads=[rg("auxA"), rg("Qs")], writes=[pq1r])
            p.op("act", A_(QL.rearrange("p k b h -> p k h b"), pq1[:, 0:128].rearrange("p (k h b) -> p k h b", k=2, h=16), AF.Copy),
                 reads=[pq1r], writes=[sar] + sa_regs)
            pq2, pq2r = nps()
            p.op("pe", [MM(pq2[0:32, h * 4:h * 4 + 4], ident_b[0:96, 64:96], Qs[0:96, h, :], True, True) for h in range(16)],
                 reads=[rg("identb"), rg("Qs")], writes=[pq2r])
            p.op("pool", lambda e: e.memset(QRz[:, :, :, :], 0.0), reads=[pq1r], writes=[sar] + sa_regs)
            for rr in range(4):
                p.op("dve", CP(QRz[32 * rr:32 * rr + 32, rr, :, :].rearrange("p b h -> p h b"),
                               pq2[0:32, 0:64].rearrange("p (h b) -> p h b", h=16)), reads=[pq2r], writes=[sar])
            steps = [(b4, g, r4) for b4 in range(4) for g in range(16) for r4 in range(2)]
            acc = {}
            st_ = {}

            def finalize(b4):
                pO, pOr = acc[b4]
                pD = PS[7][0:16, b4:b4 + 1]
                col = NP + b4
                pS_, pSr_ = nps()
                p.op("pe", [MM(pS_[0:1, 0:16], cT[:, 0, col:col + 1], QL[:, 0, b4, :], True, False),
                            MM(pS_[0:1, 0:16], cT[:, 1, col:col + 1], QL[:, 1, b4, :], False, False),
                            MM(pS_[0:1, 0:16], krT[:, col:col + 1], QRz[0:32, 0, b4, :], False, True)],
                     reads=[rg("cT", 4), rg("krT", 4), sar], writes=[pSr_])
                p.op("act", A_(PTn, pS_[0:1, 0:16], AF.Exp, scale=SCALE), reads=[pSr_], writes=[rg("PTn")])
                pC, pCr = nps()
                pCb = pC[:, :].bitcast(BF16)
                p.op("pe", [TR(pCb[0:1, kc * 128:(kc + 1) * 128], cT[:, kc, col:col + 1], ident_b[:, :]) for kc in range(2)],
                     reads=[rg("cT", 4), rg("identb")], writes=[pCr])
                p.op("dve", CP(crow, pCb[0:1, 0:256]), reads=[pCr], writes=[rg("crow")])
                p.op("pe", [MM(pO[0:16, 0:256], PTn, crow, False, True), MM(pD, PTn, ones_b[0:1, 0:1], False, True)],
                     reads=[rg("PTn"), rg("crow"), rg("ones")], writes=[pOr, psr[7]])
                p.op("dve", lambda e, pD=pD: e.reciprocal(out=rd, in_=pD), reads=[pOr, psr[7]], writes=[rg("rd")])
                p.op("dve", TS(OL, pO[0:16, 0:256], rd, None, ALU.mult), reads=[pOr, rg("rd")], writes=[rg("OL")])
                pT2, pT2r = nps()
                pT2b = pT2[:, :].bitcast(BF16)
                p.op("pe", [TR(pT2b[:, kc * 16:(kc + 1) * 16], OL[:, kc * 128:(kc + 1) * 128], ident_b[0:16, 0:16]) for kc in range(2)],
                     reads=[rg("OL"), rg("identb")], writes=[pT2r])
                p.op("act", A_(OLT[:, :, b4, :], pT2b[:, 0:32].rearrange("p (k h) -> p k h", k=2), AF.Copy), reads=[pT2r], writes=[rg("OLT")])

            def emit_T(i):
                b4, g, r4 = steps[i]
                gi = b4 * 16 + g
                cb = gi % 4
                if r4 == 0:
                    if g == 0:
                        acc[b4] = (PS[6], psr[6])
                    p.raw("pool", lambda e, b4=b4, g=g, cb=cb: e.indirect_dma_start(
                        out=CLb2[cb], out_offset=None, in_=cache_lat[:, :],
                        in_offset=bass.IndirectOffsetOnAxis(ap=idxg[:, b4, g:g + 1], axis=0)),
                        d_gl[cb], reads=[rg("idxg")], writes=[rg("CLb", cb)])
                    p.raw("pool", lambda e, b4=b4, g=g, cb=cb: e.indirect_dma_start(
                        out=KRg[cb], out_offset=None, in_=cache_kr[:, :],
                        in_offset=bass.IndirectOffsetOnAxis(ap=idxg[:, b4, g:g + 1], axis=0)),
                        d_gk[cb], reads=[rg("idxg")], writes=[rg("KRg", cb)])
                tb = i % 2
                pA, pAr = nps()
                pB, pBr = nps()
                pAb = pA[:, :].bitcast(BF16).rearrange("p (k r t) -> p k r t", k=2, r=4)
                pBb = pB[:, :].bitcast(BF16)
                fl = []
                for rr in range(4):
                    r = r4 * 4 + rr
                    for kc in range(2):
                        fl.append(TR(pAb[:, kc, rr, :], CLb[cb][:, r, kc * 128:(kc + 1) * 128], ident_b[:, :]))
                fl.append(TR(pBb[:, 0:128], KRg[cb][:, r4 * 128:(r4 + 1) * 128], ident_b[:, :]))
                p.op("pe", fl, reads=[rg("CLb", cb), rg("KRg", cb), rg("identb")], writes=[pAr, pBr])
                p.op("act", A_(tcT[tb], pAb.rearrange("p k r t -> p k (r t)"), AF.Copy), reads=[pAr], writes=[rg("tcT", tb)])
                p.op("dve", CP(tkr[tb], pBb[:, 0:128]), reads=[pBr], writes=[rg("tkr", tb)])

            def emit_S(i):
                b4, g, r4 = steps[i]
                tb = i % 2
                pS_, pSr_ = nps()
                fl = []
                for rr in range(4):
                    o_ = pS_[:, rr * 16:(rr + 1) * 16]
                    fl.append(MM(o_, tcT[tb][:, 0, rr * 128:(rr + 1) * 128], QL[:, 0, b4, :], True, False))
                    fl.append(MM(o_, tcT[tb][:, 1, rr * 128:(rr + 1) * 128], QL[:, 1, b4, :], False, False))
                    fl.append(MM(o_, tkr[tb][:, :], QRz[:, rr, b4, :], False, True))
                p.op("pe", fl, reads=[rg("tcT", tb), rg("tkr", tb), sar], writes=[pSr_])
                p.op("act", A_(PTs[tb], pS_[:, 0:64], AF.Exp, scale=SCALE), reads=[pSr_], writes=[rg("PTs", tb)])

            def emit_V(i):
                b4, g, r4 = steps[i]
                gi = b4 * 16 + g
                cb = gi % 4
                tb = i % 2
                pO, pOr = acc[b4]
                pD = PS[7][0:16, b4:b4 + 1]
                fl = []
                for rr in range(4):
                    r = r4 * 4 + rr
                    first = (g == 0 and r4 == 0 and rr == 0)
                    fl.append(MM(pO[0:16, 0:256], PTs[tb][:, rr * 16:(rr + 1) * 16], CLb[cb][:, r, :], first, False))
                    fl.append(MM(pD, PTs[tb][:, rr * 16:(rr + 1) * 16], ones_b[:, 0:1], first, False))
                p.op("pe", fl, reads=[rg("PTs", tb), rg("CLb", cb), rg("ones")], writes=[pOr, psr[7]])
                if g == 15 and r4 == 1:
                    finalize(b4)

            n_ = len(steps)
            for i in range(n_ + 2):
                if i < n_:
                    emit_T(i)
                if 1 <= i <= n_:
                    emit_S(i - 1)
                if i >= 2:
                    emit_V(i - 2)
            for pair in range(8):
                po, por = nps()
                fl = []
                i = 0
                for e2 in range(2):
                    for kc in range(2):
                        fl.append(MM(po[:, 0:4], uvp[:, kc, 2 * pair + e2, :], OLT[:, kc, :, 2 * pair + e2], i == 0, i == 3))
                        i += 1
                p.op("pe", fl, reads=[rg("auxB"), rg("OLT")], writes=[por])
                p.op("act", A_(OT[:, pair, NP:NT], po[:, 0:4], AF.Copy), reads=[por], writes=[rg("hT", 4)])

        sa_ctr = [0]

        if stage >= 3:
            for j in range(2):
                b_layer(j)

        R1f = R1[:, :].bitcast(F32)
        yos = [R1f[:, i * 4096:(i + 1) * 4096].rearrange("p (c t) -> p c t", c=8) for i in range(2)]

        def fin_norm(ti):
            t0, n = TT[ti]
            yo = yos[ti % 2]
            srcs = [xT[:, c, t0:t0 + n] for c in range(8)]
            t, tr_ = rstd_from(srcs, [rg("xT", c, ti) for c in range(8)], n, 1024.0, sq_eng="pool")
            for c in range(8):
                eng = "dve"
                p.op(eng, STT(yo[:, c, 0:n], xT[:, c, t0:t0 + n], V("fin_g", c), t[:, 0:n], ALU.mult, ALU.mult),
                     reads=[rg("xT", c, ti), tr_, rg("vecs")] + ([] if c == 0 else [rg("yo", ti % 2)]),
                     writes=([rg("yo", ti % 2)] + [r_ for k in range(5) for r_ in HR(k)]) if c == 0 else [rg("yoc", ti % 2, c)])

        def fin_out(ti):
            t0, n = TT[ti]
            yo = yos[ti % 2]
            for bi in range((n + 127) // 128):
                c0 = bi * 128
                nb = min(128, n - c0)
                b = bi % 2
                for half in range(2):
                    pt_, ptr_ = nps()
                    p.op("pe", [TR(pt_[0:nb, jj * 128:(jj + 1) * 128], yo[:, half * 4 + jj, c0:c0 + nb], ident_f[:, :]) for jj in range(4)],
                         reads=[rg("yo", ti % 2), rg("ident")] + [rg("yoc", ti % 2, c) for c in range(1, 8)], writes=[ptr_])
                    if half == 0:
                        p.op("act", A_(io[b][0:nb, 0:512], pt_[0:nb, :], AF.Copy), reads=[ptr_], writes=[rg("io", b)])
                    else:
                        p.op("dve", CP(io[b][0:nb, 512:1024], pt_[0:nb, :]), reads=[ptr_], writes=[rg("io", b)])
                p.dma("sp", y_all[t0 + c0:t0 + c0 + nb, :], io[b][0:nb, :], d_o[b], reads=[rg("io", b)])
        fin_norm(0)
        for ti in range(5):
            if ti + 1 < 5:
                fin_norm(ti + 1)
            fin_out(ti)
        for dd_ in (d_o[0], d_o[1], d_out):
            if dd_.count > 0:
                p.q["sp"].append(lambda e, dd_=dd_: e.wait_ge(dd_.h, dd_.count))
        p.emit()
    return nc


def _cols(v):
    return np.ascontiguousarray(np.asarray(v, np.float32).reshape(-1, 128).T)


def _prep_shared(inp):
    f = lambda a: np.ascontiguousarray(np.asarray(a, dtype=np.float32))
    d = {}
    pw1 = f(inp["a_pw1_w"]).reshape(2, 8, 128, 2, 8, 128)
    d["w_pw1"] = f(pw1.transpose(0, 4, 2, 3, 1, 5).reshape(2, 8, 128, 2048))
    pw2 = f(inp["a_pw2_w"]).reshape(2, 8, 128, 8, 128)
    d["w_pw2"] = f(pw2.transpose(0, 3, 2, 1, 4).reshape(2, 8, 128, 1024))
    g = f(inp["ffn_w_gate"]).reshape(4, 8, 128, 22, 128)
    u = f(inp["ffn_w_up"]).reshape(4, 8, 128, 22, 128)
    gu = np.stack([g, u], axis=0)
    d["w_gu"] = f(gu.transpose(1, 4, 3, 0, 2, 5).reshape(4, 22, 128, 2048))
    dn = f(inp["ffn_w_down"]).reshape(4, 22, 128, 8, 128)
    d["w_dn"] = f(dn.transpose(0, 3, 2, 1, 4).reshape(4, 8, 128, 2816))
    dq = f(inp["b_w_dq"]).reshape(2, 8, 128, 3, 128)
    d["w_dq"] = f(dq.transpose(0, 3, 2, 1, 4).reshape(2, 3, 128, 1024))
    uq = f(inp["b_w_uq"]).reshape(2, 3, 128, 16, 96)
    rope = uq[..., 64:96]
    swp = np.concatenate([uq[..., 80:96], uq[..., 64:80]], axis=-1)
    rs = np.stack([rope, swp], axis=0).reshape(2, 2, 3, 128, 4, 4 * 32)
    d["w_rs"] = f(rs.transpose(1, 4, 3, 2, 0, 5).reshape(2, 4, 128, 768))
    nope = np.zeros((2, 3, 128, 16, 96), np.float32)
    nope[..., 0:64] = uq[..., 0:64]
    uk = f(inp["w_uk"]).reshape(2, 128, 16, 64)
    ukp = np.zeros((2, 128, 16, 96), np.float32)
    ukp[..., 0:64] = uk
    uv = f(inp["w_uv"]).reshape(2, 128, 16, 64)
    hd = np.zeros((2, 16, 128, 608), np.float32)
    for j in range(2):
        hd[j, :, :, 0:288] = nope[j].transpose(2, 1, 0, 3).reshape(16, 128, 288)
        hd[j, :, :, 288:480] = ukp.transpose(2, 1, 0, 3).reshape(16, 128, 192)
        hd[j, :, :, 480:608] = uv.transpose(2, 1, 0, 3).reshape(16, 128, 128)
    d["w_hd"] = hd
    wo = f(inp["b_w_o"]).reshape(2, 8, 128, 8, 128)
    d["w_o"] = f(wo.transpose(0, 3, 2, 1, 4).reshape(2, 8, 128, 1024))
    dkv = f(inp["w_dkv"]).reshape(8, 128, 288)
    d["w_dkvl"] = f(dkv[..., 0:256].transpose(1, 0, 2).reshape(128, 2048))
    rsw = np.concatenate([dkv[..., 256:288], dkv[..., 272:288], dkv[..., 256:272]], axis=-1)
    d["w_dkvr"] = f(rsw.transpose(1, 0, 2).reshape(128, 512))
    ukf = f(inp["w_uk"])
    d["w_ukT"] = f(ukf.transpose(2, 1, 0).reshape(64, 4096))
    uvp = np.zeros((128, 2, 16, 128), np.float32)
    for h in range(16):
        uvp[:, :, h, (h % 2) * 64:(h % 2) * 64 + 64] = uv[:, :, h, :].transpose(1, 0, 2)
    d["w_uvp"] = f(uvp.reshape(128, 4096))
    vs = []
    for l in range(2):
        pass
    order = [("a_norm_g", 0), ("a_norm_g", 1), ("a_pw1_b", 0), ("a_pw1_b", 1), ("a_dw_b", 0), ("a_dw_b", 1),
             ("a_ln_g", 0), ("a_ln_g", 1), ("a_ln_b", 0), ("a_ln_b", 1), ("a_pw2_b", 0), ("a_pw2_b", 1),
             ("ffn_norm_g", 0), ("ffn_norm_g", 1), ("ffn_norm_g", 2), ("ffn_norm_g", 3)]
    for nm, i in order:
        vs.append(_cols(f(inp[nm])[i]))
    vs.append(_cols(inp["kv_norm_g"]))
    vs.append(_cols(inp["kv_latent_norm_g"]))
    vs.append(_cols(f(inp["b_norm_g"])[0]))
    vs.append(_cols(f(inp["b_norm_g"])[1]))
    vs.append(_cols(f(inp["b_q_norm_g"])[0]))
    vs.append(_cols(f(inp["b_q_norm_g"])[1]))
    vs.append(_cols(inp["final_norm_g"]))
    d["vecs"] = f(np.concatenate(vs, axis=1))
    assert d["vecs"].shape == (128, NV)
    dw = f(inp["a_dw_w"]).reshape(2, 31, 8, 128)
    d["dww"] = f(dw.transpose(3, 0, 2, 1).reshape(128, 2 * 8 * 31))
    d["cache_lat"] = f(inp["cache_latent"]).reshape(5120 * 16, 2048)
    d["cache_kr"] = f(inp["cache_krope"]).reshape(5120 * 16, 256)
    return d


_NC_CACHE = {}


def kernel(**inputs):
    shared = _prep_shared(inputs)
    xp = np.asarray(inputs["x_prompt"], np.float32)
    xs = np.asarray(inputs["x_sample"], np.float32)
    meta = np.asarray(inputs["meta_tokens"], np.float32)
    stc = np.asarray(inputs["state_conv"], np.float32)
    pt = np.asarray(inputs["page_table"], np.int32)
    in_maps = []
    for i in range(8):
        m = dict(shared)
        m["xin"] = np.ascontiguousarray(np.concatenate([meta, xp[i], xs[4 * i:4 * i + 4, 0]], axis=0))
        m["state"] = np.ascontiguousarray(stc[:, 4 * i:4 * i + 4].transpose(0, 2, 1, 3))
        m["ptT"] = np.ascontiguousarray(pt[4 * i:4 * i + 4].T)
        in_maps.append(m)
    if "nc" not in _NC_CACHE:
        _NC_CACHE["nc"] = build_program()
    res = run_bass_kernel_spmd(_NC_CACHE["nc"], in_maps, core_ids=list(range(8)))
    R = res.results
    y_prompt = np.stack([R[i]["y_all"][16:NP] for i in range(8)], 0)
    y_sample = np.concatenate([R[i]["y_all"][NP:NT] for i in range(8)], 0)[:, None, :]
    lat_p = np.stack([R[i]["lat_all"][0:NP] for i in range(8)], 0)
    kr_p = np.stack([R[i]["kr_all"][0:NP] for i in range(8)], 0)
    conv_p = np.stack([R[i]["conv_p"] for i in range(8)], 1)
    lat_s = np.concatenate([R[i]["lat_all"][NP:NT] for i in range(8)], 0)[:, None, :]
    kr_s = np.concatenate([R[i]["kr_all"][NP:NT] for i in range(8)], 0)[:, None, :]
    conv_s = np.concatenate([R[i]["conv_s"] for i in range(8)], 1)
    outs = (y_prompt, y_sample, lat_p, kr_p, conv_p, lat_s, kr_s, conv_s)
    return tuple(np.ascontiguousarray(o, dtype=np.float32) for o in outs)
```

```python
import math
import numpy as np
from contextlib import ExitStack
import concourse.bass as bass
import concourse.mybir as mybir
from concourse.bass_utils import run_bass_kernel_spmd

F32 = mybir.dt.float32
BF16 = mybir.dt.bfloat16
I32 = mybir.dt.int32
ALU = mybir.AluOpType
AF = mybir.ActivationFunctionType
AX = mybir.AxisListType

NT = 2068
NP = 2064
TT = [(0, 512), (512, 512), (1024, 512), (1536, 512), (2048, 20)]
CTL = [(i * 256, 256) for i in range(8)] + [(2048, 20)]
HGS = [(0, 6), (6, 6), (12, 5), (17, 5)]
EPS = 1e-6
SCALE = 1.0 / math.sqrt(96.0)
TWO_PI = 2.0 * math.pi

VOFF = {}
_o = 0
for _name, _n in [("a_norm_g0", 8), ("a_norm_g1", 8), ("a_pw1_b0", 16), ("a_pw1_b1", 16), ("a_dw_b0", 8), ("a_dw_b1", 8),
                  ("a_ln_g0", 8), ("a_ln_g1", 8), ("a_ln_b0", 8), ("a_ln_b1", 8), ("a_pw2_b0", 8), ("a_pw2_b1", 8),
                  ("ffn_g0", 8), ("ffn_g1", 8), ("ffn_g2", 8), ("ffn_g3", 8), ("kv_g", 8), ("kvl_g", 2),
                  ("b_g0", 8), ("b_g1", 8), ("bq_g0", 3), ("bq_g1", 3), ("fin_g", 8)]:
    VOFF[_name] = _o
    _o += _n
NV = _o


class Sem:
    def __init__(self, h, name):
        self.h = h
        self.name = name
        self.count = 0


class Reg:
    __slots__ = ("name", "lw", "rs")

    def __init__(self, name):
        self.name = name
        self.lw = None
        self.rs = []


class Prog:
    ENG = ("pe", "act", "dve", "pool", "sp")

    def __init__(self, nc, stack):
        self.nc = nc
        self.stack = stack
        self.q = {e: [] for e in self.ENG}
        self.sem = {e: self.new_sem("e_" + e) for e in self.ENG}
        self.waited = {e: {} for e in self.ENG}
        self.regs = {}

    def new_sem(self, name):
        return Sem(self.stack.enter_context(self.nc.semaphore(name)), name)

    def sb(self, name, shape, dt):
        return self.stack.enter_context(self.nc.sbuf_tensor("sb_" + name, list(shape), dt))

    def psum(self, name, shape, dt):
        return self.stack.enter_context(self.nc.psum_tensor(name, list(shape), dt))

    def rg(self, *key):
        r = self.regs.get(key)
        if r is None:
            r = Reg(str(key))
            self.regs[key] = r
        return r

    def _waits(self, eng, reads, writes):
        deps = {}

        def add(d):
            if d is None:
                return
            s, v = d
            if deps.get(s, 0) < v:
                deps[s] = v
        for r in reads:
            add(r.lw)
        for w in writes:
            add(w.lw)
            for d in w.rs:
                add(d)
        out = []
        wd = self.waited[eng]
        for s, v in deps.items():
            if eng == "pe" and s is self.sem["pe"]:
                continue
            if wd.get(s, 0) >= v:
                continue
            wd[s] = v
            out.append((s, v))
        return out

    def _commit(self, tok, reads, writes):
        for r in reads:
            r.rs.append(tok)
            if len(r.rs) > 48:
                m = {}
                for s, v in r.rs:
                    if m.get(s, 0) < v:
                        m[s] = v
                r.rs = list(m.items())
        for w in writes:
            w.lw = tok
            w.rs = []

    def op(self, eng, fns, reads=(), writes=()):
        if callable(fns):
            fns = [fns]
        waits = self._waits(eng, reads, writes)
        s = self.sem[eng]
        s.count += 1
        val = s.count
        q = self.q[eng]
        for ws, wv in waits:
            q.append(lambda e, ws=ws, wv=wv: e.wait_ge(ws.h, wv))
        for f in fns[:-1]:
            q.append(f)
        last = fns[-1]
        q.append(lambda e, last=last, s=s: last(e).then_inc(s.h, 1))
        self._commit((s, val), reads, writes)

    def dma(self, queue, out, in_, dsem, reads=(), writes=(), **kw):
        self.raw(queue, lambda e, out=out, in_=in_, kw=kw: e.dma_start(out=out, in_=in_, **kw), dsem, reads, writes)

    def raw(self, queue, fn, dsem, reads=(), writes=()):
        waits = self._waits(queue, reads, writes)
        dsem.count += 16
        val = dsem.count
        q = self.q[queue]
        for ws, wv in waits:
            q.append(lambda e, ws=ws, wv=wv: e.wait_ge(ws.h, wv))
        q.append(lambda e, fn=fn, dsem=dsem: fn(e).then_inc(dsem.h, 16))
        self._commit((dsem, val), reads, writes)

    def final_wait(self, eng, regs):
        for ws, wv in self._waits(eng, regs, ()):
            self.q[eng].append(lambda e, ws=ws, wv=wv: e.wait_ge(ws.h, wv))

    def emit(self):
        with self.nc.Block() as block:
            @block.tensor
            def _(e):
                for f in self.q["pe"]:
                    f(e)

            @block.scalar
            def _(e):
                for f in self.q["act"]:
                    f(e)

            @block.vector
            def _(e):
                for f in self.q["dve"]:
                    f(e)

            @block.gpsimd
            def _(e):
                for f in self.q["pool"]:
                    f(e)

            @block.sync
            def _(e):
                for f in self.q["sp"]:
                    f(e)


def A_(out, in_, func, **kw):
    return lambda e: e.activation(out=out, in_=in_, func=func, **kw)


def TTo(out, a, b, op):
    return lambda e: e.tensor_tensor(out=out, in0=a, in1=b, op=op)


def STT(out, a, s, b, op0, op1):
    return lambda e: e.scalar_tensor_tensor(out=out, in0=a, scalar=s, in1=b, op0=op0, op1=op1)


def TS(out, a, s1, s2, op0, op1=None):
    if op1 is None:
        return lambda e: e.tensor_scalar(out=out, in0=a, scalar1=s1, scalar2=None, op0=op0)
    return lambda e: e.tensor_scalar(out=out, in0=a, scalar1=s1, scalar2=s2, op0=op0, op1=op1)


def CP(out, in_):
    return lambda e: e.tensor_copy(out=out, in_=in_)


def MM(out, l, r, st, sp):
    return lambda e: e.matmul(out, lhsT=l, rhs=r, start=st, stop=sp)


def TR(out, in_, ident):
    return lambda e: e.transpose(out, in_, ident)


def build_program(stage=99, debug=False):
    nc = bass.Bass("TRN2", target_bir_lowering=False)

    def din(name, shape, dt=F32):
        return nc.dram_tensor(name, list(shape), dt, kind="ExternalInput").ap()

    def dout(name, shape):
        return nc.dram_tensor(name, list(shape), F32, kind="ExternalOutput").ap()

    xin = din("xin", [NT, 1024])
    state = din("state", [2, 30, 4, 1024])
    ptT = din("ptT", [128, 4], I32)
    cache_lat = din("cache_lat", [5120 * 16, 2048])
    cache_kr = din("cache_kr", [5120 * 16, 256])
    w_pw1 = din("w_pw1", [2, 8, 128, 2048])
    w_pw2 = din("w_pw2", [2, 8, 128, 1024])
    w_gu = din("w_gu", [4, 22, 128, 2048])
    w_dn = din("w_dn", [4, 8, 128, 2816])
    w_dq = din("w_dq", [2, 3, 128, 1024])
    w_rs = din("w_rs", [2, 4, 128, 768])
    w_hd = din("w_hd", [2, 16, 128, 608])
    w_o = din("w_o", [2, 8, 128, 1024])
    w_dkvl = din("w_dkvl", [128, 2048])
    w_dkvr = din("w_dkvr", [128, 512])
    w_ukT = din("w_ukT", [64, 4096])
    w_uvp = din("w_uvp", [128, 4096])
    vecs_d = din("vecs", [128, NV])
    dww_d = din("dww", [128, 2 * 8 * 31])
    y_all = dout("y_all", [NT, 1024])
    lat_all = dout("lat_all", [NT, 256])
    kr_all = dout("kr_all", [NT, 32])
    conv_p = dout("conv_p", [2, 30, 1024])
    conv_s = dout("conv_s", [2, 4, 30, 1024])

    with ExitStack() as st:
        p = Prog(nc, st)
        rg = p.rg
        xT = p.sb("xT", [128, 8, NT], F32)
        R1 = p.sb("R1", [128, 8 * NT], BF16)
        R2 = p.sb("R2", [128, 8 * 2098], BF16)
        ringt = p.sb("ring", [128, 4, 2048], BF16)
        auxA = p.sb("auxA", [128, 4352], BF16)
        auxB = p.sb("auxB", [128, 2048], F32)
        C4 = p.sb("C4", [128, NT], BF16)
        S4 = p.sb("S4", [128, NT], BF16)
        krT = p.sb("krT", [32, NT], BF16)
        ident_f = p.sb("ident_f", [128, 128], F32)
        ident_b = p.sb("ident_b", [128, 128], BF16)
        ones_b = p.sb("ones_b", [128, 128], BF16)
        tri = p.sb("tri", [128, 128], BF16)
        negm = p.sb("negm", [128, 128], BF16)
        selK = p.sb("selK", [32, 96], BF16)
        selQ = p.sb("selQ", [128, 4, 96], BF16)
        vecs = p.sb("vecs", [128, NV], F32)
        dww = p.sb("dww", [128, 2, 8, 31], F32)
        sqb = [p.sb("sqb%d" % i, [128, 512], BF16) for i in range(2)]
        tA = [p.sb("tA%d" % i, [128, 512], F32) for i in range(2)]
        tB = [p.sb("tB%d" % i, [128, 512], F32) for i in range(2)]
        PT = [p.sb("PT%d" % i, [128, 512], BF16) for i in range(4)]
        iot = p.sb("iot", [128, 2, 1024], F32)
        io = [iot[:, 0, :], iot[:, 1, :]]
        iob = iot[:, :, :].rearrange("p a b -> p (a b)").bitcast(BF16)
        g_tail = p.sb("g_tail", [128, 8, 34], F32)
        bufT = p.sb("bufT", [128, 8, 4, 31], F32)
        PT.append(bufT[:, :, :, :].rearrange("p a b c -> p (a b c)").bitcast(BF16)[:, 0:512])
        cols = p.sb("cols", [128, 8], F32)
        icol = p.sb("icol", [128, 4], I32)
        idx = p.sb("idx", [128, 4], I32)
        idxg = p.sb("idxg", [128, 4, 16], I32)
        Qs = p.sb("Qs", [96, 16, 4], BF16)
        PS = [p.psum("ps%d" % i, [128, 512], F32) for i in range(8)]
        psr = [rg("ps", i) for i in range(8)]
        ps_ctr = [0]

        def nps():
            i = ps_ctr[0] % 6
            ps_ctr[0] += 1
            return PS[i], psr[i]

        aps_ctr = [0]

        def aps():
            i = 6 + aps_ctr[0] % 2
            aps_ctr[0] += 1
            return PS[i], psr[i]

        d_const = p.new_sem("d_const")
        d_const2 = p.new_sem("d_const2")
        d_const3 = p.new_sem("d_const3")
        d_misc2 = p.new_sem("d_misc2")
        d_o = [p.new_sem("d_o%d" % i) for i in range(2)]
        d_io = [p.new_sem("d_io%d" % i) for i in range(2)]
        d_out = p.new_sem("d_out")
        d_ring = [p.new_sem("d_ring%d" % i) for i in range(4)]
        d_misc = p.new_sem("d_misc")
        d_qk = [p.new_sem("d_qk%d" % i) for i in range(2)]
        out_reg = rg("out")

        hT = R1[:, :].rearrange("p (c t) -> p c t", c=8)
        gpad = R2[:, :].rearrange("p (c t) -> p c t", c=8)
        hid = R2[:, 0:6 * NT].rearrange("p (c t) -> p c t", c=6)
        cqn = R2[:, 0:3 * NT].rearrange("p (c t) -> p c t", c=3)
        Qh = R2[:, 3 * NT:4 * NT]
        Kh = R2[:, 4 * NT:5 * NT]
        cT = R2[:, 6 * NT:8 * NT].rearrange("p (c t) -> p c t", c=2)
        diagB = [auxA[:, 0:31 * 128].rearrange("p (k j) -> p k j", k=31),
                 auxB[:, :].bitcast(BF16)[:, 0:31 * 128].rearrange("p (k j) -> p k j", k=31)]
        Rq = auxA[:, 0:NT]
        ybuf = auxB[:, :].rearrange("p (c t) -> p c t", c=8)
        ring_ctr = [0]

        def ring_load(src_ap, n):
            i = ring_ctr[0] % 4
            ring_ctr[0] += 1
            r = rg("ring", i)
            np_ = src_ap.shape[0]
            dst = ringt[0:np_, i, 0:n]
            p.dma("pool", dst, src_ap, d_ring[i], writes=[r])
            return ringt[:, i, :], r

        def V(name, c=0):
            o = VOFF[name] + c
            return vecs[:, o:o + 1]

        p.dma("sp", vecs[:, :], vecs_d, d_const, writes=[rg("vecs")])
        p.dma("sp", dww[:, :, :, :], dww_d.rearrange("p (l c k) -> p l c k", l=2, c=8), d_const2, writes=[rg("dww")])
        p.dma("sp", idx[:, :], ptT, d_const3, writes=[rg("idx")])
        for g in range(16):
            p.op("dve", TS(idxg[:, :, g], idx[:, :], 16, g, ALU.mult, ALU.add), reads=[rg("idx")], writes=[rg("idxg")])
        R2f = R2[:, :].bitcast(F32)
        R2i = R2[:, :].bitcast(I32)
        iota_row = R2f[:, 0:128]
        scr = rg("scr")
        p.op("pool", lambda e: e.iota(iota_row, pattern=[[1, 128]], base=0, channel_multiplier=0,
                                      allow_small_or_imprecise_dtypes=True), writes=[scr])
        p.op("pool", lambda e: e.iota(cols[:, 0:1], pattern=[[0, 1]], base=0, channel_multiplier=1,
                                      allow_small_or_imprecise_dtypes=True), writes=[rg("cols")])
        p.op("pool", lambda e: e.iota(icol[:, 0:1], pattern=[[0, 1]], base=0, channel_multiplier=1), writes=[rg("icol")])
        p.op("dve", TS(ident_f[:, :], iota_row, cols[:, 0:1], None, ALU.is_equal), reads=[scr, rg("cols")], writes=[rg("ident")])
        p.op("dve", CP(ident_b[:, :], ident_f[:, :]), reads=[rg("ident")], writes=[rg("identb")])
        p.op("dve", TS(tri[:, :], iota_row, cols[:, 0:1], None, ALU.is_ge), reads=[scr, rg("cols")], writes=[rg("tri")])
        p.op("dve", TS(negm[:, :], tri[:, :], 30000.0, -30000.0, ALU.mult, ALU.add), reads=[rg("tri")], writes=[rg("negm")])
        p.op("pool", lambda e: e.memset(ones_b[:, :], 1.0), writes=[rg("ones")])
        p.op("pool", lambda e: e.memset(selK[:, :], 0.0), writes=[rg("selK")])
        p.op("pool", lambda e: e.memset(selQ[:, :, :], 0.0), writes=[rg("selQ")])
        p.op("dve", CP(selK[:, 64:96], ident_b[0:32, 0:32]), reads=[rg("identb")], writes=[rg("selK")])
        for i in range(4):
            p.op("dve", CP(selQ[:, i, 64:96], ident_b[:, 32 * i:32 * i + 32]), reads=[rg("identb")], writes=[rg("selQ")])
        p.op("dve", lambda e: e.tensor_single_scalar(out=icol[:, 1:2], in_=icol[:, 0:1], scalar=15, op=ALU.bitwise_and),
             reads=[rg("icol")], writes=[rg("icol")])
        p.op("dve", lambda e: e.tensor_single_scalar(out=icol[:, 2:3], in_=icol[:, 0:1], scalar=16, op=ALU.bitwise_and),
             reads=[rg("icol")], writes=[rg("icol")])
        p.op("dve", CP(cols[:, 1:3], icol[:, 1:3]), reads=[rg("icol")], writes=[rg("cols")])
        p.op("act", A_(cols[:, 1:2], cols[:, 1:2], AF.Exp, scale=-math.log(10000.0) / 16.0), reads=[rg("cols")], writes=[rg("cols")])
        p.op("dve", TS(cols[:, 2:3], cols[:, 2:3], 1.0 / 8.0, -1.0, ALU.mult, ALU.add), reads=[rg("cols")], writes=[rg("cols")])
        p.op("pool", lambda e: e.memset(cols[:, 3:4], EPS), writes=[rg("cols")])
        p.op("pool", lambda e: e.memset(cols[:, 4:5], 0.0), writes=[rg("cols")])
        pos = R2f[:, 0:NT]
        ang = R2f[:, NT:2 * NT]
        kf = R2f[:, 2 * NT:3 * NT]
        ki = R2i[:, 3 * NT:4 * NT]
        p.op("pool", lambda e: e.iota(pos, pattern=[[1, NT]], base=0, channel_multiplier=0,
                                      allow_small_or_imprecise_dtypes=True), reads=[rg("ident"), rg("tri")], writes=[scr])
        p.op("pool", lambda e: e.memset(pos[:, NP:NT], 16384.0), writes=[scr])
        for which, dst in ((0, S4), (1, C4)):
            if which == 0:
                p.op("dve", TS(ang, pos, cols[:, 1:2], None, ALU.mult), reads=[rg("cols")], writes=[scr])
            else:
                p.op("dve", TS(ang, pos, cols[:, 1:2], math.pi / 2.0, ALU.mult, ALU.add), reads=[rg("cols")], writes=[scr])
            p.op("dve", TS(kf, ang, 1.0 / TWO_PI, None, ALU.mult), writes=[scr])
            p.op("dve", CP(ki, kf), writes=[scr])
            p.op("dve", CP(kf, ki), writes=[scr])
            p.op("dve", STT(ang, kf, -TWO_PI, ang, ALU.mult, ALU.add), writes=[scr])
            p.op("dve", TS(ang, ang, 3.14159, -3.14159, ALU.min, ALU.max), writes=[scr])
            p.op("act", A_(ang, ang, AF.Sin), writes=[scr])
            if which == 0:
                p.op("dve", TS(dst[:, :], ang, cols[:, 2:3], None, ALU.mult), reads=[rg("cols")], writes=[scr, rg("tabs")])
            else:
                p.op("dve", CP(dst[:, :], ang), writes=[scr, rg("tabs")])

        def rstd_from(srcs, src_regs, n, D, sq_eng="act"):
            pss, pssr = nps()
            nc_ = len(srcs)
            for c, (s, sr) in enumerate(zip(srcs, src_regs)):
                b = c % 2
                if sq_eng == "act":
                    p.op("act", A_(sqb[b][:, 0:n], s, AF.Square), reads=[sr], writes=[rg("sqb", b)])
                else:
                    p.op(sq_eng, TTo(sqb[b][:, 0:n], s, s, ALU.mult), reads=[sr], writes=[rg("sqb", b)])
                p.op("pe", MM(pss[:, 0:n], ones_b[:, :], sqb[b][:, 0:n], c == 0, c == nc_ - 1),
                     reads=[rg("sqb", b), rg("ones")], writes=[pssr])
            k = rstd_from.ctr % 2
            rstd_from.ctr += 1
            t = tA[k]
            tr_ = rg("tA", k)
            p.op("act", A_(t[:, 0:n], pss[:, 0:n], AF.Ln, bias=cols[:, 3:4], scale=1.0 / D), reads=[pssr, rg("cols")], writes=[tr_])
            p.op("act", A_(t[:, 0:n], t[:, 0:n], AF.Exp, scale=-0.5), reads=[tr_], writes=[tr_])
            return t, tr_
        rstd_from.ctr = 0

        def HR(ti):
            return [rg("hT", ti)] + [rg("hTc", c, ti) for c in range(8)]

        def norm_tile(gname, ti):
            t0, n = TT[ti]
            srcs = [xT[:, c, t0:t0 + n] for c in range(8)]
            t, tr_ = rstd_from(srcs, [rg("xT", c, ti) for c in range(8)], n, 1024.0, sq_eng="pool")
            for c in range(8):
                eng = "dve"
                if c == 0:
                    rd_, wr_ = [], [rg("hT", ti), rg("hTc", 0, ti)]
                else:
                    rd_, wr_ = [rg("hT", ti)], [rg("hTc", c, ti)]
                p.op(eng, STT(hT[:, c, t0:t0 + n], xT[:, c, t0:t0 + n], V(gname, c), t[:, 0:n], ALU.mult, ALU.mult),
                     reads=[rg("xT", c, ti), tr_, rg("vecs")] + rd_, writes=wr_)

        def first_pass(gname, per_tile):
            norm_tile(gname, 0)
            for ti in range(5):
                if ti + 1 < 5:
                    norm_tile(gname, ti + 1)
                per_tile(ti)

        def ffn(l):
            def gu(Wv, wr, nn, ti):
                t0, n = TT[ti]
                pg, pgr = nps()
                pu, pur = nps()
                p.op("pe", [MM(pg[:, 0:n], Wv[:, 0, k, :], hT[:, k, t0:t0 + n], k == 0, k == 7) for k in range(8)],
                     reads=[wr] + HR(ti), writes=[pgr])
                p.op("pe", [MM(pu[:, 0:n], Wv[:, 1, k, :], hT[:, k, t0:t0 + n], k == 0, k == 7) for k in range(8)],
                     reads=[wr] + HR(ti), writes=[pur])
                b = ti % 2
                p.op("act", A_(tB[b][:, 0:n], pg[:, 0:n], AF.Silu), reads=[pgr], writes=[rg("tB", b)])
                p.op("dve", TTo(hid[:, nn, t0:t0 + n], pu[:, 0:n], tB[b][:, 0:n], ALU.mult),
                     reads=[pur, rg("tB", b)], writes=[rg("hid", nn, ti)])
            KF = 3
            Wf = []
            for nn in range(KF):
                W, wr = ring_load(w_gu[l, nn], 2048)
                Wf.append((W.rearrange("p (g k j) -> p g k j", g=2, k=8), wr))

            def pt(ti):
                for nn in range(KF):
                    gu(Wf[nn][0], Wf[nn][1], nn, ti)
            first_pass("ffn_g%d" % l, pt)
            for gi, (n0, cnt) in enumerate(HGS):
                for nn in range(cnt):
                    if gi == 0 and nn < KF:
                        continue
                    W, wr = ring_load(w_gu[l, n0 + nn], 2048)
                    Wv = W.rearrange("p (g k j) -> p g k j", g=2, k=8)
                    for ti in range(5):
                        gu(Wv, wr, nn, ti)
                for m in range(8):
                    W, wr = ring_load(w_dn[l, m][:, n0 * 128:(n0 + cnt) * 128], cnt * 128)
                    Wv = W[:, 0:cnt * 128].rearrange("p (n j) -> p n j", n=cnt)
                    for ti, (t0, n) in enumerate(TT):
                        po, por = nps()
                        p.op("pe", [MM(po[:, 0:n], Wv[:, nn, :], hid[:, nn, t0:t0 + n], nn == 0, nn == cnt - 1) for nn in range(cnt)],
                             reads=[wr] + [rg("hid", nn, ti) for nn in range(cnt)], writes=[por])
                        p.op("dve", TTo(xT[:, m, t0:t0 + n], po[:, 0:n], xT[:, m, t0:t0 + n], ALU.add),
                             reads=[por], writes=[rg("xT", m, ti)])

        for bi in range(17):
            r0 = bi * 128
            nb = min(128, NT - r0)
            ti = r0 // 512
            b = bi % 2
            p.dma("sp", io[b][0:nb, :], xin[r0:r0 + nb, :], d_io[b], writes=[rg("io", b)])
            for half in range(2):
                pt_, ptr_ = nps()
                p.op("pe", [TR(pt_[:, j * 128:j * 128 + nb], io[b][0:nb, (half * 4 + j) * 128:(half * 4 + j + 1) * 128], ident_f[0:nb, 0:nb])
                            for j in range(4)], reads=[rg("io", b), rg("ident")], writes=[ptr_])
                eng = "act" if half == 0 else "dve"
                src = pt_[:, :].rearrange("p (j t) -> p j t", j=4)[:, :, 0:nb]
                dstv = xT[:, half * 4:half * 4 + 4, r0:r0 + nb]
                if eng == "act":
                    p.op("act", A_(dstv, src, AF.Copy), reads=[ptr_], writes=[rg("xT", half * 4 + j, ti) for j in range(4)])
                else:
                    p.op("dve", CP(dstv, src), reads=[ptr_], writes=[rg("xT", half * 4 + j, ti) for j in range(4)])

        def a_layer(l):
            p.op("pool", lambda e: e.memset(gpad[:, :, 0:30], 0.0),
                 writes=[rg("gpad", c) for c in range(8)] + [scr] + [rg("hid", nn, ti) for nn in range(6) for ti in range(5)])

            def pw1(Wv, wr, c, ti):
                t0, n = TT[ti]
                pa, par = nps()
                pb, pbr = nps()
                p.op("pe", [MM(pa[:, 0:n], Wv[:, 0, k, :], hT[:, k, t0:t0 + n], k == 0, k == 7) for k in range(8)],
                     reads=[wr] + HR(ti), writes=[par])
                p.op("pe", [MM(pb[:, 0:n], Wv[:, 1, k, :], hT[:, k, t0:t0 + n], k == 0, k == 7) for k in range(8)],
                     reads=[wr] + HR(ti), writes=[pbr])
                b = ti % 2
                p.op("act", A_(tB[b][:, 0:n], pb[:, 0:n], AF.Sigmoid, bias=V("a_pw1_b%d" % l, 8 + c)),
                     reads=[pbr, rg("vecs")], writes=[rg("tB", b)])
                p.op("dve", STT(gpad[:, c, 30 + t0:30 + t0 + n], pa[:, 0:n], V("a_pw1_b%d" % l, c), tB[b][:, 0:n], ALU.add, ALU.mult),
                     reads=[par, rg("tB", b)], writes=[rg("gpad", c)])
                if t0 + n > 2034:
                    s0 = max(t0, 2034)
                    p.op("dve", STT(g_tail[:, c, s0 - 2034:t0 + n - 2034], pa[:, s0 - t0:n], V("a_pw1_b%d" % l, c),
                                    tB[b][:, s0 - t0:n], ALU.add, ALU.mult),
                         reads=[par, rg("tB", b)], writes=[rg("g_tail")])
            KF = 3
            Wf = []
            for c in range(KF):
                W, wr = ring_load(w_pw1[l, c], 2048)
                Wf.append((W.rearrange("p (g k j) -> p g k j", g=2, k=8), wr))

            def pt(ti):
                for c in range(KF):
                    pw1(Wf[c][0], Wf[c][1], c, ti)
            first_pass("a_norm_g%d" % l, pt)
            for c in range(KF, 8):
                W, wr = ring_load(w_pw1[l, c], 2048)
                Wv = W.rearrange("p (g k j) -> p g k j", g=2, k=8)
                for ti in range(5):
                    pw1(Wv, wr, c, ti)
            for b4 in range(4):
                b = b4 % 2
                p.dma("sp", io[b][0:30, :], state[l, :, b4, :], d_io[b], writes=[rg("io", b)])
                p.dma("sp", conv_s[l, b4, 0:29, :], io[b][1:30, :], d_o[b], reads=[rg("io", b)])
                pt_, ptr_ = nps()
                p.op("pe", [TR(pt_[:, c * 30:c * 30 + 30], io[b][0:30, c * 128:(c + 1) * 128], ident_f[0:30, 0:30]) for c in range(8)],
                     reads=[rg("io", b), rg("ident")], writes=[ptr_])
                p.op("act", A_(bufT[:, :, b4, 0:30], pt_[:, 0:240].rearrange("p (c t) -> p c t", c=8), AF.Copy),
                     reads=[ptr_], writes=[rg("bufT")])
            p.op("dve", CP(bufT[:, :, :, 30], g_tail[:, :, 30:34]), reads=[rg("g_tail")], writes=[rg("bufT")])
            b = 0
            for half in range(2):
                pt_, ptr_ = nps()
                p.op("pe", [TR(pt_[0:34, j * 128:(j + 1) * 128], g_tail[:, half * 4 + j, :], ident_f[:, :]) for j in range(4)],
                     reads=[rg("g_tail"), rg("ident")], writes=[ptr_])
                p.op("act", A_(io[b][0:34, half * 512:(half + 1) * 512], pt_[0:34, :], AF.Copy), reads=[ptr_], writes=[rg("io", b)])
            p.dma("sp", conv_p[l, :, :], io[b][0:30, :], d_o[b], reads=[rg("io", b)])
            for b4 in range(4):
                p.dma("sp", conv_s[l, b4, 29:30, :], io[b][30 + b4:31 + b4, :], d_o[b], reads=[rg("io", b)])
            tmp = tB[0][:, 0:8 * 31 * 2].rearrange("p (c b k) -> p c b k", c=8, b=2)
            ysr = rg("ys")
            for hb in range(2):
                p.op("dve", TTo(tmp, bufT[:, :, hb * 2:hb * 2 + 2, :],
                                dww[:, l, :, :].unsqueeze(2).broadcast_to([128, 8, 2, 31]), ALU.mult),
                     reads=[rg("bufT"), rg("dww")], writes=[rg("tB", 0)])
                p.op("dve", lambda e, hb=hb: e.tensor_reduce(out=tA[0][:, hb * 16:hb * 16 + 16].rearrange("p (c b) -> p c b", c=8),
                                                              in_=tmp, axis=AX.X, op=ALU.add),
                     reads=[rg("tB", 0)], writes=[rg("tA", 0), ysr])
            seen_t = set()

            def evac(c, ti, pc, pcr):
                t0, n = TT[ti]
                extra = []
                if ti not in seen_t:
                    seen_t.add(ti)
                    extra = [rg("hT", ti)]
                p.op("act", A_(hT[:, c, t0:t0 + n], pc[:, 0:n], AF.Identity, bias=V("a_dw_b%d" % l, c)),
                     reads=[pcr, rg("vecs")], writes=[rg("yc", c, ti)] + extra)
                if ti == 4:
                    for hb in range(2):
                        p.op("dve", TS(hT[:, c, NP + hb * 2:NP + 2 + hb * 2], tA[0][:, hb * 16 + c * 2:hb * 16 + c * 2 + 2],
                                       V("a_dw_b%d" % l, c), None, ALU.add),
                             reads=[ysr, rg("tA", 0), rg("vecs")], writes=[rg("yc", c, 4)])
            PEC = list(range(8))
            DVC = []
            dsteps = [(c, ti) for c in DVC for ti in range(5)]
            dacc = {}

            def dve_step(j):
                c, ti = dsteps[j]
                t0, n = TT[ti]
                acc, accr = aps()
                dacc[j] = (acc, accr)
                p.op("dve", TS(acc[:, 0:n], gpad[:, c, t0:t0 + n], dww[:, l, c, 0:1], None, ALU.mult),
                     reads=[rg("gpad", c), rg("dww")], writes=[accr])
                for k in range(1, 31):
                    p.op("dve", STT(acc[:, 0:n], gpad[:, c, t0 + k:t0 + k + n], dww[:, l, c, k:k + 1], acc[:, 0:n], ALU.mult, ALU.add),
                         reads=[rg("gpad", c), rg("dww")], writes=[accr])
            psteps = [(c, ti) for c in PEC for ti in range(5)]
            nd = 0
            for i, (c, ti) in enumerate(psteps):
                t0, n = TT[ti]
                db = c % 2
                dg = diagB[db]
                if ti == 0:
                    p.op("pool", TTo(dg[:, :, :], ident_b[:, :].unsqueeze(1).broadcast_to([128, 31, 128]),
                                     dww[:, l, c, :].unsqueeze(2).broadcast_to([128, 31, 128]), ALU.mult),
                         reads=[rg("identb"), rg("dww")], writes=[rg("diag", db)] + ([rg("auxB"), rg("clf", 0), rg("clf", 1), rg("krf", 0), rg("krf", 1), rg("Vh", 1)] if db == 1 else [rg("auxA"), rg("Vh", 0)] + [rg("Rq", t_) for t_ in range(5)]))
                if i % 3 == 0 and i // 3 < len(dsteps):
                    dve_step(i // 3)
                pc, pcr = nps()
                p.op("pe", [MM(pc[:, 0:n], dg[:, k, :], gpad[:, c, t0 + k:t0 + k + n], k == 0, k == 30) for k in range(31)],
                     reads=[rg("diag", db), rg("gpad", c)], writes=[pcr])
                evac(c, ti, pc, pcr)
                if i % 3 == 2 and i // 3 < len(dsteps):
                    j = i // 3
                    evac(dsteps[j][0], dsteps[j][1], dacc[j][0], dacc[j][1])
            def ln_tile(ti):
                t0, n = TT[ti]
                ps1, ps1r = aps()
                ps2, ps2r = aps()
                for c in range(8):
                    b = c % 2
                    p.op("pool", TTo(sqb[b][:, 0:n], hT[:, c, t0:t0 + n], hT[:, c, t0:t0 + n], ALU.mult), reads=[rg("yc", c, ti)], writes=[rg("sqb", b)])
                    p.op("pe", [MM(ps2[:, 0:n], ones_b[:, :], sqb[b][:, 0:n], c == 0, c == 7),
                                MM(ps1[:, 0:n], ones_b[:, :], hT[:, c, t0:t0 + n], c == 0, c == 7)],
                         reads=[rg("sqb", b), rg("ones"), rg("yc", c, ti)], writes=[ps1r, ps2r])
                mean = tA[1][:, 0:n]
                var = tA[0][:, 0:n]
                mr = rg("tA", 1)
                vr = rg("tA", 0)
                p.op("dve", TS(mean, ps1[:, 0:n], 1.0 / 1024.0, None, ALU.mult), reads=[ps1r], writes=[mr])
                p.op("dve", TTo(var, mean, mean, ALU.mult), reads=[mr, ysr], writes=[vr])
                p.op("dve", STT(var, ps2[:, 0:n], 1.0 / 1024.0, var, ALU.mult, ALU.subtract), reads=[ps2r, vr], writes=[vr])
                p.op("act", A_(var, var, AF.Ln, bias=cols[:, 3:4]), reads=[vr, rg("cols")], writes=[vr])
                p.op("act", A_(var, var, AF.Exp, scale=-0.5), reads=[vr], writes=[vr])
                for c in range(8):
                    b = c % 2
                    eng = "dve"
                    tz = tB[b][:, 0:n]
                    p.op(eng, TTo(tz, hT[:, c, t0:t0 + n], mean, ALU.subtract), reads=[rg("yc", c, ti), mr], writes=[rg("tB", b)])
                    p.op(eng, TTo(tz, tz, var, ALU.mult), reads=[vr, rg("tB", b)], writes=[rg("tB", b)])
                    p.op("act", A_(hT[:, c, t0:t0 + n], tz, AF.Silu, bias=V("a_ln_b%d" % l, c), scale=V("a_ln_g%d" % l, c)),
                         reads=[rg("tB", b), rg("vecs")], writes=[rg("yc", c, ti)])

            def pw2(Wv, wr, m, ti):
                t0, n = TT[ti]
                po, por = nps()
                p.op("pe", [MM(po[:, 0:n], Wv[:, k, :], hT[:, k, t0:t0 + n], k == 0, k == 7) for k in range(8)],
                     reads=[wr, rg("hT", ti)] + [rg("yc", c, ti) for c in range(8)], writes=[por])
                p.op("dve", STT(xT[:, m, t0:t0 + n], po[:, 0:n], V("a_pw2_b%d" % l, m), xT[:, m, t0:t0 + n], ALU.add, ALU.add),
                     reads=[por, rg("vecs")], writes=[rg("xT", m, ti)])
            KF2 = 4
            Wf2 = []
            for m in range(KF2):
                W, wr = ring_load(w_pw2[l, m], 1024)
                Wf2.append((W[:, 0:1024].rearrange("p (k j) -> p k j", k=8), wr))
            ln_tile(0)
            for ti in range(5):
                if ti + 1 < 5:
                    ln_tile(ti + 1)
                for m in range(KF2):
                    pw2(Wf2[m][0], Wf2[m][1], m, ti)
            for m in range(KF2, 8):
                W, wr = ring_load(w_pw2[l, m], 1024)
                Wv = W[:, 0:1024].rearrange("p (k j) -> p k j", k=8)
                for ti in range(5):
                    pw2(Wv, wr, m, ti)
            ffn(l)

        for l in range(2):
            a_layer(l)

        Wl, wlr = ring_load(w_dkvl, 2048)
        Wlv = Wl.rearrange("p (k j) -> p k j", k=8)
        Wr_, wrr = ring_load(w_dkvr, 512)
        Wrv = Wr_[:, 0:512].rearrange("p (k j) -> p k j", k=8)
        auxAf = auxA[:, :].bitcast(F32)
        clfs = [auxB[:, 0:1024].rearrange("p (c t) -> p c t", c=2), auxAf[:, 0:1024].rearrange("p (c t) -> p c t", c=2)]
        krfs = [auxB[0:32, 1024:1536], auxAf[0:32, 1024:1536]]
        def kv_tile(ti):
            t0, n = TT[ti]
            clf = clfs[ti % 2]
            krf = krfs[ti % 2]
            sx = ti % 2
            pl = [nps() for _ in range(2)]
            for kc in range(2):
                p.op("pe", [MM(pl[kc][0][:, 0:n], Wlv[:, k, kc * 128:(kc + 1) * 128], hT[:, k, t0:t0 + n], k == 0, k == 7) for k in range(8)],
                     reads=[wlr] + HR(ti), writes=[pl[kc][1]])
            pr1, pr1r = nps()
            pr2, pr2r = nps()
            p.op("pe", [MM(pr1[0:32, 0:n], Wrv[:, k, 0:32], hT[:, k, t0:t0 + n], k == 0, k == 7) for k in range(8)],
                 reads=[wrr] + HR(ti), writes=[pr1r])
            p.op("pe", [MM(pr2[0:32, 0:n], Wrv[:, k, 32:64], hT[:, k, t0:t0 + n], k == 0, k == 7) for k in range(8)],
                 reads=[wrr] + HR(ti), writes=[pr2r])
            t, tr_ = rstd_from([pl[0][0][:, 0:n], pl[1][0][:, 0:n]], [pl[0][1], pl[1][1]], n, 256.0)
            for kc in range(2):
                p.op("dve", STT(clf[:, kc, 0:n], pl[kc][0][:, 0:n], V("kvl_g", kc), t[:, 0:n], ALU.mult, ALU.mult),
                     reads=[pl[kc][1], tr_, rg("vecs")], writes=[rg("clf", sx)])
            p.op("act", A_(cT[:, :, t0:t0 + n], clf[:, :, 0:n], AF.Copy), reads=[rg("clf", sx)], writes=[rg("cT", ti)])
            t1 = tB[0][0:32, 0:n]
            t2 = tB[1][0:32, 0:n]
            p.op("dve", TTo(t1, pr1[0:32, 0:n], C4[0:32, t0:t0 + n], ALU.mult), reads=[pr1r, rg("tabs")], writes=[rg("tB", 0)])
            p.op("dve", TTo(t2, pr2[0:32, 0:n], S4[0:32, t0:t0 + n], ALU.mult), reads=[pr2r, rg("tabs")], writes=[rg("tB", 1)])
            p.op("dve", TTo(krf[:, 0:n], t1, t2, ALU.add), reads=[rg("tB", 0), rg("tB", 1)], writes=[rg("krf", sx)])
            p.op("act", A_(krT[:, t0:t0 + n], krf[:, 0:n], AF.Copy), reads=[rg("krf", sx)], writes=[rg("krT", ti)])
        def kv_out(ti):
            t0, n = TT[ti]
            clf = clfs[ti % 2]
            krf = krfs[ti % 2]
            sx = ti % 2
            for bi in range((n + 127) // 128):
                c0 = bi * 128
                nb = min(128, n - c0)
                b = bi % 2
                pt_, ptr_ = nps()
                p.op("pe", [TR(pt_[0:nb, 0:128], clf[:, 0, c0:c0 + nb], ident_f[:, :]),
                            TR(pt_[0:nb, 128:256], clf[:, 1, c0:c0 + nb], ident_f[:, :]),
                            TR(pt_[0:nb, 256:288], krf[:, c0:c0 + nb], ident_f[0:32, 0:32])],
                     reads=[rg("clf", sx), rg("krf", sx), rg("ident")], writes=[ptr_])
                p.op("act", A_(io[b][0:nb, 0:288], pt_[0:nb, 0:288], AF.Copy), reads=[ptr_], writes=[rg("io", b)])
                p.dma("sp", lat_all[t0 + c0:t0 + c0 + nb, :], io[b][0:nb, 0:256], d_o[b], reads=[rg("io", b)])
                p.dma("sp", kr_all[t0 + c0:t0 + c0 + nb, :], io[b][0:nb, 256:288], d_o[b], reads=[rg("io", b)])

        def kv_pt(ti):
            kv_tile(ti)
            if ti > 0:
                kv_out(ti - 1)
        first_pass("kv_g", kv_pt)
        kv_out(4)

        Vh = [auxA[:, 2080:2080 + 2176].rearrange("p (k j) -> p k j", k=17),
              auxB[:, :].bitcast(BF16)[:, 0:2176].rearrange("p (k j) -> p k j", k=17)]
        OT = hT
        QhB = [R2[:, 3 * NT:4 * NT], R2[:, 5 * NT:6 * NT]]
        KhB = [R2[:, 4 * NT:5 * NT], iob[:, 0:NT]]

        def b_layer(j):
            l = 2 + j
            Wd = [ring_load(w_dq[j, n3], 1024) for n3 in range(3)]

            def dq_tile(ti):
                t0, n = TT[ti]
                pq = [nps() for _ in range(3)]
                for n3 in range(3):
                    Wv = Wd[n3][0][:, 0:1024].rearrange("p (k j) -> p k j", k=8)
                    p.op("pe", [MM(pq[n3][0][:, 0:n], Wv[:, k, :], hT[:, k, t0:t0 + n], k == 0, k == 7) for k in range(8)],
                         reads=[Wd[n3][1]] + HR(ti), writes=[pq[n3][1]])
                t, tr_ = rstd_from([pq[i][0][:, 0:n] for i in range(3)], [pq[i][1] for i in range(3)], n, 384.0)
                for n3 in range(3):
                    p.op("dve", STT(cqn[:, n3, t0:t0 + n], pq[n3][0][:, 0:n], V("bq_g%d" % j, n3), t[:, 0:n], ALU.mult, ALU.mult),
                         reads=[pq[n3][1], tr_, rg("vecs")], writes=[rg("cqn", ti)])
            first_pass("b_g%d" % j, dq_tile)
            p.op("pool", lambda e: e.memset(Vh[0][:, :, 64:128], 1.0), writes=[rg("Vh", 0), rg("auxA"), rg("auxB"), rg("diag", 0), rg("diag", 1), rg("clf", 0), rg("clf", 1), rg("krf", 0), rg("krf", 1)])
            p.op("pool", lambda e: e.memset(Vh[1][:, :, 0:64], 1.0), writes=[rg("Vh", 1), rg("auxA"), rg("auxB"), rg("diag", 0), rg("diag", 1), rg("clf", 0), rg("clf", 1), rg("krf", 0), rg("krf", 1)] + [rg("ybuf", c) for c in range(8)])
            def rope(q4):
                W, wr = ring_load(w_rs[j, q4], 768)
                Wv = W[:, 0:768].rearrange("p (k s j) -> p k s j", k=3, s=2)
                for ti, (t0, n) in enumerate(TT):
                    pR, pRr = nps()
                    pS, pSr = nps()
                    p.op("pe", [MM(pR[:, 0:n], Wv[:, k, 0, :], cqn[:, k, t0:t0 + n], k == 0, k == 2) for k in range(3)],
                         reads=[wr, rg("cqn", ti)], writes=[pRr])
                    p.op("pe", [MM(pS[:, 0:n], Wv[:, k, 1, :], cqn[:, k, t0:t0 + n], k == 0, k == 2) for k in range(3)],
                         reads=[wr, rg("cqn", ti)], writes=[pSr])
                    p.op("dve", TTo(tB[0][:, 0:n], pR[:, 0:n], C4[:, t0:t0 + n], ALU.mult), reads=[pRr, rg("tabs")], writes=[rg("tB", 0)])
                    p.op("dve", TTo(tB[1][:, 0:n], pS[:, 0:n], S4[:, t0:t0 + n], ALU.mult), reads=[pSr, rg("tabs")], writes=[rg("tB", 1)])
                    p.op("pool", TTo(Rq[:, t0:t0 + n], tB[0][:, 0:n], tB[1][:, 0:n], ALU.add),
                         reads=[rg("tB", 0), rg("tB", 1)], writes=[rg("Rq", ti)])

            def gen(h):
                par = h % 2
                e_ = par
                vo = 64 * e_
                Qh = QhB[par]
                Kh = KhB[par]
                kx = [rg("io", 0), rg("io", 1)] if par == 1 else []
                W, wr = ring_load(w_hd[j, h], 608)
                uqn = W[:, 0:288].rearrange("p (k j) -> p k j", k=3)
                ukp = W[:, 288:480].rearrange("p (k j) -> p k j", k=2)
                uvh = W[:, 480:608].rearrange("p (k j) -> p k j", k=2)
                for ti, (t0, n) in enumerate(TT):
                    pq_, pqr = nps()
                    p.op("pe", [MM(pq_[0:96, 0:n], uqn[:, k, :], cqn[:, k, t0:t0 + n], k == 0, k == 2) for k in range(3)],
                         reads=[wr, rg("cqn", ti)], writes=[pqr])
                    p.op("act", A_(Qh[0:64, t0:t0 + n], pq_[0:64, 0:n], AF.Copy), reads=[pqr], writes=[rg("Qh", par, ti)] + kx)
                    if ti == 4:
                        p.op("act", A_(Qs[0:64, h, :], pq_[0:64, 16:20], AF.Copy), reads=[pqr], writes=[rg("Qs")])
                        p.op("dve", CP(Qs[64:96, h, :], Rq[32 * (h % 4):32 * (h % 4) + 32, NP:NT]), reads=[rg("Rq", 4)], writes=[rg("Qs")])
                    nk_ = min(n, NP - t0)
                    pk_, pkr = nps()
                    p.op("pe", [MM(pk_[0:96, 0:nk_], ukp[:, k, :], cT[:, k, t0:t0 + nk_], k == 0, k == 1) for k in range(2)],
                         reads=[wr, rg("cT", ti)], writes=[pkr])
                    p.op("dve", CP(Kh[0:64, t0:t0 + nk_], pk_[0:64, 0:nk_]), reads=[pkr], writes=[rg("Kh", par, ti)] + kx)
                a4 = h % 4
                p.dma("sp", Qh[64:96, 0:NT], Rq[32 * a4:32 * a4 + 32, 0:NT], d_qk[par],
                      reads=[rg("Rq", t_) for t_ in range(5)], writes=[rg("Qh", par, t_) for t_ in range(5)] + kx)
                p.dma("sp", Kh[64:96, 0:NP], krT[0:32, 0:NP], d_qk[par],
                      reads=[rg("krT", t_) for t_ in range(5)], writes=[rg("Kh", par, t_) for t_ in range(5)] + kx)
                for g0 in (0, 8, 16):
                    ng = min(8, 17 - g0)
                    pv, pvr = nps()
                    fl = []
                    for i in range(ng):
                        kb = g0 + i
                        k0 = kb * 128
                        nk = min(128, NP - k0)
                        for k in range(2):
                            fl.append(MM(pv[0:nk, i * 64:(i + 1) * 64], cT[:, k, k0:k0 + nk], uvh[:, k, :], k == 0, k == 1))
                    p.op("pe", fl, reads=[wr] + [rg("cT", ti) for ti in range(5)], writes=[pvr])
                    nkk = 128 if g0 < 16 else 16
                    p.op("dve", CP(Vh[e_][0:nkk, g0:g0 + ng, vo:vo + 64], pv[0:nkk, 0:ng * 64].rearrange("p (i d) -> p i d", i=ng)),
                         reads=[pvr], writes=[rg("Vh", e_)])
                if h % 4 == 3 and h < 15:
                    rope(h // 4 + 1)

            def attn(h):
                par = h % 2
                e_ = par
                pair = h // 2
                vo = 64 * e_
                do = 64 - vo
                Qh = QhB[par]
                Kh = KhB[par]
                kx = [rg("io", 0), rg("io", 1)] if par == 1 else []
                steps = []
                for ti, (q0, n) in enumerate(TT):
                    nq = min(n, NP - q0)
                    nkb = (q0 + nq - 1) // 128 + 1
                    pO, pOr = aps()
                    for kb in range(nkb):
                        steps.append((ti, q0, nq, nkb, kb, pO, pOr))
                LOOK = 3
                pis = {}

                def emit_S(i):
                    ti, q0, nq, nkb, kb, pO, pOr = steps[i]
                    k0 = kb * 128
                    nk = min(128, NP - k0)
                    qs = max(q0, k0)
                    w = q0 + nq - qs
                    pS_, pSr_ = nps()
                    if k0 >= q0:
                        dw = min(128, w)
                        p.op("pe", [MM(pS_[0:nk, 0:w], Kh[0:96, k0:k0 + nk], Qh[0:96, qs:qs + w], True, False),
                                    MM(pS_[0:nk, 0:dw], ident_b[0:nk, 0:nk], negm[0:nk, 0:dw], False, True)],
                             reads=[rg("Kh", par, k0 // 512), rg("Qh", par, ti), rg("negm"), rg("identb")] + kx, writes=[pSr_])
                    else:
                        p.op("pe", MM(pS_[0:nk, 0:w], Kh[0:96, k0:k0 + nk], Qh[0:96, qs:qs + w], True, True),
                             reads=[rg("Kh", par, k0 // 512), rg("Qh", par, ti)] + kx, writes=[pSr_])
                    pi = att_ctr[0] % 5
                    att_ctr[0] += 1
                    pis[i] = pi
                    p.op("act", A_(PT[pi][0:nk, 0:w], pS_[0:nk, 0:w], AF.Exp, scale=SCALE), reads=[pSr_], writes=[rg("PT", pi)])

                def emit_PV(i):
                    ti, q0, nq, nkb, kb, pO, pOr = steps[i]
                    k0 = kb * 128
                    nk = min(128, NP - k0)
                    qs = max(q0, k0)
                    w = q0 + nq - qs
                    pi = pis[i]
                    p.op("pe", MM(pO[:, qs - q0:qs - q0 + w], Vh[e_][0:nk, kb, :], PT[pi][0:nk, 0:w], kb == 0, kb == nkb - 1),
                         reads=[rg("PT", pi), rg("Vh", e_)], writes=[pOr])
                    if kb == nkb - 1:
                        b = ti % 2
                        p.op("dve", lambda e, b=b, pO=pO, nq=nq, do=do: e.reciprocal(out=tA[b][do:do + 64, 0:nq], in_=pO[do:do + 64, 0:nq]),
                             reads=[pOr], writes=[rg("tA", b)])
                        p.op("dve", TTo(OT[vo:vo + 64, pair, q0:q0 + nq], pO[vo:vo + 64, 0:nq], tA[b][do:do + 64, 0:nq], ALU.mult),
                             reads=[pOr, rg("tA", b)], writes=[rg("hT", ti)])

                for i in range(len(steps)):
                    emit_S(i)
                    if i >= LOOK:
                        emit_PV(i - LOOK)
                for i in range(max(0, len(steps) - LOOK), len(steps)):
                    emit_PV(i)

            rope(0)
            gen(0)
            for h in range(16):
                if h + 1 < 16:
                    gen(h + 1)
                attn(h)
            if debug and j == 0:
                allr = list(p.regs.values())
                dd = lambda name, shape, dt=BF16: nc.dram_tensor(name, list(shape), dt, kind="ExternalOutput").ap()
                p.dma("sp", dd("dbg_OT", [128, 8 * NT]), R1[:, :], d_out, reads=allr)
                p.dma("sp", dd("dbg_R2", [128, 8 * 2098]), R2[:, :], d_out, reads=allr)
                p.dma("sp", dd("dbg_auxA", [128, 4352]), auxA[:, :], d_out, reads=allr)
                p.dma("sp", dd("dbg_auxB", [128, 2048], F32), auxB[:, :], d_out, reads=allr)
                p.dma("sp", dd("dbg_tri", [128, 128]), tri[:, :], d_out, reads=allr)
                p.dma("sp", dd("dbg_PT", [128, 512]), PT[0][:, :], d_out, reads=allr)
                p.dma("sp", dd("dbg_tA", [128, 512], F32), tA[0][:, :], d_out, reads=allr)
                p.dma("sp", dd("dbg_xT", [128, 8 * NT], F32), xT[:, :, :].rearrange("p c t -> p (c t)"), d_out, reads=allr)
                p.dma("sp", dd("dbg_C4", [128, NT]), C4[:, :], d_out, reads=allr)
                p.dma("sp", dd("dbg_S4", [128, NT]), S4[:, :], d_out, reads=allr)
            if stage >= 4:
                sample_attention(j)
            else:
                p.op("pool", lambda e: e.memset(OT[:, :, NP:NT], 0.0), writes=[rg("hT", 4)])
            for m in range(8):
                W, wr = ring_load(w_o[j, m], 1024)
                Wv = W[:, 0:1024].rearrange("p (k j) -> p k j", k=8)
                for ti, (t0, n) in enumerate(TT):
                    po, por = nps()
                    p.op("pe", [MM(po[:, 0:n], Wv[:, k, :], OT[:, k, t0:t0 + n], k == 0, k == 7) for k in range(8)],
                         reads=[wr, rg("hT", ti)], writes=[por])
                    p.op("dve", TTo(xT[:, m, t0:t0 + n], po[:, 0:n], xT[:, m, t0:t0 + n], ALU.add),
                         reads=[por], writes=[rg("xT", m, ti)])
            ffn(l)

        att_ctr = [0]

        ukT = auxA[0:64, 0:4096].rearrange("p (h c) -> p h c", h=16)
        uvp = auxB[:, :].bitcast(BF16).rearrange("p (k h j) -> p k h j", k=2, h=16)
        SA = R2[:, 0:6 * NT]
        CLb2 = [SA[:, i * 2048:(i + 1) * 2048] for i in range(4)]
        CLb = [c_.rearrange("p (r c) -> p r c", r=8) for c_ in CLb2]
        KRg = [iob[:, 3072 + i * 256:3072 + (i + 1) * 256] for i in range(4)]
        tcT = [SA[:, 8192 + i * 1536: 8192 + i * 1536 + 1024].rearrange("p (k t) -> p k t", k=2) for i in range(2)]
        tkr = [SA[:, 8192 + i * 1536 + 1024: 8192 + i * 1536 + 1152] for i in range(2)]
        QRz = SA[:, 8192 + 1152:8192 + 1152 + 256].rearrange("p (r b h) -> p r b h", r=4, b=4)
        o0 = 8192 + 3072
        QL = SA[:, o0:o0 + 128].rearrange("p (k b h) -> p k b h", k=2, b=4)
        QR = SA[0:32, o0 + 128:o0 + 192].rearrange("p (b h) -> p b h", b=4)
        OLT = SA[:, o0 + 192:o0 + 320].rearrange("p (k b h) -> p k b h", k=2, b=4)
        PTs = [SA[:, o0 + 320 + i * 64:o0 + 384 + i * 64] for i in range(2)]
        PTn = SA[0:1, o0 + 448:o0 + 464]
        crow = SA[0:1, o0 + 464:o0 + 720]
        OL = SA[0:16, o0 + 720:o0 + 976]
        rd = cols[0:16, 5:6]
        d_gl = [p.new_sem("d_gl%d" % i) for i in range(4)]
        d_gk = [p.new_sem("d_gk%d" % i) for i in range(4)]

        def sample_attention(j):
            sar = rg("sa_small")
            p.dma("pool", auxA[0:64, 0:4096], w_ukT, d_misc, writes=[rg("auxA"), rg("diag", 0)] + [rg("Rq", ti) for ti in range(5)] + [rg("Vh", 0), rg("Vh", 1)])
            p.dma("pool", auxB[:, :].bitcast(BF16), w_uvp, d_misc2, writes=[rg("auxB"), rg("diag", 1), rg("clf", 0), rg("clf", 1), rg("krf", 0), rg("krf", 1)] + [rg("ybuf", c) for c in range(8)])
            sa_regs = ([rg("cqn", ti) for ti in range(5)] + [rg("Qh", pp, ti) for ti in range(5) for pp in range(2)]
                       + [rg("Kh", pp, ti) for ti in range(5) for pp in range(2)] + [rg("io", 1)])
            pq1, pq1r = nps()
            fl = []
            for h in range(16):
                for kc in range(2):
                    c0 = (kc * 16 + h) * 4
                    fl.append(MM(pq1[:, c0:c0 + 4], ukT[:, h, kc * 128:(kc + 1) * 128], Qs[0:64, h, :], True, True))
            p.op("pe", fl, reads=[rg("auxA"), rg("Qs")], writes=[pq1r])
            p.op("act", A_(QL.rearrange("p k b h -> p k h b"), pq1[:, 0:128].rearrange("p (k h b) -> p k h b", k=2, h=16), AF.Copy),
                 reads=[pq1r], writes=[sar] + sa_regs)
            pq2, pq2r = nps()
            p.op("pe", [MM(pq2[0:32, h * 4:h * 4 + 4], ident_b[0:96, 64:96], Qs[0:96, h, :], True, True) for h in range(16)],
                 reads=[rg("identb"), rg("Qs")], writes=[pq2r])
            p.op("pool", lambda e: e.memset(QRz[:, :, :, :], 0.0), reads=[pq1r], writes=[sar] + sa_regs)
            for rr in range(4):
                p.op("dve", CP(QRz[32 * rr:32 * rr + 32, rr, :, :].rearrange("p b h -> p h b"),
                               pq2[0:32, 0:64].rearrange("p (h b) -> p h b", h=16)), reads=[pq2r], writes=[sar])
            steps = [(b4, g, r4) for b4 in range(4) for g in range(16) for r4 in range(2)]
            acc = {}
            st_ = {}

            def finalize(b4):
                pO, pOr = acc[b4]
                pD = PS[7][0:16, b4:b4 + 1]
                col = NP + b4
                pS_, pSr_ = nps()
                p.op("pe", [MM(pS_[0:1, 0:16], cT[:, 0, col:col + 1], QL[:, 0, b4, :], True, False),
                            MM(pS_[0:1, 0:16], cT[:, 1, col:col + 1], QL[:, 1, b4, :], False, False),
                            MM(pS_[0:1, 0:16], krT[:, col:col + 1], QRz[0:32, 0, b4, :], False, True)],
                     reads=[rg("cT", 4), rg("krT", 4), sar], writes=[pSr_])
                p.op("act", A_(PTn, pS_[0:1, 0:16], AF.Exp, scale=SCALE), reads=[pSr_], writes=[rg("PTn")])
                pC, pCr = nps()
                pCb = pC[:, :].bitcast(BF16)
                p.op("pe", [TR(pCb[0:1, kc * 128:(kc + 1) * 128], cT[:, kc, col:col + 1], ident_b[:, :]) for kc in range(2)],
                     reads=[rg("cT", 4), rg("identb")], writes=[pCr])
                p.op("dve", CP(crow, pCb[0:1, 0:256]), reads=[pCr], writes=[rg("crow")])
                p.op("pe", [MM(pO[0:16, 0:256], PTn, crow, False, True), MM(pD, PTn, ones_b[0:1, 0:1], False, True)],
                     reads=[rg("PTn"), rg("crow"), rg("ones")], writes=[pOr, psr[7]])
                p.op("dve", lambda e, pD=pD: e.reciprocal(out=rd, in_=pD), reads=[pOr, psr[7]], writes=[rg("rd")])
                p.op("dve", TS(OL, pO[0:16, 0:256], rd, None, ALU.mult), reads=[pOr, rg("rd")], writes=[rg("OL")])
                pT2, pT2r = nps()
                pT2b = pT2[:, :].bitcast(BF16)
                p.op("pe", [TR(pT2b[:, kc * 16:(kc + 1) * 16], OL[:, kc * 128:(kc + 1) * 128], ident_b[0:16, 0:16]) for kc in range(2)],
                     reads=[rg("OL"), rg("identb")], writes=[pT2r])
                p.op("act", A_(OLT[:, :, b4, :], pT2b[:, 0:32].rearrange("p (k h) -> p k h", k=2), AF.Copy), reads=[pT2r], writes=[rg("OLT")])

            def emit_T(i):
                b4, g, r4 = steps[i]
                gi = b4 * 16 + g
                cb = gi % 4
                if r4 == 0:
                    if g == 0:
                        acc[b4] = (PS[6], psr[6])
                    p.raw("pool", lambda e, b4=b4, g=g, cb=cb: e.indirect_dma_start(
                        out=CLb2[cb], out_offset=None, in_=cache_lat[:, :],
                        in_offset=bass.IndirectOffsetOnAxis(ap=idxg[:, b4, g:g + 1], axis=0)),
                        d_gl[cb], reads=[rg("idxg")], writes=[rg("CLb", cb)])
                    p.raw("pool", lambda e, b4=b4, g=g, cb=cb: e.indirect_dma_start(
                        out=KRg[cb], out_offset=None, in_=cache_kr[:, :],
                        in_offset=bass.IndirectOffsetOnAxis(ap=idxg[:, b4, g:g + 1], axis=0)),
                        d_gk[cb], reads=[rg("idxg")], writes=[rg("KRg", cb)])
                tb = i % 2
                pA, pAr = nps()
                pB, pBr = nps()
                pAb = pA[:, :].bitcast(BF16).rearrange("p (k r t) -> p k r t", k=2, r=4)
                pBb = pB[:, :].bitcast(BF16)
                fl = []
                for rr in range(4):
                    r = r4 * 4 + rr
                    for kc in range(2):
                        fl.append(TR(pAb[:, kc, rr, :], CLb[cb][:, r, kc * 128:(kc + 1) * 128], ident_b[:, :]))
                fl.append(TR(pBb[:, 0:128], KRg[cb][:, r4 * 128:(r4 + 1) * 128], ident_b[:, :]))
                p.op("pe", fl, reads=[rg("CLb", cb), rg("KRg", cb), rg("identb")], writes=[pAr, pBr])
                p.op("act", A_(tcT[tb], pAb.rearrange("p k r t -> p k (r t)"), AF.Copy), reads=[pAr], writes=[rg("tcT", tb)])
                p.op("dve", CP(tkr[tb], pBb[:, 0:128]), reads=[pBr], writes=[rg("tkr", tb)])

            def emit_S(i):
                b4, g, r4 = steps[i]
                tb = i % 2
                pS_, pSr_ = nps()
                fl = []
                for rr in range(4):
                    o_ = pS_[:, rr * 16:(rr + 1) * 16]
                    fl.append(MM(o_, tcT[tb][:, 0, rr * 128:(rr + 1) * 128], QL[:, 0, b4, :], True, False))
                    fl.append(MM(o_, tcT[tb][:, 1, rr * 128:(rr + 1) * 128], QL[:, 1, b4, :], False, False))
                    fl.append(MM(o_, tkr[tb][:, :], QRz[:, rr, b4, :], False, True))
                p.op("pe", fl, reads=[rg("tcT", tb), rg("tkr", tb), sar], writes=[pSr_])
                p.op("act", A_(PTs[tb], pS_[:, 0:64], AF.Exp, scale=SCALE), reads=[pSr_], writes=[rg("PTs", tb)])

            def emit_V(i):
                b4, g, r4 = steps[i]
                gi = b4 * 16 + g
                cb = gi % 4
                tb = i % 2
                pO, pOr = acc[b4]
                pD = PS[7][0:16, b4:b4 + 1]
                fl = []
                for rr in range(4):
                    r = r4 * 4 + rr
                    first = (g == 0 and r4 == 0 and rr == 0)
                    fl.append(MM(pO[0:16, 0:256], PTs[tb][:, rr * 16:(rr + 1) * 16], CLb[cb][:, r, :], first, False))
                    fl.append(MM(pD, PTs[tb][:, rr * 16:(rr + 1) * 16], ones_b[:, 0:1], first, False))
                p.op("pe", fl, reads=[rg("PTs", tb), rg("CLb", cb), rg("ones")], writes=[pOr, psr[7]])
                if g == 15 and r4 == 1:
                    finalize(b4)

            n_ = len(steps)
            for i in range(n_ + 2):
                if i < n_:
                    emit_T(i)
                if 1 <= i <= n_:
                    emit_S(i - 1)
                if i >= 2:
                    emit_V(i - 2)
            for pair in range(8):
                po, por = nps()
                fl = []
                i = 0
                for e2 in range(2):
                    for kc in range(2):
                        fl.append(MM(po[:, 0:4], uvp[:, kc, 2 * pair + e2, :], OLT[:, kc, :, 2 * pair + e2], i == 0, i == 3))
                        i += 1
                p.op("pe", fl, reads=[rg("auxB"), rg("OLT")], writes=[por])
                p.op("act", A_(OT[:, pair, NP:NT], po[:, 0:4], AF.Copy), reads=[por], writes=[rg("hT", 4)])

        sa_ctr = [0]

        if stage >= 3:
            for j in range(2):
                b_layer(j)

        R1f = R1[:, :].bitcast(F32)
        yos = [R1f[:, i * 4096:(i + 1) * 4096].rearrange("p (c t) -> p c t", c=8) for i in range(2)]

        def fin_norm(ti):
            t0, n = TT[ti]
            yo = yos[ti % 2]
            srcs = [xT[:, c, t0:t0 + n] for c in range(8)]
            t, tr_ = rstd_from(srcs, [rg("xT", c, ti) for c in range(8)], n, 1024.0, sq_eng="pool")
            for c in range(8):
                eng = "dve"
                p.op(eng, STT(yo[:, c, 0:n], xT[:, c, t0:t0 + n], V("fin_g", c), t[:, 0:n], ALU.mult, ALU.mult),
                     reads=[rg("xT", c, ti), tr_, rg("vecs")] + ([] if c == 0 else [rg("yo", ti % 2)]),
                     writes=([rg("yo", ti % 2)] + [r_ for k in range(5) for r_ in HR(k)]) if c == 0 else [rg("yoc", ti % 2, c)])

        def fin_out(ti):
            t0, n = TT[ti]
            yo = yos[ti % 2]
            for bi in range((n + 127) // 128):
                c0 = bi * 128
                nb = min(128, n - c0)
                b = bi % 2
                for half in range(2):
                    pt_, ptr_ = nps()
                    p.op("pe", [TR(pt_[0:nb, jj * 128:(jj + 1) * 128], yo[:, half * 4 + jj, c0:c0 + nb], ident_f[:, :]) for jj in range(4)],
                         reads=[rg("yo", ti % 2), rg("ident")] + [rg("yoc", ti % 2, c) for c in range(1, 8)], writes=[ptr_])
                    if half == 0:
                        p.op("act", A_(io[b][0:nb, 0:512], pt_[0:nb, :], AF.Copy), reads=[ptr_], writes=[rg("io", b)])
                    else:
                        p.op("dve", CP(io[b][0:nb, 512:1024], pt_[0:nb, :]), reads=[ptr_], writes=[rg("io", b)])
                p.dma("sp", y_all[t0 + c0:t0 + c0 + nb, :], io[b][0:nb, :], d_o[b], reads=[rg("io", b)])
        fin_norm(0)
        for ti in range(5):
            if ti + 1 < 5:
                fin_norm(ti + 1)
            fin_out(ti)
        for dd_ in (d_o[0], d_o[1], d_out):
            if dd_.count > 0:
                p.q["sp"].append(lambda e, dd_=dd_: e.wait_ge(dd_.h, dd_.count))
        p.emit()
    return nc


def _cols(v):
    return np.ascontiguousarray(np.asarray(v, np.float32).reshape(-1, 128).T)


def _prep_shared(inp):
    f = lambda a: np.ascontiguousarray(np.asarray(a, dtype=np.float32))
    d = {}
    pw1 = f(inp["a_pw1_w"]).reshape(2, 8, 128, 2, 8, 128)
    d["w_pw1"] = f(pw1.transpose(0, 4, 2, 3, 1, 5).reshape(2, 8, 128, 2048))
    pw2 = f(inp["a_pw2_w"]).reshape(2, 8, 128, 8, 128)
    d["w_pw2"] = f(pw2.transpose(0, 3, 2, 1, 4).reshape(2, 8, 128, 1024))
    g = f(inp["ffn_w_gate"]).reshape(4, 8, 128, 22, 128)
    u = f(inp["ffn_w_up"]).reshape(4, 8, 128, 22, 128)
    gu = np.stack([g, u], axis=0)
    d["w_gu"] = f(gu.transpose(1, 4, 3, 0, 2, 5).reshape(4, 22, 128, 2048))
    dn = f(inp["ffn_w_down"]).reshape(4, 22, 128, 8, 128)
    d["w_dn"] = f(dn.transpose(0, 3, 2, 1, 4).reshape(4, 8, 128, 2816))
    dq = f(inp["b_w_dq"]).reshape(2, 8, 128, 3, 128)
    d["w_dq"] = f(dq.transpose(0, 3, 2, 1, 4).reshape(2, 3, 128, 1024))
    uq = f(inp["b_w_uq"]).reshape(2, 3, 128, 16, 96)
    rope = uq[..., 64:96]
    swp = np.concatenate([uq[..., 80:96], uq[..., 64:80]], axis=-1)
    rs = np.stack([rope, swp], axis=0).reshape(2, 2, 3, 128, 4, 4 * 32)
    d["w_rs"] = f(rs.transpose(1, 4, 3, 2, 0, 5).reshape(2, 4, 128, 768))
    nope = np.zeros((2, 3, 128, 16, 96), np.float32)
    nope[..., 0:64] = uq[..., 0:64]
    uk = f(inp["w_uk"]).reshape(2, 128, 16, 64)
    ukp = np.zeros((2, 128, 16, 96), np.float32)
    ukp[..., 0:64] = uk
    uv = f(inp["w_uv"]).reshape(2, 128, 16, 64)
    hd = np.zeros((2, 16, 128, 608), np.float32)
    for j in range(2):
        hd[j, :, :, 0:288] = nope[j].transpose(2, 1, 0, 3).reshape(16, 128, 288)
        hd[j, :, :, 288:480] = ukp.transpose(2, 1, 0, 3).reshape(16, 128, 192)
        hd[j, :, :, 480:608] = uv.transpose(2, 1, 0, 3).reshape(16, 128, 128)
    d["w_hd"] = hd
    wo = f(inp["b_w_o"]).reshape(2, 8, 128, 8, 128)
    d["w_o"] = f(wo.transpose(0, 3, 2, 1, 4).reshape(2, 8, 128, 1024))
    dkv = f(inp["w_dkv"]).reshape(8, 128, 288)
    d["w_dkvl"] = f(dkv[..., 0:256].transpose(1, 0, 2).reshape(128, 2048))
    rsw = np.concatenate([dkv[..., 256:288], dkv[..., 272:288], dkv[..., 256:272]], axis=-1)
    d["w_dkvr"] = f(rsw.transpose(1, 0, 2).reshape(128, 512))
    ukf = f(inp["w_uk"])
    d["w_ukT"] = f(ukf.transpose(2, 1, 0).reshape(64, 4096))
    uvp = np.zeros((128, 2, 16, 128), np.float32)
    for h in range(16):
        uvp[:, :, h, (h % 2) * 64:(h % 2) * 64 + 64] = uv[:, :, h, :].transpose(1, 0, 2)
    d["w_uvp"] = f(uvp.reshape(128, 4096))
    vs = []
    for l in range(2):
        pass
    order = [("a_norm_g", 0), ("a_norm_g", 1), ("a_pw1_b", 0), ("a_pw1_b", 1), ("a_dw_b", 0), ("a_dw_b", 1),
             ("a_ln_g", 0), ("a_ln_g", 1), ("a_ln_b", 0), ("a_ln_b", 1), ("a_pw2_b", 0), ("a_pw2_b", 1),
             ("ffn_norm_g", 0), ("ffn_norm_g", 1), ("ffn_norm_g", 2), ("ffn_norm_g", 3)]
    for nm, i in order:
        vs.append(_cols(f(inp[nm])[i]))
    vs.append(_cols(inp["kv_norm_g"]))
    vs.append(_cols(inp["kv_latent_norm_g"]))
    vs.append(_cols(f(inp["b_norm_g"])[0]))
    vs.append(_cols(f(inp["b_norm_g"])[1]))
    vs.append(_cols(f(inp["b_q_norm_g"])[0]))
    vs.append(_cols(f(inp["b_q_norm_g"])[1]))
    vs.append(_cols(inp["final_norm_g"]))
    d["vecs"] = f(np.concatenate(vs, axis=1))
    assert d["vecs"].shape == (128, NV)
    dw = f(inp["a_dw_w"]).reshape(2, 31, 8, 128)
    d["dww"] = f(dw.transpose(3, 0, 2, 1).reshape(128, 2 * 8 * 31))
    d["cache_lat"] = f(inp["cache_latent"]).reshape(5120 * 16, 2048)
    d["cache_kr"] = f(inp["cache_krope"]).reshape(5120 * 16, 256)
    return d


_NC_CACHE = {}


def kernel(**inputs):
    shared = _prep_shared(inputs)
    xp = np.asarray(inputs["x_prompt"], np.float32)
    xs = np.asarray(inputs["x_sample"], np.float32)
    meta = np.asarray(inputs["meta_tokens"], np.float32)
    stc = np.asarray(inputs["state_conv"], np.float32)
    pt = np.asarray(inputs["page_table"], np.int32)
    in_maps = []
    for i in range(8):
        m = dict(shared)
        m["xin"] = np.ascontiguousarray(np.concatenate([meta, xp[i], xs[4 * i:4 * i + 4, 0]], axis=0))
        m["state"] = np.ascontiguousarray(stc[:, 4 * i:4 * i + 4].transpose(0, 2, 1, 3))
        m["ptT"] = np.ascontiguousarray(pt[4 * i:4 * i + 4].T)
        in_maps.append(m)
    if "nc" not in _NC_CACHE:
        _NC_CACHE["nc"] = build_program()
    res = run_bass_kernel_spmd(_NC_CACHE["nc"], in_maps, core_ids=list(range(8)))
    R = res.results
    y_prompt = np.stack([R[i]["y_all"][16:NP] for i in range(8)], 0)
    y_sample = np.concatenate([R[i]["y_all"][NP:NT] for i in range(8)], 0)[:, None, :]
    lat_p = np.stack([R[i]["lat_all"][0:NP] for i in range(8)], 0)
    kr_p = np.stack([R[i]["kr_all"][0:NP] for i in range(8)], 0)
    conv_p = np.stack([R[i]["conv_p"] for i in range(8)], 1)
    lat_s = np.concatenate([R[i]["lat_all"][NP:NT] for i in range(8)], 0)[:, None, :]
    kr_s = np.concatenate([R[i]["kr_all"][NP:NT] for i in range(8)], 0)[:, None, :]
    conv_s = np.concatenate([R[i]["conv_s"] for i in range(8)], 1)
    outs = (y_prompt, y_sample, lat_p, kr_p, conv_p, lat_s, kr_s, conv_s)
    return tuple(np.ascontiguousarray(o, dtype=np.float32) for o in outs)
```
